# Optimizing a Trainium2 kernel written in Bass

```python
import math
import jax, jax.numpy as jnp
from jax import lax
import numpy as np

D_MODEL = 1024
BATCH = 8
SEQ = 2048
DEPTH = 4
DEC_BATCH = 32
DEC_SEQ = 4
PAST_LEN = 8192
PAGE_SIZE = 128

D_MIX = D_MODEL
CHUNK = 128
A_WIDTH = D_MIX // 2
A_GROUPS = 4
A_GW = A_WIDTH // A_GROUPS
B_WIDTH = D_MIX - A_WIDTH
HEAD_DIM = 64
N_HEADS = B_WIDTH // HEAD_DIM
N_KV_HEADS = 2
KV_GROUP = N_HEADS // N_KV_HEADS
IDX_HEADS = 8
IDX_DIM = 32
TOPK_MAX = 256
ROPE_THETA = 10000.0
Q_BLOCK = 128
D_FF = 4 * D_MODEL
D_PLE = 256
ALPHA = (2.0 * DEPTH) ** 0.25
BETA = (8.0 * DEPTH) ** -0.25
LN_EPS = 1e-5

SPLIT_SIZES = (A_WIDTH, A_WIDTH, N_HEADS * HEAD_DIM, N_KV_HEADS * HEAD_DIM,
               N_KV_HEADS * HEAD_DIM, IDX_HEADS * IDX_DIM, IDX_DIM, IDX_HEADS)
D_IN = sum(SPLIT_SIZES)
SPLIT_POINTS = tuple(sum(SPLIT_SIZES[:i + 1]) for i in range(len(SPLIT_SIZES) - 1))

N_PAGES = PAST_LEN // PAGE_SIZE
N_PHYS_PAGES = (5 * DEC_BATCH * N_PAGES) // 4
TOPK_PROMPT = min(TOPK_MAX, SEQ // 4)
TOPK_SAMPLE = min(TOPK_MAX, (PAST_LEN + DEC_SEQ) // 4)

kernel_name = "hymba_gmlp_dsa_deepnorm_decoder_step"


def layer_norm(x, g, b):
    xf = x.astype(jnp.float32)
    mu = jnp.mean(xf, axis=-1, keepdims=True)
    var = jnp.mean(jnp.square(xf - mu), axis=-1, keepdims=True)
    y = (xf - mu) * lax.rsqrt(var + LN_EPS) * g.astype(jnp.float32) + b.astype(jnp.float32)
    return y.astype(x.dtype)


def rope(x, pos):
    d = x.shape[-1]
    half = d // 2
    inv = ROPE_THETA ** (-jnp.arange(half, dtype=jnp.float32) / half)
    ang = pos.astype(jnp.float32)[:, None] * inv[None, :]
    cos = jnp.cos(ang)[None, :, None, :]
    sin = jnp.sin(ang)[None, :, None, :]
    xf = x.astype(jnp.float32)
    x1, x2 = xf[..., :half], xf[..., half:]
    return jnp.concatenate([x1 * cos - x2 * sin, x2 * cos + x1 * sin], axis=-1).astype(x.dtype)


def chunk_mix(u, vn, w_s, b_s):
    B, S = u.shape[:2]
    n = min(CHUNK, S)
    nc = S // n
    mask = jnp.tril(jnp.ones((n, n), dtype=bool))
    w = jnp.where(mask[None], w_s[:, :n, :n], 0)
    vc = vn.reshape(B, nc, n, A_GROUPS, A_GW)
    gate = jnp.einsum('gts,bcsgd->bctgd', w, vc) + b_s[:, :n].T[None, None, :, :, None]
    return u * gate.reshape(B, S, A_WIDTH)


def indexer_scores(qi, wi, ki):
    s = jnp.einsum('bthd,bld->bthl', qi, ki).astype(jnp.float32) * (IDX_DIM ** -0.5)
    w = wi.astype(jnp.float32) * (IDX_HEADS ** -0.5)
    return jnp.einsum('bthl,bth->btl', jax.nn.relu(s), w)


def attend_selected(q, k_sel, v_sel, valid):
    B, T = q.shape[:2]
    qg = q.reshape(B, T, N_KV_HEADS, KV_GROUP, HEAD_DIM)
    s = jnp.einsum('bthgd,btkhd->bthgk', qg, k_sel).astype(jnp.float32) * (HEAD_DIM ** -0.5)
    s = jnp.where(valid[:, :, None, None, :], s, -jnp.inf)
    p = jax.nn.softmax(s, axis=-1)
    o = jnp.einsum('bthgk,btkhd->bthgd', p.astype(v_sel.dtype), v_sel)
    return o.reshape(B, T, N_HEADS * HEAD_DIM)


def gather_rows(x, idx):
    return jax.vmap(lambda xb, ib: xb[ib])(x, idx)


def prompt_sparse_attn(q, k, v, qi, wi, ki):
    B, S = q.shape[:2]
    nb = S // Q_BLOCK

    def to_blocks(a):
        return a.reshape((B, nb, Q_BLOCK) + a.shape[2:]).swapaxes(0, 1)

    key_pos = jnp.arange(S)

    def block(args):
        q_b, qi_b, wi_b, t0 = args
        tpos = t0 + jnp.arange(Q_BLOCK)
        I = indexer_scores(qi_b, wi_b, ki)
        causal = key_pos[None, :] <= tpos[:, None]
        I = jnp.where(causal[None], I, -jnp.inf)
        _, idx = lax.top_k(I, TOPK_PROMPT)
        valid = idx <= tpos[None, :, None]
        return attend_selected(q_b, gather_rows(k, idx), gather_rows(v, idx), valid)

    starts = jnp.arange(nb, dtype=jnp.int32) * Q_BLOCK
    out = lax.map(block, (to_blocks(q), to_blocks(qi), to_blocks(wi), starts))
    return out.swapaxes(0, 1).reshape(B, S, N_HEADS * HEAD_DIM)


def sample_sparse_attn(q, k_new, v_new, qi, wi, ki_new, ck, cv, cki, page_table):
    DB, T = q.shape[:2]
    L = PAST_LEN + T
    ki_past = cki[page_table].reshape(DB, PAST_LEN, IDX_DIM)
    ki_all = jnp.concatenate([ki_past, ki_new], axis=1)
    I = indexer_scores(qi, wi, ki_all)
    tpos = PAST_LEN + jnp.arange(T)
    causal = jnp.arange(L)[None, :] <= tpos[:, None]
    I = jnp.where(causal[None], I, -jnp.inf)
    _, idx = lax.top_k(I, TOPK_SAMPLE)
    valid = idx <= tpos[None, :, None]
    is_past = idx < PAST_LEN
    pidx = jnp.minimum(idx, PAST_LEN - 1)
    page = jnp.take_along_axis(page_table, (pidx // PAGE_SIZE).reshape(DB, -1), axis=1)
    row = page.reshape(idx.shape) * PAGE_SIZE + pidx % PAGE_SIZE
    nidx = jnp.clip(idx - PAST_LEN, 0, T - 1)
    ck_flat = ck.reshape(-1, N_KV_HEADS, HEAD_DIM)
    cv_flat = cv.reshape(-1, N_KV_HEADS, HEAD_DIM)
    sel = is_past[..., None, None]
    k_sel = jnp.where(sel, ck_flat[row], gather_rows(k_new, nidx))
    v_sel = jnp.where(sel, cv_flat[row], gather_rows(v_new, nidx))
    return attend_selected(q, k_sel, v_sel, valid)


def trunk_layer(h, pe, pos, attn_fn, w_in, sgu_ln_g, sgu_ln_b, sgu_w, sgu_b, w_o,
                ln1_g, ln1_b, w_ff1, w_ff2, w_ple_gate, w_ple_proj, ln2_g, ln2_b):
    B, S = h.shape[:2]
    z = h @ w_in
    a_u, a_v, q, k, v, qi, ki, wi = jnp.split(z, SPLIT_POINTS, axis=-1)
    a_u = jax.nn.gelu(a_u, approximate=False)
    a_v = jax.nn.gelu(a_v, approximate=False)
    vn = layer_norm(a_v.reshape(B, S, A_GROUPS, A_GW), sgu_ln_g, sgu_ln_b)
    a_out = chunk_mix(a_u, vn, sgu_w, sgu_b)
    q = rope(q.reshape(B, S, N_HEADS, HEAD_DIM), pos)
    k = rope(k.reshape(B, S, N_KV_HEADS, HEAD_DIM), pos)
    v = v.reshape(B, S, N_KV_HEADS, HEAD_DIM)
    qi = rope(qi.reshape(B, S, IDX_HEADS, IDX_DIM), pos)
    ki = rope(ki[:, :, None, :], pos)[:, :, 0, :]
    b_out = attn_fn(q, k, v, qi, wi, ki)
    mix = jnp.concatenate([a_out, b_out], axis=-1) @ w_o
    h = layer_norm(ALPHA * h + mix, ln1_g, ln1_b)
    ff = jnp.square(jax.nn.relu(h @ w_ff1)) @ w_ff2
    ple = jax.nn.sigmoid(h @ w_ple_gate) * (pe @ w_ple_proj)
    h = layer_norm(ALPHA * h + ff + ple, ln2_g, ln2_b)
    n = min(CHUNK, S)
    chunk_rows = vn.reshape(B, S, A_WIDTH)[:, S - n:]
    return h, k, v, ki, chunk_rows


def setup_inputs(seed: int = 0) -> dict:
    key = jax.random.key(seed)
    ks = jax.random.split(key, 24)
    f32 = jnp.float32
    nrm = lambda k, shp: jax.random.normal(k, shp, dtype=f32)
    col_scale = jnp.concatenate([
        jnp.ones((2 * A_WIDTH + N_HEADS * HEAD_DIM + N_KV_HEADS * HEAD_DIM,), f32),
        jnp.full((N_KV_HEADS * HEAD_DIM,), BETA, f32),
        jnp.ones((IDX_HEADS * IDX_DIM + IDX_DIM + IDX_HEADS,), f32)])
    perm = jax.random.permutation(ks[23], N_PHYS_PAGES)[:DEC_BATCH * N_PAGES]
    return {
        "x_prompt": nrm(ks[0], (BATCH, SEQ, D_MODEL)),
        "x_sample": nrm(ks[1], (DEC_BATCH, DEC_SEQ, D_MODEL)),
        "p_prompt": nrm(ks[2], (DEPTH, BATCH, SEQ, D_PLE)),
        "p_sample": nrm(ks[3], (DEPTH, DEC_BATCH, DEC_SEQ, D_PLE)),
        "cache_k": nrm(ks[4], (DEPTH, N_PHYS_PAGES, PAGE_SIZE, N_KV_HEADS, HEAD_DIM)),
        "cache_v": nrm(ks[5], (DEPTH, N_PHYS_PAGES, PAGE_SIZE, N_KV_HEADS, HEAD_DIM)),
        "cache_kidx": nrm(ks[6], (DEPTH, N_PHYS_PAGES, PAGE_SIZE, IDX_DIM)),
        "page_table": perm.reshape(DEC_BATCH, N_PAGES).astype(jnp.int32),
        "w_in": nrm(ks[7], (DEPTH, D_MODEL, D_IN)) * (D_MODEL ** -0.5) * col_scale,
        "sgu_ln_g": 1.0 + 0.1 * nrm(ks[8], (DEPTH, A_GROUPS, A_GW)),
        "sgu_ln_b": 0.1 * nrm(ks[9], (DEPTH, A_GROUPS, A_GW)),
        "sgu_w": nrm(ks[10], (DEPTH, A_GROUPS, CHUNK, CHUNK)) * (CHUNK ** -0.5),
        "sgu_b": 1.0 + 0.1 * nrm(ks[11], (DEPTH, A_GROUPS, CHUNK)),
        "w_o": nrm(ks[12], (DEPTH, D_MIX, D_MODEL)) * (D_MIX ** -0.5) * BETA,
        "ln1_g": 1.0 + 0.1 * nrm(ks[13], (DEPTH, D_MODEL)),
        "ln1_b": 0.1 * nrm(ks[14], (DEPTH, D_MODEL)),
        "w_ff1": nrm(ks[15], (DEPTH, D_MODEL, D_FF)) * (D_MODEL ** -0.5),
        "w_ff2": nrm(ks[16], (DEPTH, D_FF, D_MODEL)) * (D_FF ** -0.5) * BETA,
        "w_ple_gate": nrm(ks[17], (DEPTH, D_MODEL, D_MODEL)) * (D_MODEL ** -0.5),
        "w_ple_proj": nrm(ks[18], (DEPTH, D_PLE, D_MODEL)) * (D_PLE ** -0.5) * BETA,
        "ln2_g": 1.0 + 0.1 * nrm(ks[19], (DEPTH, D_MODEL)),
        "ln2_b": 0.1 * nrm(ks[20], (DEPTH, D_MODEL)),
    }


def reference(x_prompt, x_sample, p_prompt, p_sample, cache_k, cache_v, cache_kidx, page_table,
              w_in, sgu_ln_g, sgu_ln_b, sgu_w, sgu_b, w_o, ln1_g, ln1_b, w_ff1, w_ff2,
              w_ple_gate, w_ple_proj, ln2_g, ln2_b):
    pos_prompt = jnp.arange(x_prompt.shape[1], dtype=jnp.int32)
    pos_sample = PAST_LEN + jnp.arange(x_sample.shape[1], dtype=jnp.int32)
    hp, hs = x_prompt, x_sample
    kp, vp, kip, cp = [], [], [], []
    ksm, vsm, kism, csm = [], [], [], []
    for i in range(DEPTH):
        lw = (w_in[i], sgu_ln_g[i], sgu_ln_b[i], sgu_w[i], sgu_b[i], w_o[i], ln1_g[i], ln1_b[i],
              w_ff1[i], w_ff2[i], w_ple_gate[i], w_ple_proj[i], ln2_g[i], ln2_b[i])
        hp, k_, v_, ki_, c_ = trunk_layer(hp, p_prompt[i], pos_prompt, prompt_sparse_attn, *lw)
        kp.append(k_); vp.append(v_); kip.append(ki_); cp.append(c_)
        attn_s = (lambda ck, cv, cki: (lambda q, k, v, qi, wi, ki: sample_sparse_attn(
            q, k, v, qi, wi, ki, ck, cv, cki, page_table)))(cache_k[i], cache_v[i], cache_kidx[i])
        hs, k_, v_, ki_, c_ = trunk_layer(hs, p_sample[i], pos_sample, attn_s, *lw)
        ksm.append(k_); vsm.append(v_); kism.append(ki_); csm.append(c_)
    return (hp, hs, jnp.stack(kp), jnp.stack(vp), jnp.stack(kip), jnp.stack(cp),
            jnp.stack(ksm), jnp.stack(vsm), jnp.stack(kism), jnp.stack(csm))
```

```python
import math
from contextlib import ExitStack
import numpy as np
import ml_dtypes
import concourse.bass as bass
import concourse.mybir as mybir
from concourse.bass_utils import run_bass_kernel_spmd

F32 = mybir.dt.float32
BF16 = mybir.dt.bfloat16
I32 = mybir.dt.int32
ALU = mybir.AluOpType
AF = mybir.ActivationFunctionType
AX = mybir.AxisListType

D = 1024
DIN = 2088
DFF = 4096
DPLE = 256
ALPHA = (2.0 * 4) ** 0.25
EPS = 1e-5
NEG = -1.0e30
BIGV = 29952.0
ND = 8


class Res:
    __slots__ = ("n", "w", "rs")

    def __init__(self, n=""):
        self.n = n
        self.w = None
        self.rs = []


class Prog:
    ENG = ("pe", "act", "dve", "pool", "sp")

    def __init__(self):
        self.ops = {e: [] for e in self.ENG}
        self.cnt = {e: 0 for e in self.ENG}
        self.seen = {e: {} for e in self.ENG}
        self.dcnt = {}
        self.drr = {"sp": 0, "pool": 0}

    def _waits(self, eng, reads, writes):
        waits = []
        seen = self.seen[eng]

        def need(ev, raw):
            if ev is None:
                return
            k, v = ev
            if k == eng:
                if eng == "pe" or not raw:
                    return
            if seen.get(k, 0) >= v:
                return
            seen[k] = v
            waits.append((k, v))

        for r in reads:
            need(r.w, True)
        for w in writes:
            need(w.w, False)
            for ev in w.rs:
                need(ev, False)
        return waits

    def _commit(self, ev, reads, writes):
        for r in reads:
            r.rs.append(ev)
        for w in writes:
            w.w = ev
            w.rs = []

    def op(self, eng, fn, reads=(), writes=()):
        waits = self._waits(eng, reads, writes)
        self.cnt[eng] += 1
        ev = (eng, self.cnt[eng])
        self.ops[eng].append((waits, fn, eng, 1, 1))
        self._commit(ev, reads, writes)

    def dma(self, q, fn, reads=(), writes=(), n=1):
        i = self.drr[q]
        self.drr[q] = (i + 1) % ND
        key = (q, i)
        waits = self._waits(q, reads, writes)
        c = self.dcnt.get(key, 0)
        if c > 0 and self.seen[q].get(key, 0) < c:
            self.seen[q][key] = c
            waits.append((key, c))
        c += 16 * n
        self.dcnt[key] = c
        self.ops[q].append((waits, fn, key, 16, n))
        self._commit((key, c), reads, writes)

    def fence(self, dst, src):
        evs = []
        for s in src:
            if s.w is not None:
                evs.append(s.w)
            evs.extend(s.rs)
        for d in dst:
            d.rs.extend(evs)

    def emit(self, nc, es):
        sems = {}
        for e in self.ENG:
            sems[e] = es.enter_context(nc.semaphore("s_" + e))
        for q in ("sp", "pool"):
            for i in range(ND):
                sems[(q, i)] = es.enter_context(nc.semaphore("d_%s%d" % (q, i)))
        fin = []
        for k, v in self.dcnt.items():
            fin.append((k, v))
        for e in self.ENG:
            if e != "sp" and self.cnt[e] > 0:
                fin.append((e, self.cnt[e]))
        ops = self.ops

        def replay(name, eng):
            for waits, fn, key, inc, n in ops[name]:
                for k, v in waits:
                    eng.wait_ge(sems[k], v)
                r = fn(eng)
                if isinstance(r, (list, tuple)):
                    assert len(r) == n
                    for ins in r:
                        ins.then_inc(sems[key], inc)
                else:
                    assert n == 1
                    r.then_inc(sems[key], inc)
            if name == "sp":
                for k, v in fin:
                    eng.wait_ge(sems[k], v)

        with nc.Block() as block:
            @block.tensor
            def _(e):
                replay("pe", e)

            @block.scalar
            def _(e):
                replay("act", e)

            @block.vector
            def _(e):
                replay("dve", e)

            @block.gpsimd
            def _(e):
                replay("pool", e)

            @block.sync
            def _(e):
                replay("sp", e)


def bc(ap, shape, axis):
    return ap.unsqueeze(axis).to_broadcast(list(shape))


def build(cfg):
    S = cfg["S"]
    DEPTH = cfg["DEPTH"]
    PAST = cfg["PAST"]
    NPHYS = cfg["NPHYS"]
    NIT = cfg.get("NIT", 24)
    NTP = S // 128
    NT = NTP + 1
    ST = NTP
    NPG = PAST // 128
    KP = min(256, S // 4)
    KS = min(256, (PAST + 4) // 4)
    NBS = NPG + 1
    LMAX = NTP * 128
    assert NPG == 64

    nc = bass.Bass("TRN2", target_bir_lowering=False)
    dt = lambda n, s, d, k: nc.dram_tensor(n, list(s), d, kind=k).ap()
    x_p = dt("x_p", [S, D], F32, "ExternalInput")
    x_s = dt("x_s", [16, D], F32, "ExternalInput")
    p_p = dt("p_p", [DEPTH, S, DPLE], F32, "ExternalInput")
    p_s = dt("p_s", [DEPTH, 16, DPLE], F32, "ExternalInput")
    c_k = dt("c_k", [DEPTH, NPHYS, 128, 128], F32, "ExternalInput")
    c_v = dt("c_v", [DEPTH, NPHYS, 128, 128], F32, "ExternalInput")
    c_ki = dt("c_ki", [DEPTH, NPHYS, 128, 32], F32, "ExternalInput")
    ptab = dt("ptab", [1, 4 * NPG], I32, "ExternalInput")
    w_in = dt("w_in", [DEPTH, D, DIN], F32, "ExternalInput")
    sgu_g = dt("sgu_g", [DEPTH, 512], F32, "ExternalInput")
    sgu_bb = dt("sgu_bb", [DEPTH, 512], F32, "ExternalInput")
    sgu_w = dt("sgu_w", [DEPTH, 4, 128, 128], F32, "ExternalInput")
    sgu_b = dt("sgu_b", [DEPTH, 4, 128], F32, "ExternalInput")
    w_o = dt("w_o", [DEPTH, D, D], F32, "ExternalInput")
    ln1_g = dt("ln1_g", [DEPTH, D], F32, "ExternalInput")
    ln1_b = dt("ln1_b", [DEPTH, D], F32, "ExternalInput")
    w_f1 = dt("w_f1", [DEPTH, D, DFF], F32, "ExternalInput")
    w_f2 = dt("w_f2", [DEPTH, DFF, D], F32, "ExternalInput")
    w_pg = dt("w_pg", [DEPTH, D, D], F32, "ExternalInput")
    w_pp = dt("w_pp", [DEPTH, DPLE, D], F32, "ExternalInput")
    ln2_g = dt("ln2_g", [DEPTH, D], F32, "ExternalInput")
    ln2_b = dt("ln2_b", [DEPTH, D], F32, "ExternalInput")
    c_rq = dt("c_rq", [128, NT, 128], F32, "ExternalInput")
    c_ri = dt("c_ri", [128, NT, 64], F32, "ExternalInput")
    c_msk = dt("c_msk", [128, 8, 128], F32, "ExternalInput")
    c_misc = dt("c_misc", [128, 64], F32, "ExternalInput")
    c_sel = dt("c_sel", [128, 768], F32, "ExternalInput")
    y_p = dt("y_p", [S, D], F32, "ExternalOutput")
    y_s = dt("y_s", [16, D], F32, "ExternalOutput")
    nk_p = dt("nk_p", [DEPTH, S, 128], F32, "ExternalOutput")
    nv_p = dt("nv_p", [DEPTH, S, 128], F32, "ExternalOutput")
    nki_p = dt("nki_p", [DEPTH, S, 32], F32, "ExternalOutput")
    ncv_p = dt("ncv_p", [DEPTH, 128, 512], F32, "ExternalOutput")
    nk_s = dt("nk_s", [DEPTH, 16, 128], F32, "ExternalOutput")
    nv_s = dt("nv_s", [DEPTH, 16, 128], F32, "ExternalOutput")
    nki_s = dt("nki_s", [DEPTH, 16, 32], F32, "ExternalOutput")
    ncv_s = dt("ncv_s", [DEPTH, 16, 512], F32, "ExternalOutput")
    Hs = dt("Hs", [NT, 128, D], F32, "Internal")
    HT = dt("HT", [NT, 128, D], BF16, "Internal")

    P = Prog()
    es = ExitStack()
    with es:
        off = [0]
        TOTW = 52700
        big = es.enter_context(nc.sbuf_tensor("big", [128, TOTW], F32))

        def alloc(words):
            a = off[0]
            off[0] += int(words)
            assert off[0] <= TOTW, ("sbuf overflow", off[0])
            return a

        def f32v(a, n):
            return big[:, a:a + n]

        def bfv(a, n):
            return big[:, a:a + (n + 1) // 2].bitcast(BF16)

        def A32(n):
            return f32v(alloc(n), n)

        def A16(n):
            return bfv(alloc((n + 1) // 2), n)

        a_arena = alloc(12448)
        w_in_sb = bfv(a_arena, 8 * DIN).rearrange("p (k n) -> p k n", k=8)
        w_o_sb = bfv(a_arena + 4 * DIN, 8 * D).rearrange("p (k n) -> p k n", k=8)
        slabW = 4096
        slab = [(bfv(a_arena + i * slabW, 8 * 512).rearrange("p (k n) -> p k n", k=8),
                 bfv(a_arena + i * slabW + 2048, 4 * 1024).rearrange("p (k n) -> p k n", k=4)) for i in range(2)]
        R_win, R_wo = Res("win"), Res("wo")
        R_slab = [Res("slab0"), Res("slab1")]
        R_ple = Res("ple")
        kT_all = A16(NT * 128)
        kiT_all = A16(NT * 128)
        Vaug_all = A16(NT * 130).rearrange("p (t k e) -> p t k e", t=NT, k=2)
        R_ks = [Res("ks%d" % i) for i in range(NT)]
        lng = A32(D)
        lnb = A32(D)
        sg_g = A32(512)
        sg_b = A32(512)
        WsT = A16(512).rearrange("p (g t) -> p g t", g=4)
        WsTs = A16(512).rearrange("p (g t) -> p g t", g=4)
        bs_p = A32(4)
        bs_s = A32(4)
        R_ln, R_sg, R_ws = Res("ln"), Res("sg"), Res("ws")
        rq = A32(NT * 128).rearrange("p (t c) -> p t c", t=NT)
        ri = A32(NT * 64).rearrange("p (t c) -> p t c", t=NT)
        msk = A32(1024).rearrange("p (m c) -> p m c", m=8)
        misc = A32(64)
        ident = A16(128)
        bigI = A16(128)
        idx2 = es.enter_context(nc.sbuf_tensor("idx2", [128, 104], I32))
        R_c = Res("consts")
        QIall = A16(128)
        QIB = A16(512).rearrange("p (b n) -> p b n", b=4)
        QBD = A16(256).rearrange("p (b n) -> p b n", b=4)
        WPAD = A32(240)
        BIGSEL = A16(256).rearrange("p (b n) -> p b n", b=4)
        SELR = A16(512)
        On = A16(256).rearrange("p (b n) -> p b n", b=4)
        pow2 = misc[:, 0:NIT]
        rowmask = misc[:, 32:36]
        hT_sb = A16(1024).rearrange("p (k t) -> p k t", k=8)
        au = A32(512)
        av = A32(512)
        vn32 = A32(512)
        vnb = A16(512)
        zq = A32(512)
        zr = A32(552)
        rtA = A32(512)
        rtB = A32(512)
        qr = A16(512)
        qir = A16(256)
        kb = A16(128)
        kirep = A16(128)
        k32 = A32(128)
        ki32 = A32(32)
        qTs = [A16(512).rearrange("p (g t) -> p g t", g=4) for _ in range(2)]
        qiT = A16(384).rearrange("p (g t) -> p g t", g=3)
        wi = A32(8)
        mixs = [A16(1024) for _ in range(2)]
        MBf = [A16(2048) for _ in range(2)]
        R_MBf = [Res("mbf0"), Res("mbf1")]
        st8 = A32(32)
        R = {n: Res(n) for n in ("hT", "au", "av", "vn32", "vnb", "zq", "zr", "rtA", "rtB", "qr", "qir",
                                 "kb", "kirep", "k32", "ki32", "qT0", "qT1", "qiT", "wi", "mix0", "mix1", "st8", "junk2",
                                 "mixT", "hs", "pre", "xh", "hb", "h1T", "acc", "junk", "thr", "bis",
                                 "bo", "stage0", "stage1", "pe32", "peb", "peT", "sgm")}
        a_c = alloc(0)
        hs = A32(D)
        pre = A32(D)
        xh = A32(D)
        hb = A16(D)
        h1T = A16(D).rearrange("p (k t) -> p k t", k=8)
        mixT = A16(1024).rearrange("p (k t) -> p k t", k=8)
        assert max(LMAX, 1152) <= 2 * (off[0] - a_c)
        a_acc = alloc(0)
        AW = max(LMAX, 2048)
        acc = A32(AW)
        junk = bfv(a_c, AW)
        R_junk = [R[n_] for n_ in ("hs", "pre", "xh", "hb", "h1T", "mixT")]
        rl = [A16(512), A16(512), A16(512)]
        Wd = A16(1024).rearrange("p (h c) -> p h c", h=8)
        R_Wd = Res("Wd")
        R_rl = [Res("rl0"), Res("rl1"), Res("rl2")]
        R_MB = [Res("mb0"), Res("mb1")]
        pTt = [A16(512), A16(512), A16(512)]
        R_pT = [Res("pt0"), Res("pt1"), Res("pt2")]
        thr = A32(4)
        bisT = A32(NIT + 2)
        bisC = A32(NIT + 2)
        bisS = A32(NIT + 2)
        bisM = A32(8)
        KIg = A16(2 * 32 * 32).rearrange("p (q r d) -> p q r d", q=2, r=32)
        kiTc_s = A16(2 * 8 * 128).rearrange("p (q r n) -> p q r n", q=2, r=8)
        Kg = A16(2 * 8 * 128).rearrange("p (q r d) -> p q r d", q=2, r=8)
        Vg = A16(4 * 8 * 128).rearrange("p (b r d) -> p b r d", b=4, r=8)
        kTc_s = A16(2 * 8 * 128).rearrange("p (q r n) -> p q r n", q=2, r=8)
        Vaugc_s = A16(32 * 130).rearrange("p (b r k e) -> p b r k e", b=4, r=8, k=2)
        rls = A32(512)
        Atl = A32(128)
        MBc = A16(512)
        pTs = A16(1024).rearrange("p (h n) -> p h n", h=2)
        bm = A32(8)
        g2 = A32(16)
        g2rep = A32(128)
        RS = {n_: Res(n_) for n_ in ("kis", "kiTc", "Kst", "Vst", "kTc", "Vc", "QIall", "QIB", "QBD", "rls", "WPAD", "Atl", "MBc",
                                     "pTs0", "pTs1", "On", "bm", "g2", "sel", "MBc2")}
        alloc(max(0, 12800 - (off[0] - a_acc)))
        a_end = alloc(0)
        dw = [a_acc]

        def dalloc(words):
            a = dw[0]
            dw[0] += int(words)
            assert dw[0] <= a_end, ("D work overflow", dw[0] - a_acc, a_end - a_acc, off[0])
            return a
        h1Tgs = [bfv(dalloc(2048), 4096).rearrange("p (k n) -> p k n", k=8)] * 2
        uTs = [bfv(dalloc(1024), 2048).rearrange("p (f c) -> p f c", f=4) for _ in range(2)]
        h1Tg = h1Tgs[0]
        rD = [f32v(dalloc(512), 512) for _ in range(2)]
        stage = [f32v(dalloc(D), D) for _ in range(2)]
        pe32 = f32v(dalloc(256), 256)
        peb = bfv(dalloc(128), 256)
        peT = bfv(dalloc(128), 256).rearrange("p (k t) -> p k t", k=2)
        sgm = rD[0]
        a_ple = dalloc(5120)
        wpg_sb = bfv(a_ple, 8 * D).rearrange("p (k n) -> p k n", k=8)
        wpp_sb = bfv(a_ple + 4 * D, 2 * D).rearrange("p (k n) -> p k n", k=2)
        R_h1Tgs, R_uTs = [Res("h1Tg0")] * 2, [Res("uT0"), Res("uT1")]
        R_h1Tg = R_h1Tgs[0]
        R_rD = [Res(), Res()]
        R["sgm"] = R_rD[0]
        D_res = R_h1Tgs + R_uTs + [R_rD[0], R_rD[1], R["stage0"], R["stage1"], R["pe32"], R["peb"], R["peT"], R_ple]
        B_res = [R["acc"], R["thr"], R["bis"], R["bo"]] + R_rl + R_MB + R_pT + [RS[n_] for n_ in ("kis", "kiTc", "Kst", "Vst", "kTc", "Vc", "rls", "MBc", "MBc2", "pTs0", "pTs1")]

        Ob = [es.enter_context(nc.psum_tensor("Ob%d" % i, [128, 512], F32)) for i in range(2)]
        Tb = [es.enter_context(nc.psum_tensor("Tb%d" % i, [128, 1024], BF16)) for i in range(2)]
        Rb = [es.enter_context(nc.psum_tensor("Rb%d" % i, [128, 512], F32)) for i in range(4)]
        R_Ob = [Res("O0"), Res("O1")]
        R_Tb = [Res("T0"), Res("T1")]
        R_Rb = [Res("R%d" % i) for i in range(4)]
        rr = {"T": 0, "R": 0, "rl": 0, "pT": 0, "MB": 0}

        def nextT():
            i = rr["T"]
            rr["T"] = (i + 1) % 2
            return Tb[i], R_Tb[i]

        def nextR():
            i = rr["R"]
            rr["R"] = (i + 1) % 4
            return Rb[i], R_Rb[i]

        rr["RD"] = 0
        RDb = Rb + Ob
        R_RDb = R_Rb + R_Ob

        def nextRD():
            i = rr["RD"]
            rr["RD"] = (i + 1) % 6
            return RDb[i], R_RDb[i]

        def transposes(items, dst, dst_res, evac="act", extra_reads=(), nrow=128):
            tb, rtb = nextT()
            n = len(items)
            for i, (ap, res) in enumerate(items):
                P.op("pe", lambda e, ap=ap, i=i, tb=tb: e.transpose(tb[0:int(np.prod(ap.shape[1:])), i * 128:(i + 1) * 128], ap, ident),
                     reads=[res, R_c], writes=[rtb])
            if evac == "act":
                P.op("act", lambda e, tb=tb, n=n: e.activation(out=dst, in_=tb[0:nrow, 0:n * 128], func=AF.Copy),
                     reads=[rtb], writes=[dst_res])
            else:
                P.op("dve", lambda e, tb=tb, n=n: e.tensor_copy(out=dst, in_=tb[0:nrow, 0:n * 128]),
                     reads=[rtb], writes=[dst_res])

        def lnorm(src, rsrc, G, W, gam, bet, rpar, out32, rout, tmp, rtmp):
            s3 = src.rearrange("p (g w) -> p g w", g=G)
            P.op("dve", lambda e: e.tensor_reduce(out=st8[:, 0:G], in_=s3, axis=AX.X, op=ALU.add),
                 reads=[rsrc], writes=[R["st8"]])
            for g in range(G):
                P.op("act", lambda e, g=g: e.activation(out=tmp[:, g * W:(g + 1) * W], in_=src[:, g * W:(g + 1) * W],
                                                        func=AF.Square, accum_out=st8[:, 4 + g:5 + g]),
                     reads=[rsrc, R["st8"]], writes=[rtmp, R["st8"]])
            iw = 1.0 / W
            P.op("dve", lambda e: e.tensor_scalar(out=st8[:, 8:8 + G], in0=st8[:, 0:G], scalar1=iw, scalar2=None, op0=ALU.mult),
                 reads=[R["st8"]], writes=[R["st8"]])
            P.op("dve", lambda e: e.tensor_tensor(out=st8[:, 12:12 + G], in0=st8[:, 8:8 + G], in1=st8[:, 8:8 + G], op=ALU.mult),
                 reads=[R["st8"]], writes=[R["st8"]])
            P.op("dve", lambda e: e.scalar_tensor_tensor(out=st8[:, 16:16 + G], in0=st8[:, 4:4 + G], scalar=iw, in1=st8[:, 12:12 + G],
                                                         op0=ALU.mult, op1=ALU.subtract),
                 reads=[R["st8"]], writes=[R["st8"]])
            P.op("act", lambda e: e.activation(out=st8[:, 28:28 + G], in_=st8[:, 16:16 + G], func=AF.Sqrt, bias=misc[:, 42:43], scale=1.0),
                 reads=[R["st8"], R_c], writes=[R["st8"]])
            P.op("dve", lambda e: e.reciprocal(out=st8[:, 20:20 + G], in_=st8[:, 28:28 + G]),
                 reads=[R["st8"]], writes=[R["st8"]])
            P.op("dve", lambda e: e.scalar_tensor_tensor(out=st8[:, 24:24 + G], in0=st8[:, 8:8 + G], scalar=-1.0, in1=st8[:, 20:20 + G],
                                                         op0=ALU.mult, op1=ALU.mult),
                 reads=[R["st8"]], writes=[R["st8"]])
            for g in range(G):
                P.op("act", lambda e, g=g: e.activation(out=tmp[:, g * W:(g + 1) * W], in_=src[:, g * W:(g + 1) * W], func=AF.Identity,
                                                        scale=st8[:, 20 + g:21 + g], bias=st8[:, 24 + g:25 + g]),
                     reads=[rsrc, R["st8"]], writes=[rtmp])
            P.op("pool", lambda e: e.tensor_tensor(out=tmp, in0=tmp, in1=gam, op=ALU.mult), reads=[rtmp, rpar], writes=[rtmp])
            P.op("pool", lambda e: e.tensor_tensor(out=out32, in0=tmp, in1=bet, op=ALU.add), reads=[rtmp, rpar], writes=[rout])

        def rope(src, rsrc, H, Dh, tab, out, rout, qperm=False):
            hf = Dh // 2
            s3 = src.rearrange("p (h d) -> p h d", h=H)
            a3 = rtA[:, 0:H * Dh].rearrange("p (h d) -> p h d", h=H)
            b3 = rtB[:, 0:H * Dh].rearrange("p (h d) -> p h d", h=H)
            cos2 = bc(tab[:, 0:Dh], [128, H, Dh], 1)
            sn1 = bc(tab[:, Dh:Dh + hf], [128, H, hf], 1)
            sn2 = bc(tab[:, Dh + hf:2 * Dh], [128, H, hf], 1)
            P.op("pool", lambda e: e.tensor_tensor(out=a3, in0=s3, in1=cos2, op=ALU.mult), reads=[rsrc, R_c], writes=[R["rtA"]])
            P.op("pool", lambda e: e.tensor_tensor(out=b3[:, :, 0:hf], in0=s3[:, :, hf:Dh], in1=sn1, op=ALU.mult),
                 reads=[rsrc, R_c], writes=[R["rtB"]])
            P.op("pool", lambda e: e.tensor_tensor(out=b3[:, :, hf:Dh], in0=s3[:, :, 0:hf], in1=sn2, op=ALU.mult),
                 reads=[rsrc, R_c], writes=[R["rtB"]])
            if qperm:
                ov = out.rearrange("p (g k d) -> p k g d", g=4, k=2)
                i0 = rtA[:, 0:H * Dh].rearrange("p (k g d) -> p k g d", k=2, g=4)
                i1 = rtB[:, 0:H * Dh].rearrange("p (k g d) -> p k g d", k=2, g=4)
            else:
                ov, i0, i1 = out, rtA[:, 0:H * Dh], rtB[:, 0:H * Dh]
            P.op("pool", lambda e: e.tensor_tensor(out=ov, in0=i0, in1=i1, op=ALU.add),
                 reads=[R["rtA"], R["rtB"]], writes=[rout])

        P.dma("sp", lambda e: e.dma_start(out=rq, in_=c_rq[:, :, :]), writes=[R_c])
        P.dma("sp", lambda e: e.dma_start(out=ri, in_=c_ri[:, :, :]), writes=[R_c])
        P.dma("sp", lambda e: e.dma_start(out=msk, in_=c_msk[:, :, :]), writes=[R_c])
        P.dma("sp", lambda e: e.dma_start(out=misc, in_=c_misc[:, :]), writes=[R_c])
        P.dma("sp", lambda e: e.dma_start(out=zq.bitcast(I32)[:, 0:2], in_=ptab.rearrange("o (q p) -> p (o q)", q=2),
                                          allow_slow_non_contiguous=True), writes=[R["zq"]])
        P.op("dve", lambda e: e.tensor_copy(out=au[:, 0:2], in_=zq.bitcast(I32)[:, 0:2]), reads=[R["zq"]], writes=[R["au"]])
        P.op("pool", lambda e: e.iota(av[:, 0:16], [[1, 16]], base=0, channel_multiplier=0, allow_small_or_imprecise_dtypes=True), writes=[R["av"]])
        P.op("dve", lambda e: e.tensor_scalar(out=au[:, 2:4], in0=au[:, 0:2], scalar1=16.0, scalar2=None, op0=ALU.mult), reads=[R["au"]], writes=[R["au"]])
        P.op("dve", lambda e: e.tensor_scalar(out=au[:, 4:6], in0=au[:, 0:2], scalar1=4.0, scalar2=None, op0=ALU.mult), reads=[R["au"]], writes=[R["au"]])
        P.dma("sp", lambda e: e.dma_start(out=zq.bitcast(I32)[0:64, 2:6], in_=ptab.rearrange("o (b p) -> p (o b)", b=4),
                                          allow_slow_non_contiguous=True), writes=[R["zq"]])
        P.op("dve", lambda e: e.tensor_copy(out=au[0:64, 8:12], in_=zq.bitcast(I32)[0:64, 2:6]), reads=[R["zq"]], writes=[R["au"]])
        P.op("dve", lambda e: e.tensor_scalar(out=au[0:64, 12:16], in0=au[0:64, 8:12], scalar1=16.0, scalar2=None, op0=ALU.mult),
             reads=[R["au"]], writes=[R["au"]])
        for b in range(4):
            P.op("dve", lambda e, b=b: e.tensor_scalar(out=idx2[0:64, 40 + b * 16:56 + b * 16], in0=av[0:64, 0:16], scalar1=au[0:64, 12 + b:13 + b],
                                                       scalar2=None, op0=ALU.add), reads=[R["au"], R["av"]], writes=[R_c])
        for pr in range(2):
            P.op("dve", lambda e, pr=pr: e.tensor_scalar(out=idx2[:, pr * 16:(pr + 1) * 16], in0=av[:, 0:16], scalar1=au[:, 2 + pr:3 + pr], scalar2=None,
                                                         op0=ALU.add), reads=[R["au"], R["av"]], writes=[R_c])
            P.op("dve", lambda e, pr=pr: e.tensor_scalar(out=idx2[:, 32 + pr * 4:36 + pr * 4], in0=av[:, 0:4], scalar1=au[:, 4 + pr:5 + pr], scalar2=None,
                                                         op0=ALU.add), reads=[R["au"], R["av"]], writes=[R_c])
        P.op("pool", lambda e: e.iota(ident, [[1, 128]], base=0, channel_multiplier=-1, allow_small_or_imprecise_dtypes=True),
             writes=[R_c])
        P.op("dve", lambda e: e.tensor_scalar(out=bigI, in0=ident, scalar1=0.0, scalar2=BIGV, op0=ALU.is_equal, op1=ALU.mult),
             reads=[R_c], writes=[R_c])
        P.op("dve", lambda e: e.tensor_scalar(out=ident, in0=ident, scalar1=0.0, scalar2=None, op0=ALU.is_equal),
             reads=[R_c], writes=[R_c])
        P.op("pool", lambda e: e.memset(Vaug_all, 1.0), writes=R_ks)
        P.op("pool", lambda e: e.memset(hs, 0.0), writes=[R["hs"]])
        P.op("pool", lambda e: e.memset(WsTs, 0.0), writes=[R_ws])
        P.op("pool", lambda e: e.memset(bs_s, 0.0), writes=[R_ws])

        P.dma("sp", lambda e: e.dma_start(out=acc[:, 0:768], in_=c_sel[:, :]), writes=[R["acc"]])
        P.op("dve", lambda e: e.tensor_copy(out=BIGSEL.rearrange("p b n -> p (b n)"), in_=acc[:, 0:256]), reads=[R["acc"]], writes=[RS["sel"]])
        P.op("dve", lambda e: e.tensor_copy(out=SELR, in_=acc[:, 256:768]), reads=[R["acc"]], writes=[RS["sel"]])
        P.op("pool", lambda e: e.memset(QIB, 0.0), writes=[RS["QIB"]])
        P.op("pool", lambda e: e.memset(QBD, 0.0), writes=[RS["QBD"]])
        P.op("pool", lambda e: e.memset(WPAD, 0.0), writes=[RS["WPAD"]])
        P.op("pool", lambda e: e.memset(On, 0.0), writes=[RS["On"]])

        def rows(i):
            return 16 if i == ST else 128

        def to_hT_and_store(src32, rsrc, i):
            P.op("act", lambda e: e.activation(out=hb, in_=src32, func=AF.Copy), reads=[rsrc], writes=[R["hb"]])
            transposes([(hb[:, k * 128:(k + 1) * 128], R["hb"]) for k in range(8)],
                       h1T.rearrange("p k t -> p (k t)"), R["h1T"])
            P.dma("sp", lambda e, i=i: e.dma_start(out=HT[i], in_=h1T.rearrange("p k t -> p (k t)")), reads=[R["h1T"]], writes=[R_HT[i]])

        R_Hs = [Res("Hs%d" % i) for i in range(NT)]
        R_HT = [Res("HT%d" % i) for i in range(NT)]
        for i in range(NT):
            n = rows(i)
            src = x_s[:, :] if i == ST else x_p[i * 128:(i + 1) * 128, :]
            P.dma("sp", lambda e, src=src, n=n: e.dma_start(out=hs[0:n, :], in_=src), writes=[R["hs"]])
            P.dma("sp", lambda e, i=i: e.dma_start(out=Hs[i], in_=hs), reads=[R["hs"]], writes=[R_Hs[i]])
            to_hT_and_store(hs, R["hs"], i)

        def att_index(j):
            nblk, ktop, bias_m = j + 1, KP, 0
            L = nblk * 128
            nch = (nblk + 3) // 4
            P.op("dve", lambda e: e.tensor_tensor(out=Wd, in0=bc(ident, [128, 8, 128], 1), in1=bc(wi[:, 0:8], [128, 8, 128], 2), op=ALU.mult),
                 reads=[R["wi"], R_c], writes=[R_Wd])
            for c in range(nch):
                c0 = c * 512
                w = min(512, L - c0)
                kiT_c, r_ki = kiT_all[:, c0:c0 + 512], R_ks[c * 4:min(c * 4 + 4, j + 1)]
                oa, roa = Ob[c % 2], R_Ob[c % 2]
                def accum(h, ir, oa=oa, roa=roa, w=w):
                    P.op("pe", lambda e: e.matmul(oa[:, 0:w], lhsT=Wd[:, h, :], rhs=rl[ir][:, 0:w], start=(h == 0), stop=(h == 7)),
                         reads=[R_Wd, R_rl[ir]], writes=[roa])
                prev = None
                for h in range(8):
                    pb, rpb = nextR()
                    hq, hh = h % 3, h // 3
                    P.op("pe", lambda e, pb=pb, hq=hq, hh=hh, kiT_c=kiT_c, w=w: e.matmul(
                        pb[:, 0:w], lhsT=qiT[hq * 32:(hq + 1) * 32, hh, :], rhs=kiT_c[hq * 32:(hq + 1) * 32, 0:w], start=True, stop=True),
                        reads=[R["qiT"]] + r_ki, writes=[rpb])
                    ir = rr["rl"]
                    rr["rl"] = (ir + 1) % 3
                    P.op("act", lambda e, pb=pb, ir=ir, w=w: e.activation(out=rl[ir][:, 0:w], in_=pb[:, 0:w], func=AF.Relu),
                         reads=[rpb], writes=[R_rl[ir]])
                    if prev is not None:
                        accum(*prev)
                    prev = (h, ir)
                accum(*prev)
                P.op("act", lambda e, oa=oa, c0=c0, w=w: e.activation(out=acc[:, c0:c0 + w], in_=oa[:, 0:w], func=AF.Copy),
                     reads=[roa], writes=[R["acc"]])
            lastc = acc[:, L - 128:L]
            if L > ktop:
                P.op("dve", lambda e: e.tensor_reduce(out=bisM[:, 0:1], in_=acc[:, 0:L], axis=AX.X, op=ALU.max),
                     reads=[R["acc"]], writes=[R["bis"]])
                P.op("dve", lambda e: e.tensor_reduce(out=bisM[:, 1:2], in_=acc[:, 0:L], axis=AX.X, op=ALU.min),
                     reads=[R["acc"]], writes=[R["bis"]])
            P.op("dve", lambda e: e.tensor_tensor(out=lastc, in0=lastc, in1=msk[:, bias_m, :], op=ALU.add),
                 reads=[R["acc"], R_c], writes=[R["acc"]])
            if L > ktop:
                P.op("dve", lambda e: e.tensor_tensor(out=bisM[:, 2:3], in0=bisM[:, 0:1], in1=bisM[:, 1:2], op=ALU.subtract),
                     reads=[R["bis"]], writes=[R["bis"]])
                P.op("dve", lambda e: e.scalar_tensor_tensor(out=bisT[:, 0:1], in0=bisM[:, 2:3], scalar=0.5, in1=bisM[:, 1:2],
                                                             op0=ALU.mult, op1=ALU.add),
                     reads=[R["bis"]], writes=[R["bis"]])
                P.op("dve", lambda e: e.tensor_scalar(out=bisS[:, 0:NIT], in0=pow2, scalar1=bisM[:, 2:3], scalar2=None, op0=ALU.mult),
                     reads=[R["bis"], R_c], writes=[R["bis"]])
                P.op("dve", lambda e: e.memset(bisC[:, 0:NIT], 0.0), writes=[R["bis"]])
                for k in range(NIT):
                    P.op("dve", lambda e, k=k: e.tensor_scalar(out=MBf[j % 2][:, 0:L], in0=acc[:, 0:L], scalar1=bisT[:, k:k + 1], scalar2=0.0,
                                                               op0=ALU.is_ge, op1=ALU.add, accum_out=bisC[:, k:k + 1]),
                         reads=[R["acc"], R["bis"]], writes=[R_MBf[j % 2], R["bis"]])
                    P.op("dve", lambda e, k=k: e.tensor_scalar(out=bisM[:, 4:5], in0=bisC[:, k:k + 1], scalar1=float(ktop) - 0.5,
                                                               scalar2=0.5, op0=ALU.is_ge, op1=ALU.subtract),
                         reads=[R["bis"]], writes=[R["bis"]])
                    P.op("dve", lambda e, k=k: e.scalar_tensor_tensor(out=bisT[:, k + 1:k + 2], in0=bisM[:, 4:5], scalar=bisS[:, k:k + 1],
                                                                      in1=bisT[:, k:k + 1], op0=ALU.mult, op1=ALU.add),
                         reads=[R["bis"]], writes=[R["bis"]])
                P.op("dve", lambda e: e.scalar_tensor_tensor(out=thr[:, 0:1], in0=bisM[:, 2:3], scalar=-(2.0 ** -(NIT + 1)),
                                                             in1=bisT[:, NIT:NIT + 1], op0=ALU.mult, op1=ALU.add),
                     reads=[R["bis"]], writes=[R["thr"]])
            else:
                P.op("dve", lambda e: e.memset(thr[:, 0:1], -1.0e29), writes=[R["thr"]])
            P.op("dve", lambda e: e.tensor_scalar(out=MBf[j % 2][:, 0:L], in0=acc[:, 0:L], scalar1=thr[:, 0:1], scalar2=misc[:, 40:41],
                                                  op0=ALU.is_ge, op1=ALU.subtract),
                 reads=[R["acc"], R["thr"], R_c], writes=[R_MBf[j % 2]])

        def att_core(j):
            nblk = j + 1
            L = nblk * 128
            nch = (nblk + 3) // 4
            qT = qTs[j % 2]
            rqT = R["qT%d" % (j % 2)]
            MB = MBf[j % 2]
            rMB = R_MBf[j % 2]
            first = [True, True]

            def pv(h, ip, nb, V_c, r_kv):
                kv, g, ob = h // 4, h % 4, h // 4
                for b in range(nb):
                    st_ = first[ob]
                    first[ob] = False
                    P.op("pe", lambda e, b=b, st_=st_: e.matmul(
                        Ob[ob][:, g * 65:(g + 1) * 65], lhsT=pTt[ip][:, b * 128:(b + 1) * 128], rhs=V_c[:, b, kv, :],
                        start=st_, stop=False, skip_group_check=True),
                        reads=[R_pT[ip]] + r_kv, writes=[R_Ob[ob]])

            prev = None
            for c in range(nch):
                c0 = c * 512
                w = min(512, L - c0)
                nb = w // 128
                kT_c, V_c, r_kv = kT_all[:, c0:c0 + 512], Vaug_all[:, c * 4:c * 4 + 4], R_ks[c * 4:min(c * 4 + 4, j + 1)]
                for h in range(8):
                    kv, g = h // 4, h % 4
                    sb, rsb = nextR()
                    for b in range(nb):
                        P.op("pe", lambda e, sb=sb, b=b, kv=kv, g=g, kT_c=kT_c: e.matmul(
                            sb[:, b * 128:(b + 1) * 128], lhsT=kT_c[kv * 64:(kv + 1) * 64, b * 128:(b + 1) * 128],
                            rhs=qT[kv * 64:(kv + 1) * 64, g, :], start=True, stop=False, skip_group_check=True),
                            reads=[rqT] + r_kv, writes=[rsb])
                        P.op("pe", lambda e, sb=sb, b=b, c0=c0: e.matmul(
                            sb[:, b * 128:(b + 1) * 128], lhsT=MB[:, c0 + b * 128:c0 + (b + 1) * 128], rhs=bigI,
                            start=False, stop=True, skip_group_check=True),
                            reads=[rMB, R_c], writes=[rsb])
                    ip = rr["pT"]
                    rr["pT"] = (ip + 1) % 3
                    P.op("act", lambda e, sb=sb, ip=ip, w=w: e.activation(out=pTt[ip][:, 0:w], in_=sb[:, 0:w], func=AF.Exp, scale=0.125),
                         reads=[rsb], writes=[R_pT[ip]])
                    if prev is not None:
                        pv(*prev)
                    prev = (h, ip, nb, V_c, r_kv)
            pv(*prev)
            o_normalize(mixs[j % 2][:, 512:1024], R["mix%d" % (j % 2)])

        def o_normalize(dst, rdst):
            for ob in range(2):
                o3 = Ob[ob][:, 0:260].rearrange("p (g e) -> p g e", g=4)
                P.op("dve", lambda e, o3=o3: e.reciprocal(out=bisM[:, 4:8], in_=o3[:, :, 64]), reads=[R_Ob[ob]], writes=[R["bis"]])
                d3 = dst[:, ob * 256:(ob + 1) * 256].rearrange("p (g d) -> p g d", g=4)
                P.op("dve", lambda e, o3=o3, d3=d3: e.tensor_tensor(out=d3, in0=o3[:, :, 0:64], in1=bc(bisM[:, 4:8], [128, 4, 64], 2),
                                                                   op=ALU.mult),
                     reads=[R_Ob[ob], R["bis"]], writes=[rdst])

        def sample_attention(l):
            qT = qTs[ST % 2]
            mix = mixs[ST % 2]
            rqT = R["qT%d" % (ST % 2)]
            rmix = R["mix%d" % (ST % 2)]
            ckv = c_k.rearrange("l n (c r) d -> (l n c) (r d)", c=16)
            cvv = c_v.rearrange("l n (c r) d -> (l n c) (r d)", c=16)
            ckiv = c_ki.rearrange("l n (c r) d -> (l n c) (r d)", c=4)

            def gather(tab2, dst, col0, c, eoff):
                def f(e):
                    return [e.indirect_dma_start(out=dst[:, pr].rearrange("p r d -> p (r d)"), out_offset=None, in_=tab2,
                                                 in_offset=bass.IndirectOffsetOnAxis(ap=idx2[:, col0 + pr * (16 if col0 == 0 else 4) + c:
                                                                                             col0 + pr * (16 if col0 == 0 else 4) + c + 1], axis=0),
                                                 element_offset=eoff) for pr in range(2)]
                return f

            def gatherV(c, eoff):
                def f(e):
                    return [e.indirect_dma_start(out=Vg[0:64, b].rearrange("p r d -> p (r d)"), out_offset=None, in_=cvv,
                                                 in_offset=bass.IndirectOffsetOnAxis(ap=idx2[0:64, 40 + b * 16 + c:41 + b * 16 + c], axis=0),
                                                 element_offset=eoff) for b in range(4)]
                return f

            tb, rtb = nextT()
            for h in range(8):
                P.op("pe", lambda e, h=h, tb=tb: e.transpose(tb[0:32, h * 128:(h + 1) * 128], qir[:, h * 32:(h + 1) * 32], ident),
                     reads=[R["qir"], R_c], writes=[rtb])
            P.op("act", lambda e, tb=tb: e.activation(out=QIall[0:32, :].rearrange("p (r h) -> p r h", h=8),
                                                      in_=tb[0:32, :].rearrange("p (h r) -> p r h", h=8)[:, 0:16, :], func=AF.Copy),
                 reads=[rtb], writes=[RS["QIall"]])
            for b in range(4):
                P.op("act", lambda e, b=b: e.activation(out=QIB[0:32, b, b * 32:(b + 1) * 32], in_=QIall[0:32, b * 32:(b + 1) * 32], func=AF.Copy),
                     reads=[RS["QIall"]], writes=[RS["QIB"]])
            for kv in range(2):
                P.op("act", lambda e, kv=kv: e.activation(
                    out=QBD[kv * 64:(kv + 1) * 64, :, kv * 32:kv * 32 + 16].rearrange("p b (g t) -> p b g t", g=4),
                    in_=qT[kv * 64:(kv + 1) * 64, :, 0:16].rearrange("p g (b t) -> p b g t", b=4), func=AF.Copy),
                    reads=[rqT], writes=[RS["QBD"]])
            P.op("dve", lambda e: e.tensor_tensor(out=Atl[0:16, :].rearrange("p (r h) -> p r h", h=8), in0=bc(wi[0:16, :], [16, 16, 8], 1),
                                                  in1=msk[0:16, 4, :].rearrange("p (r h) -> p r h", h=8), op=ALU.mult),
                 reads=[R["wi"], R_c], writes=[RS["Atl"]])
            pw, rpw = nextR()
            P.op("pe", lambda e, pw=pw: e.matmul(pw[:, 0:16], lhsT=Atl[0:16, :], rhs=msk[0:16, 5, 0:16], start=True, stop=True),
                 reads=[RS["Atl"], R_c], writes=[rpw])
            P.op("act", lambda e, pw=pw: e.activation(out=WPAD[:, 112:128], in_=pw[:, 0:16], func=AF.Copy), reads=[rpw], writes=[RS["WPAD"]])

            for c in range(16):
                part, grp = c % 8, c // 8
                if c % 4 == 0:
                    P.dma("pool", gather(ckiv, KIg, 32, c // 4, l * NPHYS * 4096), reads=[R_c], writes=[RS["kis"]], n=2)
                for pr in range(2):
                    transposes([(KIg[:, pr, (c % 4) * 8 + rl_, :], RS["kis"]) for rl_ in range(8)],
                               kiTc_s[0:32, pr].rearrange("p r n -> p (r n)"), RS["kiTc"], nrow=32)
                pb, rpb = nextR()
                for rl_ in range(8):
                    for b in range(4):
                        P.op("pe", lambda e, pb=pb, b=b, rl_=rl_: e.matmul(pb[:, rl_ * 64:(rl_ + 1) * 64], lhsT=QIB[0:32, b, :],
                                                                          rhs=kiTc_s[0:32, b // 2, rl_, (b % 2) * 64:(b % 2) * 64 + 64],
                                                                          start=(b == 0), stop=(b == 3), skip_group_check=True),
                             reads=[RS["QIB"], RS["kiTc"]], writes=[rpb])
                P.op("act", lambda e, pb=pb: e.activation(out=rls, in_=pb[:, 0:512], func=AF.Relu), reads=[rpb], writes=[RS["rls"]])
                P.op("pe", lambda e, part=part, grp=grp: e.matmul(Ob[grp][:, 0:512], lhsT=WPAD[:, (7 - part) * 16:(7 - part) * 16 + 128], rhs=rls,
                                                                 start=(part == 0), stop=(part == 7)),
                     reads=[RS["WPAD"], RS["rls"]], writes=[R_Ob[grp]])
                if part == 7:
                    P.op("act", lambda e, grp=grp: e.activation(out=acc[:, grp * 512:(grp + 1) * 512], in_=Ob[grp][:, 0:512], func=AF.Copy),
                         reads=[R_Ob[grp]], writes=[R["acc"]])
            pb, rpb = nextR()
            P.op("pe", lambda e, pb=pb: e.matmul(pb[:, 0:128], lhsT=QIall[0:32, :], rhs=kiT_all[0:32, ST * 128:(ST + 1) * 128], start=True, stop=True),
                 reads=[RS["QIall"], R_ks[ST]], writes=[rpb])
            P.op("act", lambda e, pb=pb: e.activation(out=rls[:, 0:128], in_=pb[:, 0:128], func=AF.Relu), reads=[rpb], writes=[RS["rls"]])
            pn, rpn = nextR()
            P.op("pe", lambda e, pn=pn: e.matmul(pn[:, 0:128], lhsT=WPAD[:, 112:240], rhs=rls[:, 0:128], start=True, stop=True),
                 reads=[RS["WPAD"], RS["rls"]], writes=[rpn])
            P.op("dve", lambda e, pn=pn: e.tensor_tensor(out=acc[:, 1024:1152], in0=pn[:, 0:128], in1=msk[:, 7, :], op=ALU.add),
                 reads=[rpn, R_c], writes=[R["acc"]])

            LS = 1152
            P.op("dve", lambda e: e.tensor_reduce(out=bm[:, 0:1], in_=acc[:, 0:LS], axis=AX.X, op=ALU.max), reads=[R["acc"]], writes=[RS["bm"]])
            P.op("dve", lambda e: e.tensor_reduce(out=bm[:, 2:3], in_=acc[:, 0:1024], axis=AX.X, op=ALU.min), reads=[R["acc"]], writes=[RS["bm"]])
            P.op("dve", lambda e: e.tensor_scalar(out=bm[:, 1:2], in0=bm[:, 2:3], scalar1=-1.0, scalar2=None, op0=ALU.mult),
                 reads=[RS["bm"]], writes=[RS["bm"]])
            p1, rp1 = nextR()
            P.op("pe", lambda e, p1=p1: e.matmul(p1[0:2, 0:128], lhsT=bm[:, 0:2], rhs=msk[:, 5, :], start=True, stop=True),
                 reads=[RS["bm"], R_c], writes=[rp1])
            P.op("dve", lambda e, p1=p1: e.tensor_reduce(out=g2[0:2, 0:16], in_=p1[0:2, 0:128].rearrange("o (p r) -> o r p", p=8), axis=AX.X, op=ALU.max),
                 reads=[rp1], writes=[RS["g2"]])
            P.op("dve", lambda e: e.tensor_copy(out=g2rep[0:2, :].rearrange("o (p r) -> o p r", p=8), in_=bc(g2[0:2, 0:16], [2, 8, 16], 1)),
                 reads=[RS["g2"]], writes=[RS["g2"]])
            p2, rp2 = nextR()
            P.op("pe", lambda e, p2=p2: e.matmul(p2[:, 0:2], lhsT=g2rep[0:2, :], rhs=msk[0:2, 5, 0:2], start=True, stop=True),
                 reads=[RS["g2"], R_c], writes=[rp2])
            P.op("dve", lambda e, p2=p2: e.tensor_copy(out=bisM[:, 0:2], in_=p2[:, 0:2]), reads=[rp2], writes=[R["bis"]])
            P.op("dve", lambda e: e.tensor_tensor(out=bisM[:, 2:3], in0=bisM[:, 0:1], in1=bisM[:, 1:2], op=ALU.add),
                 reads=[R["bis"]], writes=[R["bis"]])
            P.op("dve", lambda e: e.tensor_scalar(out=bisM[:, 3:4], in0=bisM[:, 1:2], scalar1=-1.0, scalar2=None, op0=ALU.mult),
                 reads=[R["bis"]], writes=[R["bis"]])
            P.op("dve", lambda e: e.scalar_tensor_tensor(out=bisT[:, 0:1], in0=bisM[:, 2:3], scalar=0.5, in1=bisM[:, 3:4], op0=ALU.mult, op1=ALU.add),
                 reads=[R["bis"]], writes=[R["bis"]])
            P.op("dve", lambda e: e.tensor_scalar(out=bisS[:, 0:NIT], in0=pow2, scalar1=bisM[:, 2:3], scalar2=None, op0=ALU.mult),
                 reads=[R["bis"], R_c], writes=[R["bis"]])
            P.op("dve", lambda e: e.memset(bisC[:, 0:NIT], 0.0), writes=[R["bis"]])
            for k in range(NIT):
                P.op("dve", lambda e, k=k: e.tensor_scalar(out=junk[:, 0:LS], in0=acc[:, 0:LS], scalar1=bisT[:, k:k + 1], scalar2=0.0,
                                                           op0=ALU.is_ge, op1=ALU.add, accum_out=bisC[:, k:k + 1]),
                     reads=[R["acc"], R["bis"]], writes=R_junk + [R["bis"]])
                pc, rpc = nextR()
                P.op("pe", lambda e, k=k, pc=pc: e.matmul(pc[:, 0:1], lhsT=msk[:, 6, :], rhs=bisC[:, k:k + 1], start=True, stop=True),
                     reads=[R["bis"], R_c], writes=[rpc])
                P.op("dve", lambda e, pc=pc: e.tensor_scalar(out=bisM[:, 4:5], in0=pc[:, 0:1], scalar1=float(KS) - 0.5, scalar2=0.5,
                                                             op0=ALU.is_ge, op1=ALU.subtract),
                     reads=[rpc], writes=[R["bis"]])
                P.op("dve", lambda e, k=k: e.scalar_tensor_tensor(out=bisT[:, k + 1:k + 2], in0=bisM[:, 4:5], scalar=bisS[:, k:k + 1],
                                                                  in1=bisT[:, k:k + 1], op0=ALU.mult, op1=ALU.add),
                     reads=[R["bis"]], writes=[R["bis"]])
            P.op("dve", lambda e: e.scalar_tensor_tensor(out=thr[:, 0:1], in0=bisM[:, 2:3], scalar=-(2.0 ** -(NIT + 1)),
                                                         in1=bisT[:, NIT:NIT + 1], op0=ALU.mult, op1=ALU.add),
                 reads=[R["bis"]], writes=[R["thr"]])

            first = [True, True]

            def mask_chunk(c0, w, part):
                P.op("dve", lambda e: e.tensor_scalar(out=MBc[:, 0:w], in0=acc[:, c0:c0 + w], scalar1=thr[:, 0:1], scalar2=misc[:, 40:41],
                                                      op0=ALU.is_ge, op1=ALU.subtract),
                     reads=[R["acc"], R["thr"], R_c], writes=[RS["MBc"]])
                P.op("dve", lambda e: e.tensor_scalar(out=MBc[:, 0:w], in0=MBc[:, 0:w], scalar1=misc[:, 44 + part:45 + part], scalar2=None, op0=ALU.mult),
                     reads=[RS["MBc"], R_c], writes=[RS["MBc"]])

            def pv(ob, lhs, rhs, rd):
                st_ = first[ob // 2]
                first[ob // 2] = False
                P.op("pe", lambda e: e.matmul(Ob[ob // 2][0:64, (ob % 2) * 130:(ob % 2) * 130 + 130], lhsT=lhs, rhs=rhs,
                                              start=st_, stop=False, skip_group_check=True),
                     reads=rd, writes=[R_Ob[ob // 2]])

            for c in range(16):
                part, grp = c % 8, c // 8
                mask_chunk(grp * 512, 512, part)
                P.dma("pool", gather(ckv, Kg, 0, c, l * NPHYS * 16384), reads=[R_c], writes=[RS["Kst"]], n=2)
                P.dma("pool", gatherV(c, l * NPHYS * 16384), reads=[R_c], writes=[RS["Vst"]], n=4)
                for pr in range(2):
                    transposes([(Kg[:, pr, rl_, :], RS["Kst"]) for rl_ in range(8)],
                               kTc_s[:, pr].rearrange("p r n -> p (r n)"), RS["kTc"], evac="dve" if pr else "act")
                P.op("dve", lambda e: e.tensor_copy(out=Vaugc_s[0:64, :, :, :, 0:64].rearrange("p b r k d -> p (b r) k d"),
                                                     in_=Vg[0:64].rearrange("p b r (k d) -> p (b r) k d", k=2)),
                     reads=[RS["Vst"]], writes=[RS["Vc"]])
                for pr in range(2):
                    for r4 in range(2):
                        sb, rsb = nextR()
                        hb = (pr * 2 + r4) % 2
                        for rq in range(4):
                            rl_ = r4 * 4 + rq
                            for b2 in range(2):
                                b = 2 * pr + b2
                                o_ = (rq * 2 + b2) * 64
                                P.op("pe", lambda e, sb=sb, b=b, b2=b2, pr=pr, rl_=rl_, o_=o_: e.matmul(
                                    sb[0:64, o_:o_ + 64], lhsT=kTc_s[:, pr, rl_, b2 * 64:(b2 + 1) * 64], rhs=QBD[:, b, :],
                                    start=True, stop=False, skip_group_check=True), reads=[RS["kTc"], RS["QBD"]], writes=[rsb])
                                P.op("pe", lambda e, sb=sb, b=b, rl_=rl_, o_=o_: e.matmul(
                                    sb[0:64, o_:o_ + 64], lhsT=MBc[:, rl_ * 64:(rl_ + 1) * 64], rhs=BIGSEL[:, b, :],
                                    start=False, stop=True, skip_group_check=True), reads=[RS["MBc"], RS["sel"]], writes=[rsb])
                        P.op("act", lambda e, sb=sb, hb=hb: e.activation(out=pTs[0:64, hb, :], in_=sb[0:64, 0:512], func=AF.Exp, scale=0.125),
                             reads=[rsb], writes=[RS["pTs%d" % hb]])
                        for rq in range(4):
                            rl_ = r4 * 4 + rq
                            for b2 in range(2):
                                b = 2 * pr + b2
                                o_ = (rq * 2 + b2) * 64
                                pv(b, pTs[0:64, hb, o_:o_ + 64],
                                   Vaugc_s[0:64, b, rl_].rearrange("p k e -> p (k e)"), [RS["pTs%d" % hb], RS["Vc"]])
            mask_chunk(1024, 128, 0)
            sb, rsb = nextR()
            for b in range(4):
                P.op("pe", lambda e, sb=sb, b=b: e.matmul(sb[:, b * 64:(b + 1) * 64], lhsT=kT_all[:, ST * 128:(ST + 1) * 128], rhs=QBD[:, b, :],
                                                          start=True, stop=False, skip_group_check=True), reads=[R_ks[ST], RS["QBD"]], writes=[rsb])
                P.op("pe", lambda e, sb=sb, b=b: e.matmul(sb[:, b * 64:(b + 1) * 64], lhsT=MBc[:, 0:128], rhs=BIGSEL[:, b, :],
                                                          start=False, stop=True, skip_group_check=True), reads=[RS["MBc"], RS["sel"]], writes=[rsb])
            P.op("act", lambda e, sb=sb: e.activation(out=pTs[:, 0, 0:256], in_=sb[:, 0:256], func=AF.Exp, scale=0.125), reads=[rsb], writes=[RS["pTs0"]])
            for b in range(4):
                pv(b, pTs[:, 0, b * 64:(b + 1) * 64], Vaug_all[:, ST].rearrange("p k e -> p (k e)"), [RS["pTs0"], R_ks[ST]])
            for b in range(4):
                o3 = Ob[b // 2][0:64, (b % 2) * 130:(b % 2) * 130 + 130].rearrange("p (k e) -> p k e", k=2)
                for kv in range(2):
                    r0 = kv * 32
                    P.op("dve", lambda e, o3=o3, kv=kv, r0=r0: e.reciprocal(out=bm[r0:r0 + 16, 4:5], in_=o3[r0:r0 + 16, kv, 64:65]),
                         reads=[R_Ob[b // 2]], writes=[RS["bm"]])
                    P.op("dve", lambda e, o3=o3, kv=kv, r0=r0, b=b: e.tensor_scalar(out=On[r0:r0 + 16, b, :], in0=o3[r0:r0 + 16, kv, 0:64],
                                                                                   scalar1=bm[r0:r0 + 16, 4:5], scalar2=None, op0=ALU.mult),
                         reads=[R_Ob[b // 2], RS["bm"]], writes=[RS["On"]])
            bp, rbp = nextR()
            for h in range(8):
                for b in range(4):
                    P.op("pe", lambda e, bp=bp, h=h, b=b: e.matmul(bp[0:16, h * 64:(h + 1) * 64], lhsT=SELR[0:64, (b * 8 + h) * 16:(b * 8 + h) * 16 + 16],
                                                                  rhs=On[0:64, b, :], start=(b == 0), stop=(b == 3), skip_group_check=True),
                         reads=[RS["sel"], RS["On"]], writes=[rbp])
            P.op("act", lambda e, bp=bp: e.activation(out=mix[0:16, 512:1024], in_=bp[0:16, 0:512], func=AF.Copy), reads=[rbp], writes=[rmix])

        for l in range(DEPTH):
            last = l == DEPTH - 1
            def load_attn_weights(ll):
                P.fence([R_win, R_wo], [R_slab[0], R_slab[1]])
                P.dma("pool", lambda e: e.dma_start(out=w_in_sb, in_=w_in[ll].rearrange("(k p) n -> p k n", p=128)), writes=[R_win])
                P.dma("pool", lambda e: e.dma_start(out=w_o_sb, in_=w_o[ll].rearrange("(k p) n -> p k n", p=128)), writes=[R_wo])

            def load_slab(p_, l=l):
                s_ = p_ % 2
                W1s_, W2s_ = slab[s_]
                P.dma("pool", lambda e: e.dma_start(
                    out=W1s_, in_=w_f1[l, :, p_ * 512:(p_ + 1) * 512].rearrange("(k p) n -> p k n", p=128)), writes=[R_slab[s_]])
                P.dma("pool", lambda e: e.dma_start(
                    out=W2s_, in_=w_f2[l, p_ * 512:(p_ + 1) * 512, :].rearrange("(k p) n -> p k n", p=128)), writes=[R_slab[s_]])

            if l == 0:
                load_attn_weights(0)
            P.dma("sp", lambda e, l=l: e.dma_start(out=lng, in_=ln1_g[l:l + 1, :].to_broadcast([128, D])), writes=[R_ln])
            P.dma("sp", lambda e, l=l: e.dma_start(out=lnb, in_=ln1_b[l:l + 1, :].to_broadcast([128, D])), writes=[R_ln])
            P.dma("sp", lambda e, l=l: e.dma_start(out=sg_g, in_=sgu_g[l:l + 1, :].to_broadcast([128, 512])), writes=[R_sg])
            P.dma("sp", lambda e, l=l: e.dma_start(out=sg_b, in_=sgu_bb[l:l + 1, :].to_broadcast([128, 512])), writes=[R_sg])
            P.dma("sp", lambda e, l=l: e.dma_start(out=zq.rearrange("p (g s) -> p g s", g=4), in_=sgu_w[l].rearrange("g t s -> t g s")),
                  writes=[R["zq"]])
            P.op("dve", lambda e: e.tensor_tensor(out=zq.rearrange("p (g s) -> p g s", g=4), in0=zq.rearrange("p (g s) -> p g s", g=4),
                                                  in1=bc(msk[:, 2, :], [128, 4, 128], 1), op=ALU.mult),
                 reads=[R["zq"], R_c], writes=[R["zq"]])
            P.op("dve", lambda e: e.tensor_copy(out=qr, in_=zq), reads=[R["zq"]], writes=[R["qr"]])
            transposes([(qr[:, g * 128:(g + 1) * 128], R["qr"]) for g in range(4)], WsT.rearrange("p g t -> p (g t)"), R_ws)
            P.dma("sp", lambda e, l=l: [e.dma_start(out=zq[b * 4:(b + 1) * 4, :].rearrange("p (g s) -> p g s", g=4)[:, :, b2 * 4:(b2 + 1) * 4],
                                                    in_=sgu_w[l, :, 0:4, 0:4].rearrange("g t s -> t g s"))
                                        for b in range(4) for b2 in range(4)],
                  reads=[R["qr"]], writes=[R["zq"]], n=16)
            P.op("dve", lambda e: e.tensor_tensor(out=zq[0:16, :].rearrange("p (g s) -> p g s", g=4)[:, :, 0:16],
                                                  in0=zq[0:16, :].rearrange("p (g s) -> p g s", g=4)[:, :, 0:16],
                                                  in1=bc(msk[0:16, 3, 0:16], [16, 4, 16], 1), op=ALU.mult),
                 reads=[R["zq"], R_c], writes=[R["zq"]])
            P.op("dve", lambda e: e.memset(qr, 0.0), reads=[], writes=[R["qr"]])
            P.op("dve", lambda e: e.tensor_copy(out=qr[0:16, :].rearrange("p (g s) -> p g s", g=4)[:, :, 0:16],
                                                in_=zq[0:16, :].rearrange("p (g s) -> p g s", g=4)[:, :, 0:16]),
                 reads=[R["zq"]], writes=[R["qr"]])
            transposes([(qr[:, g * 128:(g + 1) * 128], R["qr"]) for g in range(4)], WsTs.rearrange("p g t -> p (g t)"), R_ws)
            P.dma("sp", lambda e, l=l: e.dma_start(out=bs_p, in_=sgu_b[l].rearrange("g t -> t g"), allow_slow_non_contiguous=True),
                  writes=[R_ws])
            P.dma("sp", lambda e, l=l: [e.dma_start(out=bs_s[b * 4:(b + 1) * 4, :], in_=sgu_b[l, :, 0:4].rearrange("g t -> t g"),
                                                    allow_slow_non_contiguous=True) for b in range(4)],
                  writes=[R_ws], n=4)
            P.fence(B_res, D_res)
            P.op("pool", lambda e: e.memset(Vaugc_s, 1.0), writes=[RS["Vc"]])
            P.fence(R_ks, [])

            def phaseA(i):
                n = rows(i)
                is_s = i == ST
                qT = qTs[i % 2]
                mix = mixs[i % 2]
                rqT = R["qT%d" % (i % 2)]
                rmix = R["mix%d" % (i % 2)]
                P.dma("sp", lambda e, i=i: e.dma_start(out=hT_sb.rearrange("p k t -> p (k t)"), in_=HT[i]), reads=[R_HT[i]], writes=[R["hT"]])
                chunks = [(0, 512), (512, 512), (1024, 512), (1536, 512), (2048, 40)]
                zb = []
                for (c0, w) in chunks:
                    pb, rpb = nextR()
                    for k in range(8):
                        P.op("pe", lambda e, pb=pb, k=k, c0=c0, w=w: e.matmul(pb[:, 0:w], lhsT=hT_sb[:, k, :], rhs=w_in_sb[:, k, c0:c0 + w],
                                                                            start=(k == 0), stop=(k == 7)),
                             reads=[R["hT"], R_win], writes=[rpb])
                    zb.append((pb, rpb))
                    if c0 == 0:
                        P.op("act", lambda e, pb=pb: e.activation(out=au, in_=pb[:, 0:512], func=AF.Gelu), reads=[rpb], writes=[R["au"]])
                    elif c0 == 512:
                        P.op("act", lambda e, pb=pb: e.activation(out=av, in_=pb[:, 0:512], func=AF.Gelu), reads=[rpb], writes=[R["av"]])
                    elif c0 == 1024:
                        P.op("act", lambda e, pb=pb: e.activation(out=zq, in_=pb[:, 0:512], func=AF.Copy), reads=[rpb], writes=[R["zq"]])
                    elif c0 == 1536:
                        P.op("act", lambda e, pb=pb: e.activation(out=zr[:, 0:512], in_=pb[:, 0:512], func=AF.Copy), reads=[rpb], writes=[R["zr"]])
                    else:
                        P.op("act", lambda e, pb=pb: e.activation(out=zr[:, 512:552], in_=pb[:, 0:40], func=AF.Copy), reads=[rpb], writes=[R["zr"]])
                lnorm(av, R["av"], 4, 128, sg_g, sg_b, R_sg, vn32, R["vn32"], rtA, R["rtA"])
                P.op("pool", lambda e: e.tensor_copy(out=vnb, in_=vn32), reads=[R["vn32"]], writes=[R["vnb"]])
                if i == NTP - 1:
                    P.dma("sp", lambda e, l=l: e.dma_start(out=ncv_p[l], in_=vn32), reads=[R["vn32"]])
                if is_s:
                    P.dma("sp", lambda e, l=l: e.dma_start(out=ncv_s[l], in_=vn32[0:16, :]), reads=[R["vn32"]])
                gb, rgb = nextR()
                wst = WsTs if is_s else WsT
                bsx = bs_s if is_s else bs_p
                for g in range(4):
                    P.op("pe", lambda e, g=g, gb=gb, wst=wst: e.matmul(gb[:, g * 128:(g + 1) * 128], lhsT=wst[:, g, :],
                                                                      rhs=vnb[:, g * 128:(g + 1) * 128], start=True, stop=True,
                                                                      skip_group_check=True),
                         reads=[R_ws, R["vnb"]], writes=[rgb])
                for g in range(4):
                    P.op("dve", lambda e, g=g, gb=gb, bsx=bsx: e.scalar_tensor_tensor(
                        out=mix[:, g * 128:(g + 1) * 128], in0=gb[:, g * 128:(g + 1) * 128], scalar=bsx[:, g:g + 1],
                        in1=au[:, g * 128:(g + 1) * 128], op0=ALU.add, op1=ALU.mult),
                        reads=[rgb, R_ws, R["au"]], writes=[rmix])
                rope(zq, R["zq"], 8, 64, rq[:, i, :], qr, R["qr"], qperm=True)
                rope(zr[:, 0:128], R["zr"], 2, 64, rq[:, i, :], k32, R["k32"])
                P.op("act", lambda e: e.activation(out=kb, in_=k32, func=AF.Copy), reads=[R["k32"]], writes=[R["kb"]])
                rope(zr[:, 256:512], R["zr"], 8, 32, ri[:, i, :], qir, R["qir"])
                rope(zr[:, 512:544], R["zr"], 1, 32, ri[:, i, :], ki32, R["ki32"])
                P.op("act", lambda e: e.activation(out=kirep.rearrange("p (r d) -> p r d", r=4), in_=bc(ki32, [128, 4, 32], 1), func=AF.Copy),
                     reads=[R["ki32"]], writes=[R["kirep"]])
                P.op("act", lambda e: e.activation(out=wi, in_=zr[:, 544:552], func=AF.Copy), reads=[R["zr"]], writes=[R["wi"]])
                P.op("act", lambda e, i=i: e.activation(out=Vaug_all[:, i, :, 0:64], in_=zr[:, 128:256].rearrange("p (k d) -> p k d", k=2),
                                                        func=AF.Copy), reads=[R["zr"]], writes=[R_ks[i]])
                if is_s:
                    P.dma("sp", lambda e, l=l: e.dma_start(out=nk_s[l], in_=k32[0:16, :]), reads=[R["k32"]])
                    P.dma("sp", lambda e, l=l: e.dma_start(out=nv_s[l], in_=zr[0:16, 128:256]), reads=[R["zr"]])
                    P.dma("sp", lambda e, l=l: e.dma_start(out=nki_s[l], in_=ki32[0:16, :]), reads=[R["ki32"]])
                else:
                    P.dma("sp", lambda e, l=l, i=i: e.dma_start(out=nk_p[l, i * 128:(i + 1) * 128, :], in_=k32), reads=[R["k32"]])
                    P.dma("sp", lambda e, l=l, i=i: e.dma_start(out=nv_p[l, i * 128:(i + 1) * 128, :], in_=zr[:, 128:256]), reads=[R["zr"]])
                    P.dma("sp", lambda e, l=l, i=i: e.dma_start(out=nki_p[l, i * 128:(i + 1) * 128, :], in_=ki32), reads=[R["ki32"]])
                tb, rtb = nextT()
                its = [(qr[:, g * 128:(g + 1) * 128], R["qr"]) for g in range(4)] + [(kb, R["kb"])] + \
                      [(qir[:, 0:96], R["qir"]), (qir[:, 96:192], R["qir"]), (qir[:, 192:256], R["qir"])]
                for j_, (ap, res) in enumerate(its):
                    P.op("pe", lambda e, ap=ap, j_=j_, tb=tb: e.transpose(tb[0:int(np.prod(ap.shape[1:])), j_ * 128:(j_ + 1) * 128], ap, ident),
                         reads=[res, R_c], writes=[rtb])
                P.op("act", lambda e, tb=tb: e.activation(out=qT.rearrange("p g t -> p (g t)"), in_=tb[:, 0:512], func=AF.Copy),
                     reads=[rtb], writes=[rqT])
                P.op("act", lambda e, tb=tb, i=i: e.activation(out=kT_all[:, i * 128:(i + 1) * 128], in_=tb[:, 512:640], func=AF.Copy),
                     reads=[rtb], writes=[R_ks[i]])
                P.op("act", lambda e, tb=tb: e.activation(out=qiT.rearrange("p g t -> p (g t)"), in_=tb[:, 640:1024], func=AF.Copy),
                     reads=[rtb], writes=[R["qiT"]])
                transposes([(kirep, R["kirep"])], kiT_all[:, i * 128:(i + 1) * 128], R_ks[i])


            def phaseC(i):
                n = rows(i)
                is_s = i == ST
                qT = qTs[i % 2]
                mix = mixs[i % 2]
                rqT = R["qT%d" % (i % 2)]
                rmix = R["mix%d" % (i % 2)]
                transposes([(mix[:, k * 128:(k + 1) * 128], rmix) for k in range(8)], mixT.rearrange("p k t -> p (k t)"), R["mixT"])
                P.dma("sp", lambda e, i=i: e.dma_start(out=hs, in_=Hs[i]), reads=[R_Hs[i]], writes=[R["hs"]])
                for hf in range(2):
                    pb, rpb = nextR()
                    for k in range(8):
                        P.op("pe", lambda e, pb=pb, k=k, hf=hf: e.matmul(pb[:, 0:512], lhsT=mixT[:, k, :], rhs=w_o_sb[:, k, hf * 512:(hf + 1) * 512],
                                                                        start=(k == 0), stop=(k == 7)),
                             reads=[R["mixT"], R_wo], writes=[rpb])
                    P.op("dve", lambda e, pb=pb, hf=hf: e.scalar_tensor_tensor(out=pre[:, hf * 512:(hf + 1) * 512], in0=hs[:, hf * 512:(hf + 1) * 512],
                                                                               scalar=ALPHA, in1=pb[:, 0:512], op0=ALU.mult, op1=ALU.add),
                         reads=[rpb, R["hs"]], writes=[R["pre"]])
                lnorm(pre, R["pre"], 1, D, lng, lnb, R_ln, hs, R["hs"], xh, R["xh"])
                P.op("act", lambda e: e.activation(out=xh, in_=hs, func=AF.Copy, scale=ALPHA), reads=[R["hs"]], writes=[R["xh"]])
                P.dma("sp", lambda e, i=i: e.dma_start(out=Hs[i], in_=xh), reads=[R["xh"]], writes=[R_Hs[i]])
                to_hT_and_store(hs, R["hs"], i)


            phaseA(0)
            att_index(0)
            for j in range(NTP):
                if j + 1 < NTP:
                    phaseA(j + 1)
                    att_index(j + 1)
                else:
                    phaseA(ST)
                    P.fence([R_slab[0], R_slab[1]], [R_win])
                    load_slab(0)
                    load_slab(1)
                att_core(j)
                phaseC(j)
            sample_attention(l)
            phaseC(ST)

            P.fence(D_res, B_res)
            P.op("pool", lambda e: e.memset(pe32, 0.0), writes=[R["pe32"]])
            P.dma("sp", lambda e, l=l: e.dma_start(out=lng, in_=ln2_g[l:l + 1, :].to_broadcast([128, D])), writes=[R_ln])
            P.dma("sp", lambda e, l=l: e.dma_start(out=lnb, in_=ln2_b[l:l + 1, :].to_broadcast([128, D])), writes=[R_ln])
            groups = [list(range(g0, min(g0 + 4, NTP))) for g0 in range(0, NTP, 4)] + [[ST]]
            sgi = [0]

            def accum(i, src, rsrc):
                P.dma("pool", lambda e, i=i: e.dma_start(out=Hs[i], in_=src, accum_op=ALU.add), reads=[rsrc], writes=[R_Hs[i]])

            gsel = [0]
            for p_ in range(8):
                s = p_ % 2
                W1s, W2s = slab[s]
                if 1 <= p_ <= 6:
                    load_slab(p_ + 1)
                if p_ == 0:
                    P.dma("pool", lambda e, l=l: e.dma_start(out=wpg_sb, in_=w_pg[l].rearrange("(k p) n -> p k n", p=128)), writes=[R_ple])
                    P.dma("pool", lambda e, l=l: e.dma_start(out=wpp_sb, in_=w_pp[l].rearrange("(k p) n -> p k n", p=128)), writes=[R_ple])
                for grp in groups:
                    ng = len(grp)
                    gi_ = gsel[0]
                    gsel[0] = 1 - gi_
                    h1Tg_, R_h1Tg_, uT, R_uT = h1Tgs[gi_], R_h1Tgs[gi_], uTs[gi_], R_uTs[gi_]
                    P.dma("sp", lambda e, grp=grp, ng=ng, h1Tg_=h1Tg_: [e.dma_start(
                        out=h1Tg_[:, :, ti * 128:(ti + 1) * 128], in_=HT[t].rearrange("p (k c) -> p k c", k=8)) for ti, t in enumerate(grp)],
                        n=ng,
                        reads=[R_HT[t] for t in grp], writes=[R_h1Tg_])
                    for fb in range(4):
                        pb, rpb = nextRD()
                        for k in range(8):
                            P.op("pe", lambda e, pb=pb, k=k, fb=fb, ng=ng, W1s=W1s, h1Tg_=h1Tg_: e.matmul(
                                pb[:, 0:ng * 128], lhsT=W1s[:, k, fb * 128:(fb + 1) * 128], rhs=h1Tg_[:, k, 0:ng * 128],
                                start=(k == 0), stop=(k == 7)), reads=[R_slab[s], R_h1Tg_], writes=[rpb])
                        ir = fb % 2
                        P.op("act", lambda e, pb=pb, ir=ir, ng=ng: e.activation(out=rD[ir][:, 0:ng * 128], in_=pb[:, 0:ng * 128], func=AF.Relu),
                             reads=[rpb], writes=[R_rD[ir]])
                        P.op("pool", lambda e, ir=ir, fb=fb, ng=ng, uT=uT: e.tensor_tensor(out=uT[:, fb, 0:ng * 128], in0=rD[ir][:, 0:ng * 128],
                                                                                   in1=rD[ir][:, 0:ng * 128], op=ALU.mult),
                             reads=[R_rD[ir]], writes=[R_uT])
                    for ti, t in enumerate(grp):
                        si = sgi[0]
                        sgi[0] = 1 - si
                        for hf in range(2):
                            pb, rpb = nextRD()
                            for fb in range(4):
                                P.op("pe", lambda e, pb=pb, fb=fb, ti=ti, hf=hf, W2s=W2s, uT=uT: e.matmul(
                                    pb[:, 0:512], lhsT=uT[:, fb, ti * 128:(ti + 1) * 128], rhs=W2s[:, fb, hf * 512:(hf + 1) * 512],
                                    start=(fb == 0), stop=(fb == 3)), reads=[R_slab[s], R_uT], writes=[rpb])
                            P.op("act", lambda e, pb=pb, si=si, hf=hf: e.activation(out=stage[si][:, hf * 512:(hf + 1) * 512], in_=pb[:, 0:512],
                                                                                    func=AF.Copy),
                                 reads=[rpb], writes=[R["stage%d" % si]])
                        accum(t, stage[si], R["stage%d" % si])
            if l + 1 < DEPTH:
                load_attn_weights(l + 1)
            for i in range(NT):
                n = rows(i)
                src = p_s[l] if i == ST else p_p[l, i * 128:(i + 1) * 128, :]
                P.dma("sp", lambda e, src=src, n=n: e.dma_start(out=pe32[0:n, :], in_=src), writes=[R["pe32"]])
                P.op("act", lambda e: e.activation(out=peb, in_=pe32, func=AF.Copy), reads=[R["pe32"]], writes=[R["peb"]])
                transposes([(peb[:, k * 128:(k + 1) * 128], R["peb"]) for k in range(2)], peT.rearrange("p k t -> p (k t)"), R["peT"])
                P.dma("sp", lambda e, i=i: e.dma_start(out=h1Tg[:, :, 0:128], in_=HT[i].rearrange("p (k c) -> p k c", k=8)), reads=[R_HT[i]], writes=[R_h1Tg])
                si = sgi[0]
                sgi[0] = 1 - si
                for hf in range(2):
                    pbg, rpbg = nextR()
                    for k in range(8):
                        P.op("pe", lambda e, pbg=pbg, k=k, hf=hf: e.matmul(pbg[:, 0:512], lhsT=h1Tg[:, k, 0:128], rhs=wpg_sb[:, k, hf * 512:(hf + 1) * 512],
                                                                          start=(k == 0), stop=(k == 7)), reads=[R_ple, R_h1Tg], writes=[rpbg])
                    pbp, rpbp = nextR()
                    for k in range(2):
                        P.op("pe", lambda e, pbp=pbp, k=k, hf=hf: e.matmul(pbp[:, 0:512], lhsT=peT[:, k, :], rhs=wpp_sb[:, k, hf * 512:(hf + 1) * 512],
                                                                          start=(k == 0), stop=(k == 1)), reads=[R_ple, R["peT"]], writes=[rpbp])
                    P.op("act", lambda e, pbg=pbg: e.activation(out=sgm, in_=pbg[:, 0:512], func=AF.Sigmoid), reads=[rpbg], writes=[R["sgm"]])
                    P.op("dve", lambda e, pbp=pbp, si=si, hf=hf: e.tensor_tensor(out=stage[si][:, hf * 512:(hf + 1) * 512], in0=sgm, in1=pbp[:, 0:512],
                                                                                 op=ALU.mult),
                         reads=[rpbp, R["sgm"]], writes=[R["stage%d" % si]])
                P.dma("sp", lambda e, i=i: e.dma_start(out=pre, in_=Hs[i]), reads=[R_Hs[i]], writes=[R["pre"]])
                P.op("dve", lambda e, si=si: e.tensor_tensor(out=pre, in0=pre, in1=stage[si], op=ALU.add),
                     reads=[R["pre"], R["stage%d" % si]], writes=[R["pre"]])
                lnorm(pre, R["pre"], 1, D, lng, lnb, R_ln, hs, R["hs"], xh, R["xh"])
                if last:
                    if i == ST:
                        P.dma("sp", lambda e: e.dma_start(out=y_s[:, :], in_=hs[0:16, :]), reads=[R["hs"]])
                    else:
                        P.dma("sp", lambda e, i=i: e.dma_start(out=y_p[i * 128:(i + 1) * 128, :], in_=hs), reads=[R["hs"]])
                else:
                    P.dma("sp", lambda e, i=i: e.dma_start(out=Hs[i], in_=hs), reads=[R["hs"]], writes=[R_Hs[i]])
                    to_hT_and_store(hs, R["hs"], i)

        P.emit(nc, es)
    return nc


def host_consts(S, PAST, NIT):
    NTP = S // 128
    NT = NTP + 1
    pos = np.zeros((NT, 128), np.float64)
    for i in range(NTP):
        pos[i] = i * 128 + np.arange(128)
    pos[NTP, :16] = PAST + (np.arange(16) % 4)

    def tab(dh):
        half = dh // 2
        inv = (10000.0 ** (-np.arange(half, dtype=np.float32) / half)).astype(np.float32)
        ang = pos.astype(np.float32)[:, :, None] * inv[None, None, :]
        c, s = np.cos(ang), np.sin(ang)
        t = np.concatenate([c, c, -s, s], axis=-1)
        return np.ascontiguousarray(t.transpose(1, 0, 2)).astype(np.float32)

    rq = tab(64)
    ri = tab(32)
    msk = np.zeros((128, 8, 128), np.float32)
    t = np.arange(128)[:, None]
    s = np.arange(128)[None, :]
    msk[:, 0, :] = np.where(s <= t, 0.0, NEG)
    nb = np.full((128, 128), NEG, np.float32)
    for r in range(16):
        for r2 in range(16):
            if r // 4 == r2 // 4 and r2 % 4 <= r % 4:
                nb[r, r2] = 0.0
    msk[:, 1, :] = nb
    msk[:, 2, :] = (s <= t).astype(np.float32)
    bd = np.zeros((128, 128), np.float32)
    for r in range(16):
        for r2 in range(16):
            if r // 4 == r2 // 4 and r2 % 4 <= r % 4:
                bd[r, r2] = 1.0
    msk[:, 3, :] = bd
    for r in range(16):
        msk[r, 4, r * 8:(r + 1) * 8] = 1.0
    msk[:, 5, :] = np.eye(128, dtype=np.float32)
    pidx = np.arange(128)
    msk[:, 6, :] = ((pidx[:, None] % 16) == (pidx[None, :] % 16)).astype(np.float32)
    msk[:, 7, :] = NEG
    msk[0:16, 7, :] = nb[0:16, :]
    sel = np.zeros((128, 768), np.float32)
    bigsel = np.zeros((128, 4, 64), np.float32)
    selr = np.zeros((64, 4, 8, 16), np.float32)
    for p in range(128):
        r = p % 16
        b, t = r // 4, r % 4
        for kv in range(2):
            for g in range(4):
                bigsel[p, b, kv * 32 + g * 4 + t] = BIGV
    for b in range(4):
        for h in range(8):
            kv, g = h // 4, h % 4
            for t in range(4):
                selr[kv * 32 + g * 4 + t, b, h, b * 4 + t] = 1.0
    sel[:, 0:256] = bigsel.reshape(128, 256)
    sel[0:64, 256:768] = selr.reshape(64, 512)
    misc = np.zeros((128, 64), np.float32)
    misc[:, :NIT] = (2.0 ** -(np.arange(NIT) + 1.0))[None, :]
    for b in range(4):
        misc[b * 4:(b + 1) * 4, 32 + b] = 1.0
    misc[:, 40] = 1.0
    misc[:, 41] = 128.0
    misc[:, 42] = EPS
    for part in range(8):
        misc[part * 16:(part + 1) * 16, 44 + part] = 1.0
    return rq, ri, msk, misc, sel


_CACHE = {}


def run(cfg, ncores, inputs):
    key = tuple(sorted(cfg.items()))
    if key not in _CACHE:
        _CACHE[key] = build(cfg)
    nc = _CACHE[key]
    S, DEPTH, PAST, NPHYS = cfg["S"], cfg["DEPTH"], cfg["PAST"], cfg["NPHYS"]
    NIT = cfg.get("NIT", 24)
    NPG = PAST // 128
    rq, ri, msk, misc, sel = host_consts(S, PAST, NIT)
    f = lambda a: np.ascontiguousarray(np.asarray(a, dtype=np.float32))
    ck = f(inputs["cache_k"]).reshape(DEPTH, NPHYS, 128, 128)
    cv = f(inputs["cache_v"]).reshape(DEPTH, NPHYS, 128, 128)
    cki = f(inputs["cache_kidx"])
    shared = {
        "c_k": ck, "c_v": cv, "c_ki": cki,
        "w_in": f(inputs["w_in"]), "sgu_g": f(inputs["sgu_ln_g"]).reshape(DEPTH, 512), "sgu_bb": f(inputs["sgu_ln_b"]).reshape(DEPTH, 512),
        "sgu_w": f(inputs["sgu_w"]), "sgu_b": f(inputs["sgu_b"]), "w_o": f(inputs["w_o"]),
        "ln1_g": f(inputs["ln1_g"]), "ln1_b": f(inputs["ln1_b"]), "w_f1": f(inputs["w_ff1"]), "w_f2": f(inputs["w_ff2"]),
        "w_pg": f(inputs["w_ple_gate"]), "w_pp": f(inputs["w_ple_proj"]), "ln2_g": f(inputs["ln2_g"]), "ln2_b": f(inputs["ln2_b"]),
        "c_rq": rq, "c_ri": ri, "c_msk": msk, "c_misc": misc, "c_sel": sel,
    }
    xp, xs = f(inputs["x_prompt"]), f(inputs["x_sample"])
    pp, ps = f(inputs["p_prompt"]), f(inputs["p_sample"])
    pt = np.asarray(inputs["page_table"]).astype(np.int32)
    in_maps = []
    for c in range(ncores):
        m = dict(shared)
        m["x_p"] = np.ascontiguousarray(xp[c])
        m["x_s"] = np.ascontiguousarray(xs[4 * c:4 * c + 4].reshape(16, D))
        m["p_p"] = np.ascontiguousarray(pp[:, c])
        m["p_s"] = np.ascontiguousarray(ps[:, 4 * c:4 * c + 4].reshape(DEPTH, 16, DPLE))
        m["ptab"] = np.ascontiguousarray(pt[4 * c:4 * c + 4].reshape(1, 4 * NPG))
        in_maps.append(m)
    res = run_bass_kernel_spmd(nc, in_maps, core_ids=list(range(ncores)))
    rs = res.results
    cat = lambda k, ax: np.stack([np.asarray(r[k]) for r in rs], axis=ax)
    y_p = cat("y_p", 0)
    y_s = cat("y_s", 0).reshape(4 * ncores, 4, D)
    nk_p = cat("nk_p", 1).reshape(DEPTH, ncores, S, 2, 64)
    nv_p = cat("nv_p", 1).reshape(DEPTH, ncores, S, 2, 64)
    nki_p = cat("nki_p", 1)
    ncv_p = cat("ncv_p", 1)
    nk_s = cat("nk_s", 1).reshape(DEPTH, 4 * ncores, 4, 2, 64)
    nv_s = cat("nv_s", 1).reshape(DEPTH, 4 * ncores, 4, 2, 64)
    nki_s = cat("nki_s", 1).reshape(DEPTH, 4 * ncores, 4, 32)
    ncv_s = cat("ncv_s", 1).reshape(DEPTH, 4 * ncores, 4, 512)
    outs = (y_p, y_s, nk_p, nv_p, nki_p, ncv_p, nk_s, nv_s, nki_s, ncv_s)
    return tuple(np.ascontiguousarray(o, dtype=np.float32) for o in outs)


def kernel(**inputs):
    cfg = {"S": 2048, "DEPTH": 4, "PAST": 8192, "NPHYS": 2560, "NIT": 18}
    return run(cfg, 8, inputs)
```

```python
import math
from contextlib import ExitStack
import numpy as np
import ml_dtypes
import concourse.bass as bass
import concourse.mybir as mybir
from concourse.bass_utils import run_bass_kernel_spmd

F32 = mybir.dt.float32
BF16 = mybir.dt.bfloat16
I32 = mybir.dt.int32
ALU = mybir.AluOpType
AF = mybir.ActivationFunctionType
AX = mybir.AxisListType

D = 1024
DIN = 2088
DFF = 4096
DPLE = 256
ALPHA = (2.0 * 4) ** 0.25
EPS = 1e-5
NEG = -1.0e30
BIGV = 29952.0
ND = 8


class Res:
    __slots__ = ("n", "w", "rs")

    def __init__(self, n=""):
        self.n = n
        self.w = None
        self.rs = []


class Prog:
    ENG = ("pe", "act", "dve", "pool", "sp")

    def __init__(self):
        self.ops = {e: [] for e in self.ENG}
        self.cnt = {e: 0 for e in self.ENG}
        self.seen = {e: {} for e in self.ENG}
        self.dcnt = {}
        self.drr = {"sp": 0, "pool": 0}

    def _waits(self, eng, reads, writes):
        waits = []
        seen = self.seen[eng]

        def need(ev, raw):
            if ev is None:
                return
            k, v = ev
            if k == eng:
                if eng == "pe" or not raw:
                    return
            if seen.get(k, 0) >= v:
                return
            seen[k] = v
            waits.append((k, v))

        for r in reads:
            need(r.w, True)
        for w in writes:
            need(w.w, False)
            for ev in w.rs:
                need(ev, False)
        return waits

    def _commit(self, ev, reads, writes):
        for r in reads:
            r.rs.append(ev)
        for w in writes:
            w.w = ev
            w.rs = []

    def op(self, eng, fn, reads=(), writes=()):
        waits = self._waits(eng, reads, writes)
        self.cnt[eng] += 1
        ev = (eng, self.cnt[eng])
        self.ops[eng].append((waits, fn, eng, 1, 1))
        self._commit(ev, reads, writes)

    def dma(self, q, fn, reads=(), writes=(), n=1):
        i = self.drr[q]
        self.drr[q] = (i + 1) % ND
        key = (q, i)
        waits = self._waits(q, reads, writes)
        c = self.dcnt.get(key, 0)
        if c > 0 and self.seen[q].get(key, 0) < c:
            self.seen[q][key] = c
            waits.append((key, c))
        c += 16 * n
        self.dcnt[key] = c
        self.ops[q].append((waits, fn, key, 16, n))
        self._commit((key, c), reads, writes)

    def fence(self, dst, src):
        evs = []
        for s in src:
            if s.w is not None:
                evs.append(s.w)
            evs.extend(s.rs)
        for d in dst:
            d.rs.extend(evs)

    def emit(self, nc, es):
        sems = {}
        for e in self.ENG:
            sems[e] = es.enter_context(nc.semaphore("s_" + e))
        for q in ("sp", "pool"):
            for i in range(ND):
                sems[(q, i)] = es.enter_context(nc.semaphore("d_%s%d" % (q, i)))
        fin = []
        for k, v in self.dcnt.items():
            fin.append((k, v))
        for e in self.ENG:
            if e != "sp" and self.cnt[e] > 0:
                fin.append((e, self.cnt[e]))
        ops = self.ops

        def replay(name, eng):
            for waits, fn, key, inc, n in ops[name]:
                for k, v in waits:
                    eng.wait_ge(sems[k], v)
                r = fn(eng)
                if isinstance(r, (list, tuple)):
                    assert len(r) == n
                    for ins in r:
                        ins.then_inc(sems[key], inc)
                else:
                    assert n == 1
                    r.then_inc(sems[key], inc)
            if name == "sp":
                for k, v in fin:
                    eng.wait_ge(sems[k], v)

        with nc.Block() as block:
            @block.tensor
            def _(e):
                replay("pe", e)

            @block.scalar
            def _(e):
                replay("act", e)

            @block.vector
            def _(e):
                replay("dve", e)

            @block.gpsimd
            def _(e):
                replay("pool", e)

            @block.sync
            def _(e):
                replay("sp", e)


def bc(ap, shape, axis):
    return ap.unsqueeze(axis).to_broadcast(list(shape))


def build(cfg):
    S = cfg["S"]
    DEPTH = cfg["DEPTH"]
    PAST = cfg["PAST"]
    NPHYS = cfg["NPHYS"]
    NIT = cfg.get("NIT", 24)
    NTP = S // 128
    NT = NTP + 1
    ST = NTP
    NPG = PAST // 128
    KP = min(256, S // 4)
    KS = min(256, (PAST + 4) // 4)
    NBS = NPG + 1
    LMAX = NTP * 128
    assert NPG == 64

    nc = bass.Bass("TRN2", target_bir_lowering=False)
    dt = lambda n, s, d, k: nc.dram_tensor(n, list(s), d, kind=k).ap()
    x_p = dt("x_p", [S, D], F32, "ExternalInput")
    x_s = dt("x_s", [16, D], F32, "ExternalInput")
    p_p = dt("p_p", [DEPTH, S, DPLE], F32, "ExternalInput")
    p_s = dt("p_s", [DEPTH, 16, DPLE], F32, "ExternalInput")
    c_k = dt("c_k", [DEPTH, NPHYS, 128, 128], F32, "ExternalInput")
    c_v = dt("c_v", [DEPTH, NPHYS, 128, 128], F32, "ExternalInput")
    c_ki = dt("c_ki", [DEPTH, NPHYS, 128, 32], F32, "ExternalInput")
    ptab = dt("ptab", [1, 4 * NPG], I32, "ExternalInput")
    w_in = dt("w_in", [DEPTH, D, DIN], F32, "ExternalInput")
    sgu_g = dt("sgu_g", [DEPTH, 512], F32, "ExternalInput")
    sgu_bb = dt("sgu_bb", [DEPTH, 512], F32, "ExternalInput")
    sgu_w = dt("sgu_w", [DEPTH, 4, 128, 128], F32, "ExternalInput")
    sgu_b = dt("sgu_b", [DEPTH, 4, 128], F32, "ExternalInput")
    w_o = dt("w_o", [DEPTH, D, D], F32, "ExternalInput")
    ln1_g = dt("ln1_g", [DEPTH, D], F32, "ExternalInput")
    ln1_b = dt("ln1_b", [DEPTH, D], F32, "ExternalInput")
    w_f1 = dt("w_f1", [DEPTH, D, DFF], F32, "ExternalInput")
    w_f2 = dt("w_f2", [DEPTH, DFF, D], F32, "ExternalInput")
    w_pg = dt("w_pg", [DEPTH, D, D], F32, "ExternalInput")
    w_pp = dt("w_pp", [DEPTH, DPLE, D], F32, "ExternalInput")
    ln2_g = dt("ln2_g", [DEPTH, D], F32, "ExternalInput")
    ln2_b = dt("ln2_b", [DEPTH, D], F32, "ExternalInput")
    c_rq = dt("c_rq", [128, NT, 128], F32, "ExternalInput")
    c_ri = dt("c_ri", [128, NT, 64], F32, "ExternalInput")
    c_msk = dt("c_msk", [128, 8, 128], F32, "ExternalInput")
    c_misc = dt("c_misc", [128, 64], F32, "ExternalInput")
    c_sel = dt("c_sel", [128, 768], F32, "ExternalInput")
    y_p = dt("y_p", [S, D], F32, "ExternalOutput")
    y_s = dt("y_s", [16, D], F32, "ExternalOutput")
    nk_p = dt("nk_p", [DEPTH, S, 128], F32, "ExternalOutput")
    nv_p = dt("nv_p", [DEPTH, S, 128], F32, "ExternalOutput")
    nki_p = dt("nki_p", [DEPTH, S, 32], F32, "ExternalOutput")
    ncv_p = dt("ncv_p", [DEPTH, 128, 512], F32, "ExternalOutput")
    nk_s = dt("nk_s", [DEPTH, 16, 128], F32, "ExternalOutput")
    nv_s = dt("nv_s", [DEPTH, 16, 128], F32, "ExternalOutput")
    nki_s = dt("nki_s", [DEPTH, 16, 32], F32, "ExternalOutput")
    ncv_s = dt("ncv_s", [DEPTH, 16, 512], F32, "ExternalOutput")
    Hs = dt("Hs", [NT, 128, D], F32, "Internal")
    HT = dt("HT", [NT, 128, D], BF16, "Internal")

    P = Prog()
    es = ExitStack()
    with es:
        off = [0]
        TOTW = 52700
        big = es.enter_context(nc.sbuf_tensor("big", [128, TOTW], F32))

        def alloc(words):
            a = off[0]
            off[0] += int(words)
            assert off[0] <= TOTW, ("sbuf overflow", off[0])
            return a

        def f32v(a, n):
            return big[:, a:a + n]

        def bfv(a, n):
            return big[:, a:a + (n + 1) // 2].bitcast(BF16)

        def A32(n):
            return f32v(alloc(n), n)

        def A16(n):
            return bfv(alloc((n + 1) // 2), n)

        a_arena = alloc(12448)
        w_in_sb = bfv(a_arena, 8 * DIN).rearrange("p (k n) -> p k n", k=8)
        w_o_sb = bfv(a_arena + 4 * DIN, 8 * D).rearrange("p (k n) -> p k n", k=8)
        slabW = 4096
        slab = [(bfv(a_arena + i * slabW, 8 * 512).rearrange("p (k n) -> p k n", k=8),
                 bfv(a_arena + i * slabW + 2048, 4 * 1024).rearrange("p (k n) -> p k n", k=4)) for i in range(2)]
        R_win, R_wo = Res("win"), Res("wo")
        R_slab = [Res("slab0"), Res("slab1")]
        R_ple = Res("ple")
        kT_all = A16(NT * 128)
        kiT_all = A16(NT * 128)
        Vaug_all = A16(NT * 130).rearrange("p (t k e) -> p t k e", t=NT, k=2)
        R_ks = [Res("ks%d" % i) for i in range(NT)]
        lng = A32(D)
        lnb = A32(D)
        sg_g = A32(512)
        sg_b = A32(512)
        WsT = A16(512).rearrange("p (g t) -> p g t", g=4)
        WsTs = A16(512).rearrange("p (g t) -> p g t", g=4)
        bs_p = A32(4)
        bs_s = A32(4)
        R_ln, R_sg, R_ws = Res("ln"), Res("sg"), Res("ws")
        rq = A32(NT * 128).rearrange("p (t c) -> p t c", t=NT)
        ri = A32(NT * 64).rearrange("p (t c) -> p t c", t=NT)
        msk = A32(1024).rearrange("p (m c) -> p m c", m=8)
        misc = A32(64)
        ident = A16(128)
        bigI = A16(128)
        idx2 = es.enter_context(nc.sbuf_tensor("idx2", [128, 104], I32))
        R_c = Res("consts")
        QIall = A16(128)
        QIB = A16(512).rearrange("p (b n) -> p b n", b=4)
        QBD = A16(256).rearrange("p (b n) -> p b n", b=4)
        WPAD = A32(240)
        BIGSEL = A16(256).rearrange("p (b n) -> p b n", b=4)
        SELR = A16(512)
        On = A16(256).rearrange("p (b n) -> p b n", b=4)
        pow2 = misc[:, 0:NIT]
        rowmask = misc[:, 32:36]
        hT_sb = A16(1024).rearrange("p (k t) -> p k t", k=8)
        au = A32(512)
        av = A32(512)
        vn32 = A32(512)
        vnb = A16(512)
        zq = A32(512)
        zr = A32(552)
        rtA = A32(512)
        rtB = A32(512)
        qr = A16(512)
        qir = A16(256)
        kb = A16(128)
        kirep = A16(128)
        k32 = A32(128)
        ki32 = A32(32)
        qTs = [A16(512).rearrange("p (g t) -> p g t", g=4) for _ in range(2)]
        qiT = A16(384).rearrange("p (g t) -> p g t", g=3)
        wi = A32(8)
        mixs = [A16(1024) for _ in range(2)]
        MBf = [A16(2048) for _ in range(2)]
        R_MBf = [Res("mbf0"), Res("mbf1")]
        st8 = A32(32)
        R = {n: Res(n) for n in ("hT", "au", "av", "vn32", "vnb", "zq", "zr", "rtA", "rtB", "qr", "qir",
                                 "kb", "kirep", "k32", "ki32", "qT0", "qT1", "qiT", "wi", "mix0", "mix1", "st8", "junk2",
                                 "mixT", "hs", "pre", "xh", "hb", "h1T", "acc", "junk", "thr", "bis",
                                 "bo", "stage0", "stage1", "pe32", "peb", "peT", "sgm")}
        a_c = alloc(0)
        hs = A32(D)
        pre = A32(D)
        xh = A32(D)
        hb = A16(D)
        h1T = A16(D).rearrange("p (k t) -> p k t", k=8)
        mixT = A16(1024).rearrange("p (k t) -> p k t", k=8)
        assert max(LMAX, 1152) <= 2 * (off[0] - a_c)
        a_acc = alloc(0)
        AW = max(LMAX, 2048)
        acc = A32(AW)
        junk = bfv(a_c, AW)
        R_junk = [R[n_] for n_ in ("hs", "pre", "xh", "hb", "h1T", "mixT")]
        rl = [A16(512), A16(512), A16(512)]
        Wd = A16(1024).rearrange("p (h c) -> p h c", h=8)
        R_Wd = Res("Wd")
        R_rl = [Res("rl0"), Res("rl1"), Res("rl2")]
        R_MB = [Res("mb0"), Res("mb1")]
        pTt = [A16(512), A16(512), A16(512)]
        R_pT = [Res("pt0"), Res("pt1"), Res("pt2")]
        thr = A32(4)
        bisT = A32(NIT + 2)
        bisC = A32(NIT + 2)
        bisS = A32(NIT + 2)
        bisM = A32(8)
        KIg = A16(2 * 32 * 32).rearrange("p (q r d) -> p q r d", q=2, r=32)
        kiTc_s = A16(2 * 8 * 128).rearrange("p (q r n) -> p q r n", q=2, r=8)
        Kg = A16(2 * 8 * 128).rearrange("p (q r d) -> p q r d", q=2, r=8)
        Vg = A16(4 * 8 * 128).rearrange("p (b r d) -> p b r d", b=4, r=8)
        kTc_s = A16(2 * 8 * 128).rearrange("p (q r n) -> p q r n", q=2, r=8)
        Vaugc_s = A16(32 * 130).rearrange("p (b r k e) -> p b r k e", b=4, r=8, k=2)
        rls = A32(512)
        Atl = A32(128)
        MBc = A16(512)
        pTs = A16(1024).rearrange("p (h n) -> p h n", h=2)
        bm = A32(8)
        g2 = A32(16)
        g2rep = A32(128)
        RS = {n_: Res(n_) for n_ in ("kis", "kiTc", "Kst", "Vst", "kTc", "Vc", "QIall", "QIB", "QBD", "rls", "WPAD", "Atl", "MBc",
                                     "pTs0", "pTs1", "On", "bm", "g2", "sel", "MBc2")}
        alloc(max(0, 12800 - (off[0] - a_acc)))
        a_end = alloc(0)
        dw = [a_acc]

        def dalloc(words):
            a = dw[0]
            dw[0] += int(words)
            assert dw[0] <= a_end, ("D work overflow", dw[0] - a_acc, a_end - a_acc, off[0])
            return a
        a_h1Tg = dalloc(2048)
        h1Tgs = [bfv(a_h1Tg + 1024 * i_, 2048).rearrange("p (k n) -> p k n", k=8) for i_ in range(2)]
        uTs = [bfv(dalloc(1024), 2048).rearrange("p (f c) -> p f c", f=4) for _ in range(2)]
        h1Tg = h1Tgs[0]
        rD = [f32v(dalloc(512), 512) for _ in range(2)]
        stage = [f32v(dalloc(D), D) for _ in range(2)]
        pe32 = f32v(dalloc(256), 256)
        peb = bfv(dalloc(128), 256)
        peT = bfv(dalloc(128), 256).rearrange("p (k t) -> p k t", k=2)
        sgm = rD[0]
        a_ple = dalloc(5120)
        wpg_sb = bfv(a_ple, 8 * D).rearrange("p (k n) -> p k n", k=8)
        wpp_sb = bfv(a_ple + 4 * D, 2 * D).rearrange("p (k n) -> p k n", k=2)
        R_h1Tgs, R_uTs = [Res("h1Tg0"), Res("h1Tg1")], [Res("uT0"), Res("uT1")]
        R_h1Tg = R_h1Tgs[0]
        R_rD = [Res(), Res()]
        R["sgm"] = R_rD[0]
        D_res = R_h1Tgs + R_uTs + [R_rD[0], R_rD[1], R["stage0"], R["stage1"], R["pe32"], R["peb"], R["peT"], R_ple]
        B_res = [R["acc"], R["thr"], R["bis"], R["bo"]] + R_rl + R_MB + R_pT + [RS[n_] for n_ in ("kis", "kiTc", "Kst", "Vst", "kTc", "Vc", "rls", "MBc", "MBc2", "pTs0", "pTs1")]

        Ob = [es.enter_context(nc.psum_tensor("Ob%d" % i, [128, 512], F32)) for i in range(2)]
        Tb = [es.enter_context(nc.psum_tensor("Tb%d" % i, [128, 1024], BF16)) for i in range(2)]
        Rb = [es.enter_context(nc.psum_tensor("Rb%d" % i, [128, 512], F32)) for i in range(4)]
        R_Ob = [Res("O0"), Res("O1")]
        R_Tb = [Res("T0"), Res("T1")]
        R_Rb = [Res("R%d" % i) for i in range(4)]
        rr = {"T": 0, "R": 0, "rl": 0, "pT": 0, "MB": 0}

        def nextT():
            i = rr["T"]
            rr["T"] = (i + 1) % 2
            return Tb[i], R_Tb[i]

        def nextR():
            i = rr["R"]
            rr["R"] = (i + 1) % 4
            return Rb[i], R_Rb[i]

        rr["RD"] = 0
        RDb = Rb + Ob
        R_RDb = R_Rb + R_Ob

        def nextRD():
            i = rr["RD"]
            rr["RD"] = (i + 1) % 6
            return RDb[i], R_RDb[i]

        def transposes(items, dst, dst_res, evac="act", extra_reads=(), nrow=128):
            tb, rtb = nextT()
            n = len(items)
            for i, (ap, res) in enumerate(items):
                P.op("pe", lambda e, ap=ap, i=i, tb=tb: e.transpose(tb[0:int(np.prod(ap.shape[1:])), i * 128:(i + 1) * 128], ap, ident),
                     reads=[res, R_c], writes=[rtb])
            if evac == "act":
                P.op("act", lambda e, tb=tb, n=n: e.activation(out=dst, in_=tb[0:nrow, 0:n * 128], func=AF.Copy),
                     reads=[rtb], writes=[dst_res])
            else:
                P.op("dve", lambda e, tb=tb, n=n: e.tensor_copy(out=dst, in_=tb[0:nrow, 0:n * 128]),
                     reads=[rtb], writes=[dst_res])

        def lnorm(src, rsrc, G, W, gam, bet, rpar, out32, rout, tmp, rtmp):
            s3 = src.rearrange("p (g w) -> p g w", g=G)
            P.op("dve", lambda e: e.tensor_reduce(out=st8[:, 0:G], in_=s3, axis=AX.X, op=ALU.add),
                 reads=[rsrc], writes=[R["st8"]])
            for g in range(G):
                P.op("act", lambda e, g=g: e.activation(out=tmp[:, g * W:(g + 1) * W], in_=src[:, g * W:(g + 1) * W],
                                                        func=AF.Square, accum_out=st8[:, 4 + g:5 + g]),
                     reads=[rsrc, R["st8"]], writes=[rtmp, R["st8"]])
            iw = 1.0 / W
            P.op("dve", lambda e: e.tensor_scalar(out=st8[:, 8:8 + G], in0=st8[:, 0:G], scalar1=iw, scalar2=None, op0=ALU.mult),
                 reads=[R["st8"]], writes=[R["st8"]])
            P.op("dve", lambda e: e.tensor_tensor(out=st8[:, 12:12 + G], in0=st8[:, 8:8 + G], in1=st8[:, 8:8 + G], op=ALU.mult),
                 reads=[R["st8"]], writes=[R["st8"]])
            P.op("dve", lambda e: e.scalar_tensor_tensor(out=st8[:, 16:16 + G], in0=st8[:, 4:4 + G], scalar=iw, in1=st8[:, 12:12 + G],
                                                         op0=ALU.mult, op1=ALU.subtract),
                 reads=[R["st8"]], writes=[R["st8"]])
            P.op("act", lambda e: e.activation(out=st8[:, 28:28 + G], in_=st8[:, 16:16 + G], func=AF.Sqrt, bias=misc[:, 42:43], scale=1.0),
                 reads=[R["st8"], R_c], writes=[R["st8"]])
            P.op("dve", lambda e: e.reciprocal(out=st8[:, 20:20 + G], in_=st8[:, 28:28 + G]),
                 reads=[R["st8"]], writes=[R["st8"]])
            P.op("dve", lambda e: e.scalar_tensor_tensor(out=st8[:, 24:24 + G], in0=st8[:, 8:8 + G], scalar=-1.0, in1=st8[:, 20:20 + G],
                                                         op0=ALU.mult, op1=ALU.mult),
                 reads=[R["st8"]], writes=[R["st8"]])
            for g in range(G):
                P.op("act", lambda e, g=g: e.activation(out=tmp[:, g * W:(g + 1) * W], in_=src[:, g * W:(g + 1) * W], func=AF.Identity,
                                                        scale=st8[:, 20 + g:21 + g], bias=st8[:, 24 + g:25 + g]),
                     reads=[rsrc, R["st8"]], writes=[rtmp])
            P.op("pool", lambda e: e.tensor_tensor(out=tmp, in0=tmp, in1=gam, op=ALU.mult), reads=[rtmp, rpar], writes=[rtmp])
            P.op("pool", lambda e: e.tensor_tensor(out=out32, in0=tmp, in1=bet, op=ALU.add), reads=[rtmp, rpar], writes=[rout])

        def rope(src, rsrc, H, Dh, tab, out, rout, qperm=False):
            hf = Dh // 2
            s3 = src.rearrange("p (h d) -> p h d", h=H)
            a3 = rtA[:, 0:H * Dh].rearrange("p (h d) -> p h d", h=H)
            b3 = rtB[:, 0:H * Dh].rearrange("p (h d) -> p h d", h=H)
            cos2 = bc(tab[:, 0:Dh], [128, H, Dh], 1)
            sn1 = bc(tab[:, Dh:Dh + hf], [128, H, hf], 1)
            sn2 = bc(tab[:, Dh + hf:2 * Dh], [128, H, hf], 1)
            P.op("pool", lambda e: e.tensor_tensor(out=a3, in0=s3, in1=cos2, op=ALU.mult), reads=[rsrc, R_c], writes=[R["rtA"]])
            P.op("pool", lambda e: e.tensor_tensor(out=b3[:, :, 0:hf], in0=s3[:, :, hf:Dh], in1=sn1, op=ALU.mult),
                 reads=[rsrc, R_c], writes=[R["rtB"]])
            P.op("pool", lambda e: e.tensor_tensor(out=b3[:, :, hf:Dh], in0=s3[:, :, 0:hf], in1=sn2, op=ALU.mult),
                 reads=[rsrc, R_c], writes=[R["rtB"]])
            if qperm:
                ov = out.rearrange("p (g k d) -> p k g d", g=4, k=2)
                i0 = rtA[:, 0:H * Dh].rearrange("p (k g d) -> p k g d", k=2, g=4)
                i1 = rtB[:, 0:H * Dh].rearrange("p (k g d) -> p k g d", k=2, g=4)
            else:
                ov, i0, i1 = out, rtA[:, 0:H * Dh], rtB[:, 0:H * Dh]
            P.op("pool", lambda e: e.tensor_tensor(out=ov, in0=i0, in1=i1, op=ALU.add),
                 reads=[R["rtA"], R["rtB"]], writes=[rout])

        P.dma("sp", lambda e: e.dma_start(out=rq, in_=c_rq[:, :, :]), writes=[R_c])
        P.dma("sp", lambda e: e.dma_start(out=ri, in_=c_ri[:, :, :]), writes=[R_c])
        P.dma("sp", lambda e: e.dma_start(out=msk, in_=c_msk[:, :, :]), writes=[R_c])
        P.dma("sp", lambda e: e.dma_start(out=misc, in_=c_misc[:, :]), writes=[R_c])
        P.dma("sp", lambda e: e.dma_start(out=zq.bitcast(I32)[:, 0:2], in_=ptab.rearrange("o (q p) -> p (o q)", q=2),
                                          allow_slow_non_contiguous=True), writes=[R["zq"]])
        P.op("dve", lambda e: e.tensor_copy(out=au[:, 0:2], in_=zq.bitcast(I32)[:, 0:2]), reads=[R["zq"]], writes=[R["au"]])
        P.op("pool", lambda e: e.iota(av[:, 0:16], [[1, 16]], base=0, channel_multiplier=0, allow_small_or_imprecise_dtypes=True), writes=[R["av"]])
        P.op("dve", lambda e: e.tensor_scalar(out=au[:, 2:4], in0=au[:, 0:2], scalar1=16.0, scalar2=None, op0=ALU.mult), reads=[R["au"]], writes=[R["au"]])
        P.op("dve", lambda e: e.tensor_scalar(out=au[:, 4:6], in0=au[:, 0:2], scalar1=4.0, scalar2=None, op0=ALU.mult), reads=[R["au"]], writes=[R["au"]])
        P.dma("sp", lambda e: e.dma_start(out=zq.bitcast(I32)[0:64, 2:6], in_=ptab.rearrange("o (b p) -> p (o b)", b=4),
                                          allow_slow_non_contiguous=True), writes=[R["zq"]])
        P.op("dve", lambda e: e.tensor_copy(out=au[0:64, 8:12], in_=zq.bitcast(I32)[0:64, 2:6]), reads=[R["zq"]], writes=[R["au"]])
        P.op("dve", lambda e: e.tensor_scalar(out=au[0:64, 12:16], in0=au[0:64, 8:12], scalar1=16.0, scalar2=None, op0=ALU.mult),
             reads=[R["au"]], writes=[R["au"]])
        for b in range(4):
            P.op("dve", lambda e, b=b: e.tensor_scalar(out=idx2[0:64, 40 + b * 16:56 + b * 16], in0=av[0:64, 0:16], scalar1=au[0:64, 12 + b:13 + b],
                                                       scalar2=None, op0=ALU.add), reads=[R["au"], R["av"]], writes=[R_c])
        for pr in range(2):
            P.op("dve", lambda e, pr=pr: e.tensor_scalar(out=idx2[:, pr * 16:(pr + 1) * 16], in0=av[:, 0:16], scalar1=au[:, 2 + pr:3 + pr], scalar2=None,
                                                         op0=ALU.add), reads=[R["au"], R["av"]], writes=[R_c])
            P.op("dve", lambda e, pr=pr: e.tensor_scalar(out=idx2[:, 32 + pr * 4:36 + pr * 4], in0=av[:, 0:4], scalar1=au[:, 4 + pr:5 + pr], scalar2=None,
                                                         op0=ALU.add), reads=[R["au"], R["av"]], writes=[R_c])
        P.op("pool", lambda e: e.iota(ident, [[1, 128]], base=0, channel_multiplier=-1, allow_small_or_imprecise_dtypes=True),
             writes=[R_c])
        P.op("dve", lambda e: e.tensor_scalar(out=bigI, in0=ident, scalar1=0.0, scalar2=BIGV, op0=ALU.is_equal, op1=ALU.mult),
             reads=[R_c], writes=[R_c])
        P.op("dve", lambda e: e.tensor_scalar(out=ident, in0=ident, scalar1=0.0, scalar2=None, op0=ALU.is_equal),
             reads=[R_c], writes=[R_c])
        P.op("pool", lambda e: e.memset(Vaug_all, 1.0), writes=R_ks)
        P.op("pool", lambda e: e.memset(hs, 0.0), writes=[R["hs"]])
        P.op("pool", lambda e: e.memset(WsTs, 0.0), writes=[R_ws])
        P.op("pool", lambda e: e.memset(bs_s, 0.0), writes=[R_ws])

        P.dma("sp", lambda e: e.dma_start(out=acc[:, 0:768], in_=c_sel[:, :]), writes=[R["acc"]])
        P.op("dve", lambda e: e.tensor_copy(out=BIGSEL.rearrange("p b n -> p (b n)"), in_=acc[:, 0:256]), reads=[R["acc"]], writes=[RS["sel"]])
        P.op("dve", lambda e: e.tensor_copy(out=SELR, in_=acc[:, 256:768]), reads=[R["acc"]], writes=[RS["sel"]])
        P.op("pool", lambda e: e.memset(QIB, 0.0), writes=[RS["QIB"]])
        P.op("pool", lambda e: e.memset(QBD, 0.0), writes=[RS["QBD"]])
        P.op("pool", lambda e: e.memset(WPAD, 0.0), writes=[RS["WPAD"]])
        P.op("pool", lambda e: e.memset(On, 0.0), writes=[RS["On"]])

        def rows(i):
            return 16 if i == ST else 128

        def to_hT_and_store(src32, rsrc, i):
            P.op("act", lambda e: e.activation(out=hb, in_=src32, func=AF.Copy), reads=[rsrc], writes=[R["hb"]])
            transposes([(hb[:, k * 128:(k + 1) * 128], R["hb"]) for k in range(8)],
                       h1T.rearrange("p k t -> p (k t)"), R["h1T"])
            P.dma("sp", lambda e, i=i: e.dma_start(out=HT[i], in_=h1T.rearrange("p k t -> p (k t)")), reads=[R["h1T"]], writes=[R_HT[i]])

        R_Hs = [Res("Hs%d" % i) for i in range(NT)]
        R_HT = [Res("HT%d" % i) for i in range(NT)]
        for i in range(NT):
            n = rows(i)
            src = x_s[:, :] if i == ST else x_p[i * 128:(i + 1) * 128, :]
            P.dma("sp", lambda e, src=src, n=n: e.dma_start(out=hs[0:n, :], in_=src), writes=[R["hs"]])
            P.dma("sp", lambda e, i=i: e.dma_start(out=Hs[i], in_=hs), reads=[R["hs"]], writes=[R_Hs[i]])
            to_hT_and_store(hs, R["hs"], i)

        def att_index(j):
            nblk, ktop, bias_m = j + 1, KP, 0
            L = nblk * 128
            nch = (nblk + 3) // 4
            P.op("dve", lambda e: e.tensor_tensor(out=Wd, in0=bc(ident, [128, 8, 128], 1), in1=bc(wi[:, 0:8], [128, 8, 128], 2), op=ALU.mult),
                 reads=[R["wi"], R_c], writes=[R_Wd])
            for c in range(nch):
                c0 = c * 512
                w = min(512, L - c0)
                kiT_c, r_ki = kiT_all[:, c0:c0 + 512], R_ks[c * 4:min(c * 4 + 4, j + 1)]
                oa, roa = Ob[c % 2], R_Ob[c % 2]
                def accum(h, ir, oa=oa, roa=roa, w=w):
                    P.op("pe", lambda e: e.matmul(oa[:, 0:w], lhsT=Wd[:, h, :], rhs=rl[ir][:, 0:w], start=(h == 0), stop=(h == 7)),
                         reads=[R_Wd, R_rl[ir]], writes=[roa])
                prev = None
                for h in range(8):
                    pb, rpb = nextR()
                    hq, hh = h % 3, h // 3
                    P.op("pe", lambda e, pb=pb, hq=hq, hh=hh, kiT_c=kiT_c, w=w: e.matmul(
                        pb[:, 0:w], lhsT=qiT[hq * 32:(hq + 1) * 32, hh, :], rhs=kiT_c[hq * 32:(hq + 1) * 32, 0:w], start=True, stop=True),
                        reads=[R["qiT"]] + r_ki, writes=[rpb])
                    ir = rr["rl"]
                    rr["rl"] = (ir + 1) % 3
                    P.op("act", lambda e, pb=pb, ir=ir, w=w: e.activation(out=rl[ir][:, 0:w], in_=pb[:, 0:w], func=AF.Relu),
                         reads=[rpb], writes=[R_rl[ir]])
                    if prev is not None:
                        accum(*prev)
                    prev = (h, ir)
                accum(*prev)
                P.op("act", lambda e, oa=oa, c0=c0, w=w: e.activation(out=acc[:, c0:c0 + w], in_=oa[:, 0:w], func=AF.Copy),
                     reads=[roa], writes=[R["acc"]])
            lastc = acc[:, L - 128:L]
            if L > ktop:
                P.op("dve", lambda e: e.tensor_reduce(out=bisM[:, 0:1], in_=acc[:, 0:L], axis=AX.X, op=ALU.max),
                     reads=[R["acc"]], writes=[R["bis"]])
                P.op("dve", lambda e: e.tensor_reduce(out=bisM[:, 1:2], in_=acc[:, 0:L], axis=AX.X, op=ALU.min),
                     reads=[R["acc"]], writes=[R["bis"]])
            P.op("dve", lambda e: e.tensor_tensor(out=lastc, in0=lastc, in1=msk[:, bias_m, :], op=ALU.add),
                 reads=[R["acc"], R_c], writes=[R["acc"]])
            if L > ktop:
                P.op("dve", lambda e: e.tensor_tensor(out=bisM[:, 2:3], in0=bisM[:, 0:1], in1=bisM[:, 1:2], op=ALU.subtract),
                     reads=[R["bis"]], writes=[R["bis"]])
                P.op("dve", lambda e: e.scalar_tensor_tensor(out=bisT[:, 0:1], in0=bisM[:, 2:3], scalar=0.5, in1=bisM[:, 1:2],
                                                             op0=ALU.mult, op1=ALU.add),
                     reads=[R["bis"]], writes=[R["bis"]])
                P.op("dve", lambda e: e.tensor_scalar(out=bisS[:, 0:NIT], in0=pow2, scalar1=bisM[:, 2:3], scalar2=None, op0=ALU.mult),
                     reads=[R["bis"], R_c], writes=[R["bis"]])
                P.op("dve", lambda e: e.memset(bisC[:, 0:NIT], 0.0), writes=[R["bis"]])
                for k in range(NIT):
                    P.op("dve", lambda e, k=k: e.tensor_scalar(out=MBf[j % 2][:, 0:L], in0=acc[:, 0:L], scalar1=bisT[:, k:k + 1], scalar2=0.0,
                                                               op0=ALU.is_ge, op1=ALU.add, accum_out=bisC[:, k:k + 1]),
                         reads=[R["acc"], R["bis"]], writes=[R_MBf[j % 2], R["bis"]])
                    P.op("dve", lambda e, k=k: e.tensor_scalar(out=bisM[:, 4:5], in0=bisC[:, k:k + 1], scalar1=float(ktop) - 0.5,
                                                               scalar2=0.5, op0=ALU.is_ge, op1=ALU.subtract),
                         reads=[R["bis"]], writes=[R["bis"]])
                    P.op("dve", lambda e, k=k: e.scalar_tensor_tensor(out=bisT[:, k + 1:k + 2], in0=bisM[:, 4:5], scalar=bisS[:, k:k + 1],
                                                                      in1=bisT[:, k:k + 1], op0=ALU.mult, op1=ALU.add),
                         reads=[R["bis"]], writes=[R["bis"]])
                P.op("dve", lambda e: e.scalar_tensor_tensor(out=thr[:, 0:1], in0=bisM[:, 2:3], scalar=-(2.0 ** -(NIT + 1)),
                                                             in1=bisT[:, NIT:NIT + 1], op0=ALU.mult, op1=ALU.add),
                     reads=[R["bis"]], writes=[R["thr"]])
            else:
                P.op("dve", lambda e: e.memset(thr[:, 0:1], -1.0e29), writes=[R["thr"]])
            P.op("dve", lambda e: e.tensor_scalar(out=MBf[j % 2][:, 0:L], in0=acc[:, 0:L], scalar1=thr[:, 0:1], scalar2=misc[:, 40:41],
                                                  op0=ALU.is_ge, op1=ALU.subtract),
                 reads=[R["acc"], R["thr"], R_c], writes=[R_MBf[j % 2]])

        def att_core(j):
            nblk = j + 1
            L = nblk * 128
            nch = (nblk + 3) // 4
            qT = qTs[j % 2]
            rqT = R["qT%d" % (j % 2)]
            MB = MBf[j % 2]
            rMB = R_MBf[j % 2]
            first = [True, True]

            def pv(h, ip, nb, V_c, r_kv):
                kv, g, ob = h // 4, h % 4, h // 4
                for b in range(nb):
                    st_ = first[ob]
                    first[ob] = False
                    P.op("pe", lambda e, b=b, st_=st_: e.matmul(
                        Ob[ob][:, g * 65:(g + 1) * 65], lhsT=pTt[ip][:, b * 128:(b + 1) * 128], rhs=V_c[:, b, kv, :],
                        start=st_, stop=False, skip_group_check=True),
                        reads=[R_pT[ip]] + r_kv, writes=[R_Ob[ob]])

            prev = None
            for c in range(nch):
                c0 = c * 512
                w = min(512, L - c0)
                nb = w // 128
                kT_c, V_c, r_kv = kT_all[:, c0:c0 + 512], Vaug_all[:, c * 4:c * 4 + 4], R_ks[c * 4:min(c * 4 + 4, j + 1)]
                for h in range(8):
                    kv, g = h // 4, h % 4
                    sb, rsb = nextR()
                    for b in range(nb):
                        P.op("pe", lambda e, sb=sb, b=b, kv=kv, g=g, kT_c=kT_c: e.matmul(
                            sb[:, b * 128:(b + 1) * 128], lhsT=kT_c[kv * 64:(kv + 1) * 64, b * 128:(b + 1) * 128],
                            rhs=qT[kv * 64:(kv + 1) * 64, g, :], start=True, stop=False, skip_group_check=True),
                            reads=[rqT] + r_kv, writes=[rsb])
                        P.op("pe", lambda e, sb=sb, b=b, c0=c0: e.matmul(
                            sb[:, b * 128:(b + 1) * 128], lhsT=MB[:, c0 + b * 128:c0 + (b + 1) * 128], rhs=bigI,
                            start=False, stop=True, skip_group_check=True),
                            reads=[rMB, R_c], writes=[rsb])
                    ip = rr["pT"]
                    rr["pT"] = (ip + 1) % 3
                    P.op("act", lambda e, sb=sb, ip=ip, w=w: e.activation(out=pTt[ip][:, 0:w], in_=sb[:, 0:w], func=AF.Exp, scale=0.125),
                         reads=[rsb], writes=[R_pT[ip]])
                    if prev is not None:
                        pv(*prev)
                    prev = (h, ip, nb, V_c, r_kv)
            pv(*prev)
            o_normalize(mixs[j % 2][:, 512:1024], R["mix%d" % (j % 2)])

        def o_normalize(dst, rdst):
            for ob in range(2):
                o3 = Ob[ob][:, 0:260].rearrange("p (g e) -> p g e", g=4)
                P.op("dve", lambda e, o3=o3: e.reciprocal(out=bisM[:, 4:8], in_=o3[:, :, 64]), reads=[R_Ob[ob]], writes=[R["bis"]])
                d3 = dst[:, ob * 256:(ob + 1) * 256].rearrange("p (g d) -> p g d", g=4)
                P.op("dve", lambda e, o3=o3, d3=d3: e.tensor_tensor(out=d3, in0=o3[:, :, 0:64], in1=bc(bisM[:, 4:8], [128, 4, 64], 2),
                                                                   op=ALU.mult),
                     reads=[R_Ob[ob], R["bis"]], writes=[rdst])

        def sample_attention(l):
            qT = qTs[ST % 2]
            mix = mixs[ST % 2]
            rqT = R["qT%d" % (ST % 2)]
            rmix = R["mix%d" % (ST % 2)]
            ckv = c_k.rearrange("l n (c r) d -> (l n c) (r d)", c=16)
            cvv = c_v.rearrange("l n (c r) d -> (l n c) (r d)", c=16)
            ckiv = c_ki.rearrange("l n (c r) d -> (l n c) (r d)", c=4)

            def gather(tab2, dst, col0, c, eoff):
                def f(e):
                    return [e.indirect_dma_start(out=dst[:, pr].rearrange("p r d -> p (r d)"), out_offset=None, in_=tab2,
                                                 in_offset=bass.IndirectOffsetOnAxis(ap=idx2[:, col0 + pr * (16 if col0 == 0 else 4) + c:
                                                                                             col0 + pr * (16 if col0 == 0 else 4) + c + 1], axis=0),
                                                 element_offset=eoff) for pr in range(2)]
                return f

            def gatherV(c, eoff):
                def f(e):
                    return [e.indirect_dma_start(out=Vg[0:64, b].rearrange("p r d -> p (r d)"), out_offset=None, in_=cvv,
                                                 in_offset=bass.IndirectOffsetOnAxis(ap=idx2[0:64, 40 + b * 16 + c:41 + b * 16 + c], axis=0),
                                                 element_offset=eoff) for b in range(4)]
                return f

            tb, rtb = nextT()
            for h in range(8):
                P.op("pe", lambda e, h=h, tb=tb: e.transpose(tb[0:32, h * 128:(h + 1) * 128], qir[:, h * 32:(h + 1) * 32], ident),
                     reads=[R["qir"], R_c], writes=[rtb])
            P.op("act", lambda e, tb=tb: e.activation(out=QIall[0:32, :].rearrange("p (r h) -> p r h", h=8),
                                                      in_=tb[0:32, :].rearrange("p (h r) -> p r h", h=8)[:, 0:16, :], func=AF.Copy),
                 reads=[rtb], writes=[RS["QIall"]])
            for b in range(4):
                P.op("act", lambda e, b=b: e.activation(out=QIB[0:32, b, b * 32:(b + 1) * 32], in_=QIall[0:32, b * 32:(b + 1) * 32], func=AF.Copy),
                     reads=[RS["QIall"]], writes=[RS["QIB"]])
            for kv in range(2):
                P.op("act", lambda e, kv=kv: e.activation(
                    out=QBD[kv * 64:(kv + 1) * 64, :, kv * 32:kv * 32 + 16].rearrange("p b (g t) -> p b g t", g=4),
                    in_=qT[kv * 64:(kv + 1) * 64, :, 0:16].rearrange("p g (b t) -> p b g t", b=4), func=AF.Copy),
                    reads=[rqT], writes=[RS["QBD"]])
            P.op("dve", lambda e: e.tensor_tensor(out=Atl[0:16, :].rearrange("p (r h) -> p r h", h=8), in0=bc(wi[0:16, :], [16, 16, 8], 1),
                                                  in1=msk[0:16, 4, :].rearrange("p (r h) -> p r h", h=8), op=ALU.mult),
                 reads=[R["wi"], R_c], writes=[RS["Atl"]])
            pw, rpw = nextR()
            P.op("pe", lambda e, pw=pw: e.matmul(pw[:, 0:16], lhsT=Atl[0:16, :], rhs=msk[0:16, 5, 0:16], start=True, stop=True),
                 reads=[RS["Atl"], R_c], writes=[rpw])
            P.op("act", lambda e, pw=pw: e.activation(out=WPAD[:, 112:128], in_=pw[:, 0:16], func=AF.Copy), reads=[rpw], writes=[RS["WPAD"]])

            for c in range(16):
                part, grp = c % 8, c // 8
                if c % 4 == 0:
                    P.dma("pool", gather(ckiv, KIg, 32, c // 4, l * NPHYS * 4096), reads=[R_c], writes=[RS["kis"]], n=2)
                for pr in range(2):
                    transposes([(KIg[:, pr, (c % 4) * 8 + rl_, :], RS["kis"]) for rl_ in range(8)],
                               kiTc_s[0:32, pr].rearrange("p r n -> p (r n)"), RS["kiTc"], nrow=32)
                pb, rpb = nextR()
                for rl_ in range(8):
                    for b in range(4):
                        P.op("pe", lambda e, pb=pb, b=b, rl_=rl_: e.matmul(pb[:, rl_ * 64:(rl_ + 1) * 64], lhsT=QIB[0:32, b, :],
                                                                          rhs=kiTc_s[0:32, b // 2, rl_, (b % 2) * 64:(b % 2) * 64 + 64],
                                                                          start=(b == 0), stop=(b == 3), skip_group_check=True),
                             reads=[RS["QIB"], RS["kiTc"]], writes=[rpb])
                P.op("act", lambda e, pb=pb: e.activation(out=rls, in_=pb[:, 0:512], func=AF.Relu), reads=[rpb], writes=[RS["rls"]])
                P.op("pe", lambda e, part=part, grp=grp: e.matmul(Ob[grp][:, 0:512], lhsT=WPAD[:, (7 - part) * 16:(7 - part) * 16 + 128], rhs=rls,
                                                                 start=(part == 0), stop=(part == 7)),
                     reads=[RS["WPAD"], RS["rls"]], writes=[R_Ob[grp]])
                if part == 7:
                    P.op("act", lambda e, grp=grp: e.activation(out=acc[:, grp * 512:(grp + 1) * 512], in_=Ob[grp][:, 0:512], func=AF.Copy),
                         reads=[R_Ob[grp]], writes=[R["acc"]])
            pb, rpb = nextR()
            P.op("pe", lambda e, pb=pb: e.matmul(pb[:, 0:128], lhsT=QIall[0:32, :], rhs=kiT_all[0:32, ST * 128:(ST + 1) * 128], start=True, stop=True),
                 reads=[RS["QIall"], R_ks[ST]], writes=[rpb])
            P.op("act", lambda e, pb=pb: e.activation(out=rls[:, 0:128], in_=pb[:, 0:128], func=AF.Relu), reads=[rpb], writes=[RS["rls"]])
            pn, rpn = nextR()
            P.op("pe", lambda e, pn=pn: e.matmul(pn[:, 0:128], lhsT=WPAD[:, 112:240], rhs=rls[:, 0:128], start=True, stop=True),
                 reads=[RS["WPAD"], RS["rls"]], writes=[rpn])
            P.op("dve", lambda e, pn=pn: e.tensor_tensor(out=acc[:, 1024:1152], in0=pn[:, 0:128], in1=msk[:, 7, :], op=ALU.add),
                 reads=[rpn, R_c], writes=[R["acc"]])

            LS = 1152
            P.op("dve", lambda e: e.tensor_reduce(out=bm[:, 0:1], in_=acc[:, 0:LS], axis=AX.X, op=ALU.max), reads=[R["acc"]], writes=[RS["bm"]])
            P.op("dve", lambda e: e.tensor_reduce(out=bm[:, 2:3], in_=acc[:, 0:1024], axis=AX.X, op=ALU.min), reads=[R["acc"]], writes=[RS["bm"]])
            P.op("dve", lambda e: e.tensor_scalar(out=bm[:, 1:2], in0=bm[:, 2:3], scalar1=-1.0, scalar2=None, op0=ALU.mult),
                 reads=[RS["bm"]], writes=[RS["bm"]])
            p1, rp1 = nextR()
            P.op("pe", lambda e, p1=p1: e.matmul(p1[0:2, 0:128], lhsT=bm[:, 0:2], rhs=msk[:, 5, :], start=True, stop=True),
                 reads=[RS["bm"], R_c], writes=[rp1])
            P.op("dve", lambda e, p1=p1: e.tensor_reduce(out=g2[0:2, 0:16], in_=p1[0:2, 0:128].rearrange("o (p r) -> o r p", p=8), axis=AX.X, op=ALU.max),
                 reads=[rp1], writes=[RS["g2"]])
            P.op("dve", lambda e: e.tensor_copy(out=g2rep[0:2, :].rearrange("o (p r) -> o p r", p=8), in_=bc(g2[0:2, 0:16], [2, 8, 16], 1)),
                 reads=[RS["g2"]], writes=[RS["g2"]])
            p2, rp2 = nextR()
            P.op("pe", lambda e, p2=p2: e.matmul(p2[:, 0:2], lhsT=g2rep[0:2, :], rhs=msk[0:2, 5, 0:2], start=True, stop=True),
                 reads=[RS["g2"], R_c], writes=[rp2])
            P.op("dve", lambda e, p2=p2: e.tensor_copy(out=bisM[:, 0:2], in_=p2[:, 0:2]), reads=[rp2], writes=[R["bis"]])
            P.op("dve", lambda e: e.tensor_tensor(out=bisM[:, 2:3], in0=bisM[:, 0:1], in1=bisM[:, 1:2], op=ALU.add),
                 reads=[R["bis"]], writes=[R["bis"]])
            P.op("dve", lambda e: e.tensor_scalar(out=bisM[:, 3:4], in0=bisM[:, 1:2], scalar1=-1.0, scalar2=None, op0=ALU.mult),
                 reads=[R["bis"]], writes=[R["bis"]])
            P.op("dve", lambda e: e.scalar_tensor_tensor(out=bisT[:, 0:1], in0=bisM[:, 2:3], scalar=0.5, in1=bisM[:, 3:4], op0=ALU.mult, op1=ALU.add),
                 reads=[R["bis"]], writes=[R["bis"]])
            P.op("dve", lambda e: e.tensor_scalar(out=bisS[:, 0:NIT], in0=pow2, scalar1=bisM[:, 2:3], scalar2=None, op0=ALU.mult),
                 reads=[R["bis"], R_c], writes=[R["bis"]])
            P.op("dve", lambda e: e.memset(bisC[:, 0:NIT], 0.0), writes=[R["bis"]])
            for k in range(NIT):
                P.op("dve", lambda e, k=k: e.tensor_scalar(out=junk[:, 0:LS], in0=acc[:, 0:LS], scalar1=bisT[:, k:k + 1], scalar2=0.0,
                                                           op0=ALU.is_ge, op1=ALU.add, accum_out=bisC[:, k:k + 1]),
                     reads=[R["acc"], R["bis"]], writes=R_junk + [R["bis"]])
                pc, rpc = nextR()
                P.op("pe", lambda e, k=k, pc=pc: e.matmul(pc[:, 0:1], lhsT=msk[:, 6, :], rhs=bisC[:, k:k + 1], start=True, stop=True),
                     reads=[R["bis"], R_c], writes=[rpc])
                P.op("dve", lambda e, pc=pc: e.tensor_scalar(out=bisM[:, 4:5], in0=pc[:, 0:1], scalar1=float(KS) - 0.5, scalar2=0.5,
                                                             op0=ALU.is_ge, op1=ALU.subtract),
                     reads=[rpc], writes=[R["bis"]])
                P.op("dve", lambda e, k=k: e.scalar_tensor_tensor(out=bisT[:, k + 1:k + 2], in0=bisM[:, 4:5], scalar=bisS[:, k:k + 1],
                                                                  in1=bisT[:, k:k + 1], op0=ALU.mult, op1=ALU.add),
                     reads=[R["bis"]], writes=[R["bis"]])
            P.op("dve", lambda e: e.scalar_tensor_tensor(out=thr[:, 0:1], in0=bisM[:, 2:3], scalar=-(2.0 ** -(NIT + 1)),
                                                         in1=bisT[:, NIT:NIT + 1], op0=ALU.mult, op1=ALU.add),
                 reads=[R["bis"]], writes=[R["thr"]])

            first = [True, True]

            def mask_chunk(c0, w, part):
                P.op("dve", lambda e: e.tensor_scalar(out=MBc[:, 0:w], in0=acc[:, c0:c0 + w], scalar1=thr[:, 0:1], scalar2=misc[:, 40:41],
                                                      op0=ALU.is_ge, op1=ALU.subtract),
                     reads=[R["acc"], R["thr"], R_c], writes=[RS["MBc"]])
                P.op("dve", lambda e: e.tensor_scalar(out=MBc[:, 0:w], in0=MBc[:, 0:w], scalar1=misc[:, 44 + part:45 + part], scalar2=None, op0=ALU.mult),
                     reads=[RS["MBc"], R_c], writes=[RS["MBc"]])

            def pv(ob, lhs, rhs, rd):
                st_ = first[ob // 2]
                first[ob // 2] = False
                P.op("pe", lambda e: e.matmul(Ob[ob // 2][0:64, (ob % 2) * 130:(ob % 2) * 130 + 130], lhsT=lhs, rhs=rhs,
                                              start=st_, stop=False, skip_group_check=True),
                     reads=rd, writes=[R_Ob[ob // 2]])

            for c in range(16):
                part, grp = c % 8, c // 8
                mask_chunk(grp * 512, 512, part)
                P.dma("pool", gather(ckv, Kg, 0, c, l * NPHYS * 16384), reads=[R_c], writes=[RS["Kst"]], n=2)
                P.dma("pool", gatherV(c, l * NPHYS * 16384), reads=[R_c], writes=[RS["Vst"]], n=4)
                for pr in range(2):
                    transposes([(Kg[:, pr, rl_, :], RS["Kst"]) for rl_ in range(8)],
                               kTc_s[:, pr].rearrange("p r n -> p (r n)"), RS["kTc"], evac="dve" if pr else "act")
                P.op("dve", lambda e: e.tensor_copy(out=Vaugc_s[0:64, :, :, :, 0:64].rearrange("p b r k d -> p (b r) k d"),
                                                     in_=Vg[0:64].rearrange("p b r (k d) -> p (b r) k d", k=2)),
                     reads=[RS["Vst"]], writes=[RS["Vc"]])
                for pr in range(2):
                    for r4 in range(2):
                        sb, rsb = nextR()
                        hb = (pr * 2 + r4) % 2
                        for rq in range(4):
                            rl_ = r4 * 4 + rq
                            for b2 in range(2):
                                b = 2 * pr + b2
                                o_ = (rq * 2 + b2) * 64
                                P.op("pe", lambda e, sb=sb, b=b, b2=b2, pr=pr, rl_=rl_, o_=o_: e.matmul(
                                    sb[0:64, o_:o_ + 64], lhsT=kTc_s[:, pr, rl_, b2 * 64:(b2 + 1) * 64], rhs=QBD[:, b, :],
                                    start=True, stop=False, skip_group_check=True), reads=[RS["kTc"], RS["QBD"]], writes=[rsb])
                                P.op("pe", lambda e, sb=sb, b=b, rl_=rl_, o_=o_: e.matmul(
                                    sb[0:64, o_:o_ + 64], lhsT=MBc[:, rl_ * 64:(rl_ + 1) * 64], rhs=BIGSEL[:, b, :],
                                    start=False, stop=True, skip_group_check=True), reads=[RS["MBc"], RS["sel"]], writes=[rsb])
                        P.op("act", lambda e, sb=sb, hb=hb: e.activation(out=pTs[0:64, hb, :], in_=sb[0:64, 0:512], func=AF.Exp, scale=0.125),
                             reads=[rsb], writes=[RS["pTs%d" % hb]])
                        for rq in range(4):
                            rl_ = r4 * 4 + rq
                            for b2 in range(2):
                                b = 2 * pr + b2
                                o_ = (rq * 2 + b2) * 64
                                pv(b, pTs[0:64, hb, o_:o_ + 64],
                                   Vaugc_s[0:64, b, rl_].rearrange("p k e -> p (k e)"), [RS["pTs%d" % hb], RS["Vc"]])
            mask_chunk(1024, 128, 0)
            sb, rsb = nextR()
            for b in range(4):
                P.op("pe", lambda e, sb=sb, b=b: e.matmul(sb[:, b * 64:(b + 1) * 64], lhsT=kT_all[:, ST * 128:(ST + 1) * 128], rhs=QBD[:, b, :],
                                                          start=True, stop=False, skip_group_check=True), reads=[R_ks[ST], RS["QBD"]], writes=[rsb])
                P.op("pe", lambda e, sb=sb, b=b: e.matmul(sb[:, b * 64:(b + 1) * 64], lhsT=MBc[:, 0:128], rhs=BIGSEL[:, b, :],
                                                          start=False, stop=True, skip_group_check=True), reads=[RS["MBc"], RS["sel"]], writes=[rsb])
            P.op("act", lambda e, sb=sb: e.activation(out=pTs[:, 0, 0:256], in_=sb[:, 0:256], func=AF.Exp, scale=0.125), reads=[rsb], writes=[RS["pTs0"]])
            for b in range(4):
                pv(b, pTs[:, 0, b * 64:(b + 1) * 64], Vaug_all[:, ST].rearrange("p k e -> p (k e)"), [RS["pTs0"], R_ks[ST]])
            for b in range(4):
                o3 = Ob[b // 2][0:64, (b % 2) * 130:(b % 2) * 130 + 130].rearrange("p (k e) -> p k e", k=2)
                for kv in range(2):
                    r0 = kv * 32
                    P.op("dve", lambda e, o3=o3, kv=kv, r0=r0: e.reciprocal(out=bm[r0:r0 + 16, 4:5], in_=o3[r0:r0 + 16, kv, 64:65]),
                         reads=[R_Ob[b // 2]], writes=[RS["bm"]])
                    P.op("dve", lambda e, o3=o3, kv=kv, r0=r0, b=b: e.tensor_scalar(out=On[r0:r0 + 16, b, :], in0=o3[r0:r0 + 16, kv, 0:64],
                                                                                   scalar1=bm[r0:r0 + 16, 4:5], scalar2=None, op0=ALU.mult),
                         reads=[R_Ob[b // 2], RS["bm"]], writes=[RS["On"]])
            bp, rbp = nextR()
            for h in range(8):
                for b in range(4):
                    P.op("pe", lambda e, bp=bp, h=h, b=b: e.matmul(bp[0:16, h * 64:(h + 1) * 64], lhsT=SELR[0:64, (b * 8 + h) * 16:(b * 8 + h) * 16 + 16],
                                                                  rhs=On[0:64, b, :], start=(b == 0), stop=(b == 3), skip_group_check=True),
                         reads=[RS["sel"], RS["On"]], writes=[rbp])
            P.op("act", lambda e, bp=bp: e.activation(out=mix[0:16, 512:1024], in_=bp[0:16, 0:512], func=AF.Copy), reads=[rbp], writes=[rmix])

        for l in range(DEPTH):
            last = l == DEPTH - 1
            def load_attn_weights(ll):
                P.fence([R_win, R_wo], [R_slab[0], R_slab[1]])
                P.dma("pool", lambda e: e.dma_start(out=w_in_sb, in_=w_in[ll].rearrange("(k p) n -> p k n", p=128)), writes=[R_win])
                P.dma("pool", lambda e: e.dma_start(out=w_o_sb, in_=w_o[ll].rearrange("(k p) n -> p k n", p=128)), writes=[R_wo])

            def load_slab(p_, l=l):
                s_ = p_ % 2
                W1s_, W2s_ = slab[s_]
                P.dma("pool", lambda e: e.dma_start(
                    out=W1s_, in_=w_f1[l, :, p_ * 512:(p_ + 1) * 512].rearrange("(k p) n -> p k n", p=128)), writes=[R_slab[s_]])
                P.dma("pool", lambda e: e.dma_start(
                    out=W2s_, in_=w_f2[l, p_ * 512:(p_ + 1) * 512, :].rearrange("(k p) n -> p k n", p=128)), writes=[R_slab[s_]])

            if l == 0:
                load_attn_weights(0)
            P.dma("sp", lambda e, l=l: e.dma_start(out=lng, in_=ln1_g[l:l + 1, :].to_broadcast([128, D])), writes=[R_ln])
            P.dma("sp", lambda e, l=l: e.dma_start(out=lnb, in_=ln1_b[l:l + 1, :].to_broadcast([128, D])), writes=[R_ln])
            P.dma("sp", lambda e, l=l: e.dma_start(out=sg_g, in_=sgu_g[l:l + 1, :].to_broadcast([128, 512])), writes=[R_sg])
            P.dma("sp", lambda e, l=l: e.dma_start(out=sg_b, in_=sgu_bb[l:l + 1, :].to_broadcast([128, 512])), writes=[R_sg])
            P.dma("sp", lambda e, l=l: e.dma_start(out=zq.rearrange("p (g s) -> p g s", g=4), in_=sgu_w[l].rearrange("g t s -> t g s")),
                  writes=[R["zq"]])
            P.op("dve", lambda e: e.tensor_tensor(out=zq.rearrange("p (g s) -> p g s", g=4), in0=zq.rearrange("p (g s) -> p g s", g=4),
                                                  in1=bc(msk[:, 2, :], [128, 4, 128], 1), op=ALU.mult),
                 reads=[R["zq"], R_c], writes=[R["zq"]])
            P.op("dve", lambda e: e.tensor_copy(out=qr, in_=zq), reads=[R["zq"]], writes=[R["qr"]])
            transposes([(qr[:, g * 128:(g + 1) * 128], R["qr"]) for g in range(4)], WsT.rearrange("p g t -> p (g t)"), R_ws)
            P.dma("sp", lambda e, l=l: [e.dma_start(out=zq[b * 4:(b + 1) * 4, :].rearrange("p (g s) -> p g s", g=4)[:, :, b2 * 4:(b2 + 1) * 4],
                                                    in_=sgu_w[l, :, 0:4, 0:4].rearrange("g t s -> t g s"))
                                        for b in range(4) for b2 in range(4)],
                  reads=[R["qr"]], writes=[R["zq"]], n=16)
            P.op("dve", lambda e: e.tensor_tensor(out=zq[0:16, :].rearrange("p (g s) -> p g s", g=4)[:, :, 0:16],
                                                  in0=zq[0:16, :].rearrange("p (g s) -> p g s", g=4)[:, :, 0:16],
                                                  in1=bc(msk[0:16, 3, 0:16], [16, 4, 16], 1), op=ALU.mult),
                 reads=[R["zq"], R_c], writes=[R["zq"]])
            P.op("dve", lambda e: e.memset(qr, 0.0), reads=[], writes=[R["qr"]])
            P.op("dve", lambda e: e.tensor_copy(out=qr[0:16, :].rearrange("p (g s) -> p g s", g=4)[:, :, 0:16],
                                                in_=zq[0:16, :].rearrange("p (g s) -> p g s", g=4)[:, :, 0:16]),
                 reads=[R["zq"]], writes=[R["qr"]])
            transposes([(qr[:, g * 128:(g + 1) * 128], R["qr"]) for g in range(4)], WsTs.rearrange("p g t -> p (g t)"), R_ws)
            P.dma("sp", lambda e, l=l: e.dma_start(out=bs_p, in_=sgu_b[l].rearrange("g t -> t g"), allow_slow_non_contiguous=True),
                  writes=[R_ws])
            P.dma("sp", lambda e, l=l: [e.dma_start(out=bs_s[b * 4:(b + 1) * 4, :], in_=sgu_b[l, :, 0:4].rearrange("g t -> t g"),
                                                    allow_slow_non_contiguous=True) for b in range(4)],
                  writes=[R_ws], n=4)
            P.fence(B_res, D_res)
            P.op("pool", lambda e: e.memset(Vaugc_s, 1.0), writes=[RS["Vc"]])
            P.fence(R_ks, [])

            def phaseA(i):
                n = rows(i)
                is_s = i == ST
                qT = qTs[i % 2]
                mix = mixs[i % 2]
                rqT = R["qT%d" % (i % 2)]
                rmix = R["mix%d" % (i % 2)]
                P.dma("sp", lambda e, i=i: e.dma_start(out=hT_sb.rearrange("p k t -> p (k t)"), in_=HT[i]), reads=[R_HT[i]], writes=[R["hT"]])
                chunks = [(0, 512), (512, 512), (1024, 512), (1536, 512), (2048, 40)]
                zb = []
                for (c0, w) in chunks:
                    pb, rpb = nextR()
                    for k in range(8):
                        P.op("pe", lambda e, pb=pb, k=k, c0=c0, w=w: e.matmul(pb[:, 0:w], lhsT=hT_sb[:, k, :], rhs=w_in_sb[:, k, c0:c0 + w],
                                                                            start=(k == 0), stop=(k == 7)),
                             reads=[R["hT"], R_win], writes=[rpb])
                    zb.append((pb, rpb))
                    if c0 == 0:
                        P.op("act", lambda e, pb=pb: e.activation(out=au, in_=pb[:, 0:512], func=AF.Gelu), reads=[rpb], writes=[R["au"]])
                    elif c0 == 512:
                        P.op("act", lambda e, pb=pb: e.activation(out=av, in_=pb[:, 0:512], func=AF.Gelu), reads=[rpb], writes=[R["av"]])
                    elif c0 == 1024:
                        P.op("act", lambda e, pb=pb: e.activation(out=zq, in_=pb[:, 0:512], func=AF.Copy), reads=[rpb], writes=[R["zq"]])
                    elif c0 == 1536:
                        P.op("act", lambda e, pb=pb: e.activation(out=zr[:, 0:512], in_=pb[:, 0:512], func=AF.Copy), reads=[rpb], writes=[R["zr"]])
                    else:
                        P.op("act", lambda e, pb=pb: e.activation(out=zr[:, 512:552], in_=pb[:, 0:40], func=AF.Copy), reads=[rpb], writes=[R["zr"]])
                lnorm(av, R["av"], 4, 128, sg_g, sg_b, R_sg, vn32, R["vn32"], rtA, R["rtA"])
                P.op("pool", lambda e: e.tensor_copy(out=vnb, in_=vn32), reads=[R["vn32"]], writes=[R["vnb"]])
                if i == NTP - 1:
                    P.dma("sp", lambda e, l=l: e.dma_start(out=ncv_p[l], in_=vn32), reads=[R["vn32"]])
                if is_s:
                    P.dma("sp", lambda e, l=l: e.dma_start(out=ncv_s[l], in_=vn32[0:16, :]), reads=[R["vn32"]])
                gb, rgb = nextR()
                wst = WsTs if is_s else WsT
                bsx = bs_s if is_s else bs_p
                for g in range(4):
                    P.op("pe", lambda e, g=g, gb=gb, wst=wst: e.matmul(gb[:, g * 128:(g + 1) * 128], lhsT=wst[:, g, :],
                                                                      rhs=vnb[:, g * 128:(g + 1) * 128], start=True, stop=True,
                                                                      skip_group_check=True),
                         reads=[R_ws, R["vnb"]], writes=[rgb])
                for g in range(4):
                    P.op("dve", lambda e, g=g, gb=gb, bsx=bsx: e.scalar_tensor_tensor(
                        out=mix[:, g * 128:(g + 1) * 128], in0=gb[:, g * 128:(g + 1) * 128], scalar=bsx[:, g:g + 1],
                        in1=au[:, g * 128:(g + 1) * 128], op0=ALU.add, op1=ALU.mult),
                        reads=[rgb, R_ws, R["au"]], writes=[rmix])
                rope(zq, R["zq"], 8, 64, rq[:, i, :], qr, R["qr"], qperm=True)
                rope(zr[:, 0:128], R["zr"], 2, 64, rq[:, i, :], k32, R["k32"])
                P.op("act", lambda e: e.activation(out=kb, in_=k32, func=AF.Copy), reads=[R["k32"]], writes=[R["kb"]])
                rope(zr[:, 256:512], R["zr"], 8, 32, ri[:, i, :], qir, R["qir"])
                rope(zr[:, 512:544], R["zr"], 1, 32, ri[:, i, :], ki32, R["ki32"])
                P.op("act", lambda e: e.activation(out=kirep.rearrange("p (r d) -> p r d", r=4), in_=bc(ki32, [128, 4, 32], 1), func=AF.Copy),
                     reads=[R["ki32"]], writes=[R["kirep"]])
                P.op("act", lambda e: e.activation(out=wi, in_=zr[:, 544:552], func=AF.Copy), reads=[R["zr"]], writes=[R["wi"]])
                P.op("act", lambda e, i=i: e.activation(out=Vaug_all[:, i, :, 0:64], in_=zr[:, 128:256].rearrange("p (k d) -> p k d", k=2),
                                                        func=AF.Copy), reads=[R["zr"]], writes=[R_ks[i]])
                if is_s:
                    P.dma("sp", lambda e, l=l: e.dma_start(out=nk_s[l], in_=k32[0:16, :]), reads=[R["k32"]])
                    P.dma("sp", lambda e, l=l: e.dma_start(out=nv_s[l], in_=zr[0:16, 128:256]), reads=[R["zr"]])
                    P.dma("sp", lambda e, l=l: e.dma_start(out=nki_s[l], in_=ki32[0:16, :]), reads=[R["ki32"]])
                else:
                    P.dma("sp", lambda e, l=l, i=i: e.dma_start(out=nk_p[l, i * 128:(i + 1) * 128, :], in_=k32), reads=[R["k32"]])
                    P.dma("sp", lambda e, l=l, i=i: e.dma_start(out=nv_p[l, i * 128:(i + 1) * 128, :], in_=zr[:, 128:256]), reads=[R["zr"]])
                    P.dma("sp", lambda e, l=l, i=i: e.dma_start(out=nki_p[l, i * 128:(i + 1) * 128, :], in_=ki32), reads=[R["ki32"]])
                tb, rtb = nextT()
                its = [(qr[:, g * 128:(g + 1) * 128], R["qr"]) for g in range(4)] + [(kb, R["kb"])] + \
                      [(qir[:, 0:96], R["qir"]), (qir[:, 96:192], R["qir"]), (qir[:, 192:256], R["qir"])]
                for j_, (ap, res) in enumerate(its):
                    P.op("pe", lambda e, ap=ap, j_=j_, tb=tb: e.transpose(tb[0:int(np.prod(ap.shape[1:])), j_ * 128:(j_ + 1) * 128], ap, ident),
                         reads=[res, R_c], writes=[rtb])
                P.op("act", lambda e, tb=tb: e.activation(out=qT.rearrange("p g t -> p (g t)"), in_=tb[:, 0:512], func=AF.Copy),
                     reads=[rtb], writes=[rqT])
                P.op("act", lambda e, tb=tb, i=i: e.activation(out=kT_all[:, i * 128:(i + 1) * 128], in_=tb[:, 512:640], func=AF.Copy),
                     reads=[rtb], writes=[R_ks[i]])
                P.op("act", lambda e, tb=tb: e.activation(out=qiT.rearrange("p g t -> p (g t)"), in_=tb[:, 640:1024], func=AF.Copy),
                     reads=[rtb], writes=[R["qiT"]])
                transposes([(kirep, R["kirep"])], kiT_all[:, i * 128:(i + 1) * 128], R_ks[i])


            def phaseC(i):
                n = rows(i)
                is_s = i == ST
                qT = qTs[i % 2]
                mix = mixs[i % 2]
                rqT = R["qT%d" % (i % 2)]
                rmix = R["mix%d" % (i % 2)]
                transposes([(mix[:, k * 128:(k + 1) * 128], rmix) for k in range(8)], mixT.rearrange("p k t -> p (k t)"), R["mixT"])
                P.dma("sp", lambda e, i=i: e.dma_start(out=hs, in_=Hs[i]), reads=[R_Hs[i]], writes=[R["hs"]])
                for hf in range(2):
                    pb, rpb = nextR()
                    for k in range(8):
                        P.op("pe", lambda e, pb=pb, k=k, hf=hf: e.matmul(pb[:, 0:512], lhsT=mixT[:, k, :], rhs=w_o_sb[:, k, hf * 512:(hf + 1) * 512],
                                                                        start=(k == 0), stop=(k == 7)),
                             reads=[R["mixT"], R_wo], writes=[rpb])
                    P.op("dve", lambda e, pb=pb, hf=hf: e.scalar_tensor_tensor(out=pre[:, hf * 512:(hf + 1) * 512], in0=hs[:, hf * 512:(hf + 1) * 512],
                                                                               scalar=ALPHA, in1=pb[:, 0:512], op0=ALU.mult, op1=ALU.add),
                         reads=[rpb, R["hs"]], writes=[R["pre"]])
                lnorm(pre, R["pre"], 1, D, lng, lnb, R_ln, hs, R["hs"], xh, R["xh"])
                P.op("act", lambda e: e.activation(out=xh, in_=hs, func=AF.Copy, scale=ALPHA), reads=[R["hs"]], writes=[R["xh"]])
                P.dma("sp", lambda e, i=i: e.dma_start(out=Hs[i], in_=xh), reads=[R["xh"]], writes=[R_Hs[i]])
                to_hT_and_store(hs, R["hs"], i)


            phaseA(0)
            att_index(0)
            for j in range(NTP):
                if j + 1 < NTP:
                    phaseA(j + 1)
                    att_index(j + 1)
                else:
                    phaseA(ST)
                    P.fence([R_slab[0], R_slab[1]], [R_win])
                    load_slab(0)
                    load_slab(1)
                att_core(j)
                phaseC(j)
            sample_attention(l)
            phaseC(ST)

            P.fence(D_res, B_res)
            P.op("pool", lambda e: e.memset(pe32, 0.0), writes=[R["pe32"]])
            P.dma("sp", lambda e, l=l: e.dma_start(out=lng, in_=ln2_g[l:l + 1, :].to_broadcast([128, D])), writes=[R_ln])
            P.dma("sp", lambda e, l=l: e.dma_start(out=lnb, in_=ln2_b[l:l + 1, :].to_broadcast([128, D])), writes=[R_ln])
            groups = [list(range(g0, min(g0 + 2, NTP))) for g0 in range(0, NTP, 2)] + [[ST]]
            sgi = [0]

            def accum(i, src, rsrc):
                P.dma("pool", lambda e, i=i: e.dma_start(out=Hs[i], in_=src, accum_op=ALU.add), reads=[rsrc], writes=[R_Hs[i]])

            gsel = [0]
            for p_ in range(8):
                s = p_ % 2
                W1s, W2s = slab[s]
                if 1 <= p_ <= 6:
                    load_slab(p_ + 1)
                if p_ == 0:
                    P.dma("pool", lambda e, l=l: e.dma_start(out=wpg_sb, in_=w_pg[l].rearrange("(k p) n -> p k n", p=128)), writes=[R_ple])
                    P.dma("pool", lambda e, l=l: e.dma_start(out=wpp_sb, in_=w_pp[l].rearrange("(k p) n -> p k n", p=128)), writes=[R_ple])
                for grp in groups:
                    ng = len(grp)
                    gi_ = gsel[0]
                    gsel[0] = 1 - gi_
                    h1Tg_, R_h1Tg_, uT, R_uT = h1Tgs[gi_], R_h1Tgs[gi_], uTs[gi_], R_uTs[gi_]
                    P.dma("sp", lambda e, grp=grp, ng=ng, h1Tg_=h1Tg_: [e.dma_start(
                        out=h1Tg_[:, :, ti * 128:(ti + 1) * 128], in_=HT[t].rearrange("p (k c) -> p k c", k=8)) for ti, t in enumerate(grp)],
                        n=ng,
                        reads=[R_HT[t] for t in grp], writes=[R_h1Tg_])
                    for fb in range(4):
                        pb, rpb = nextRD()
                        for k in range(8):
                            P.op("pe", lambda e, pb=pb, k=k, fb=fb, ng=ng, W1s=W1s, h1Tg_=h1Tg_: e.matmul(
                                pb[:, 0:ng * 128], lhsT=W1s[:, k, fb * 128:(fb + 1) * 128], rhs=h1Tg_[:, k, 0:ng * 128],
                                start=(k == 0), stop=(k == 7)), reads=[R_slab[s], R_h1Tg_], writes=[rpb])
                        ir = fb % 2
                        P.op("act", lambda e, pb=pb, ir=ir, ng=ng: e.activation(out=rD[ir][:, 0:ng * 128], in_=pb[:, 0:ng * 128], func=AF.Relu),
                             reads=[rpb], writes=[R_rD[ir]])
                        P.op("pool", lambda e, ir=ir, fb=fb, ng=ng, uT=uT: e.tensor_tensor(out=uT[:, fb, 0:ng * 128], in0=rD[ir][:, 0:ng * 128],
                                                                                   in1=rD[ir][:, 0:ng * 128], op=ALU.mult),
                             reads=[R_rD[ir]], writes=[R_uT])
                    for ti, t in enumerate(grp):
                        si = sgi[0]
                        sgi[0] = 1 - si
                        for hf in range(2):
                            pb, rpb = nextRD()
                            for fb in range(4):
                                P.op("pe", lambda e, pb=pb, fb=fb, ti=ti, hf=hf, W2s=W2s, uT=uT: e.matmul(
                                    pb[:, 0:512], lhsT=uT[:, fb, ti * 128:(ti + 1) * 128], rhs=W2s[:, fb, hf * 512:(hf + 1) * 512],
                                    start=(fb == 0), stop=(fb == 3)), reads=[R_slab[s], R_uT], writes=[rpb])
                            P.op("act", lambda e, pb=pb, si=si, hf=hf: e.activation(out=stage[si][:, hf * 512:(hf + 1) * 512], in_=pb[:, 0:512],
                                                                                    func=AF.Copy),
                                 reads=[rpb], writes=[R["stage%d" % si]])
                        accum(t, stage[si], R["stage%d" % si])
            if l + 1 < DEPTH:
                load_attn_weights(l + 1)
            for i in range(NT):
                n = rows(i)
                src = p_s[l] if i == ST else p_p[l, i * 128:(i + 1) * 128, :]
                P.dma("sp", lambda e, src=src, n=n: e.dma_start(out=pe32[0:n, :], in_=src), writes=[R["pe32"]])
                P.op("act", lambda e: e.activation(out=peb, in_=pe32, func=AF.Copy), reads=[R["pe32"]], writes=[R["peb"]])
                transposes([(peb[:, k * 128:(k + 1) * 128], R["peb"]) for k in range(2)], peT.rearrange("p k t -> p (k t)"), R["peT"])
                P.dma("sp", lambda e, i=i: e.dma_start(out=h1Tg[:, :, 0:128], in_=HT[i].rearrange("p (k c) -> p k c", k=8)), reads=[R_HT[i]], writes=[R_h1Tg])
                si = sgi[0]
                sgi[0] = 1 - si
                for hf in range(2):
                    pbg, rpbg = nextR()
                    for k in range(8):
                        P.op("pe", lambda e, pbg=pbg, k=k, hf=hf: e.matmul(pbg[:, 0:512], lhsT=h1Tg[:, k, 0:128], rhs=wpg_sb[:, k, hf * 512:(hf + 1) * 512],
                                                                          start=(k == 0), stop=(k == 7)), reads=[R_ple, R_h1Tg], writes=[rpbg])
                    pbp, rpbp = nextR()
                    for k in range(2):
                        P.op("pe", lambda e, pbp=pbp, k=k, hf=hf: e.matmul(pbp[:, 0:512], lhsT=peT[:, k, :], rhs=wpp_sb[:, k, hf * 512:(hf + 1) * 512],
                                                                          start=(k == 0), stop=(k == 1)), reads=[R_ple, R["peT"]], writes=[rpbp])
                    P.op("act", lambda e, pbg=pbg: e.activation(out=sgm, in_=pbg[:, 0:512], func=AF.Sigmoid), reads=[rpbg], writes=[R["sgm"]])
                    P.op("dve", lambda e, pbp=pbp, si=si, hf=hf: e.tensor_tensor(out=stage[si][:, hf * 512:(hf + 1) * 512], in0=sgm, in1=pbp[:, 0:512],
                                                                                 op=ALU.mult),
                         reads=[rpbp, R["sgm"]], writes=[R["stage%d" % si]])
                P.dma("sp", lambda e, i=i: e.dma_start(out=pre, in_=Hs[i]), reads=[R_Hs[i]], writes=[R["pre"]])
                P.op("dve", lambda e, si=si: e.tensor_tensor(out=pre, in0=pre, in1=stage[si], op=ALU.add),
                     reads=[R["pre"], R["stage%d" % si]], writes=[R["pre"]])
                lnorm(pre, R["pre"], 1, D, lng, lnb, R_ln, hs, R["hs"], xh, R["xh"])
                if last:
                    if i == ST:
                        P.dma("sp", lambda e: e.dma_start(out=y_s[:, :], in_=hs[0:16, :]), reads=[R["hs"]])
                    else:
                        P.dma("sp", lambda e, i=i: e.dma_start(out=y_p[i * 128:(i + 1) * 128, :], in_=hs), reads=[R["hs"]])
                else:
                    P.dma("sp", lambda e, i=i: e.dma_start(out=Hs[i], in_=hs), reads=[R["hs"]], writes=[R_Hs[i]])
                    to_hT_and_store(hs, R["hs"], i)

        P.emit(nc, es)
    return nc


def host_consts(S, PAST, NIT):
    NTP = S // 128
    NT = NTP + 1
    pos = np.zeros((NT, 128), np.float64)
    for i in range(NTP):
        pos[i] = i * 128 + np.arange(128)
    pos[NTP, :16] = PAST + (np.arange(16) % 4)

    def tab(dh):
        half = dh // 2
        inv = (10000.0 ** (-np.arange(half, dtype=np.float32) / half)).astype(np.float32)
        ang = pos.astype(np.float32)[:, :, None] * inv[None, None, :]
        c, s = np.cos(ang), np.sin(ang)
        t = np.concatenate([c, c, -s, s], axis=-1)
        return np.ascontiguousarray(t.transpose(1, 0, 2)).astype(np.float32)

    rq = tab(64)
    ri = tab(32)
    msk = np.zeros((128, 8, 128), np.float32)
    t = np.arange(128)[:, None]
    s = np.arange(128)[None, :]
    msk[:, 0, :] = np.where(s <= t, 0.0, NEG)
    nb = np.full((128, 128), NEG, np.float32)
    for r in range(16):
        for r2 in range(16):
            if r // 4 == r2 // 4 and r2 % 4 <= r % 4:
                nb[r, r2] = 0.0
    msk[:, 1, :] = nb
    msk[:, 2, :] = (s <= t).astype(np.float32)
    bd = np.zeros((128, 128), np.float32)
    for r in range(16):
        for r2 in range(16):
            if r // 4 == r2 // 4 and r2 % 4 <= r % 4:
                bd[r, r2] = 1.0
    msk[:, 3, :] = bd
    for r in range(16):
        msk[r, 4, r * 8:(r + 1) * 8] = 1.0
    msk[:, 5, :] = np.eye(128, dtype=np.float32)
    pidx = np.arange(128)
    msk[:, 6, :] = ((pidx[:, None] % 16) == (pidx[None, :] % 16)).astype(np.float32)
    msk[:, 7, :] = NEG
    msk[0:16, 7, :] = nb[0:16, :]
    sel = np.zeros((128, 768), np.float32)
    bigsel = np.zeros((128, 4, 64), np.float32)
    selr = np.zeros((64, 4, 8, 16), np.float32)
    for p in range(128):
        r = p % 16
        b, t = r // 4, r % 4
        for kv in range(2):
            for g in range(4):
                bigsel[p, b, kv * 32 + g * 4 + t] = BIGV
    for b in range(4):
        for h in range(8):
            kv, g = h // 4, h % 4
            for t in range(4):
                selr[kv * 32 + g * 4 + t, b, h, b * 4 + t] = 1.0
    sel[:, 0:256] = bigsel.reshape(128, 256)
    sel[0:64, 256:768] = selr.reshape(64, 512)
    misc = np.zeros((128, 64), np.float32)
    misc[:, :NIT] = (2.0 ** -(np.arange(NIT) + 1.0))[None, :]
    for b in range(4):
        misc[b * 4:(b + 1) * 4, 32 + b] = 1.0
    misc[:, 40] = 1.0
    misc[:, 41] = 128.0
    misc[:, 42] = EPS
    for part in range(8):
        misc[part * 16:(part + 1) * 16, 44 + part] = 1.0
    return rq, ri, msk, misc, sel


_CACHE = {}


def run(cfg, ncores, inputs):
    key = tuple(sorted(cfg.items()))
    if key not in _CACHE:
        _CACHE[key] = build(cfg)
    nc = _CACHE[key]
    S, DEPTH, PAST, NPHYS = cfg["S"], cfg["DEPTH"], cfg["PAST"], cfg["NPHYS"]
    NIT = cfg.get("NIT", 24)
    NPG = PAST // 128
    rq, ri, msk, misc, sel = host_consts(S, PAST, NIT)
    f = lambda a: np.ascontiguousarray(np.asarray(a, dtype=np.float32))
    ck = f(inputs["cache_k"]).reshape(DEPTH, NPHYS, 128, 128)
    cv = f(inputs["cache_v"]).reshape(DEPTH, NPHYS, 128, 128)
    cki = f(inputs["cache_kidx"])
    shared = {
        "c_k": ck, "c_v": cv, "c_ki": cki,
        "w_in": f(inputs["w_in"]), "sgu_g": f(inputs["sgu_ln_g"]).reshape(DEPTH, 512), "sgu_bb": f(inputs["sgu_ln_b"]).reshape(DEPTH, 512),
        "sgu_w": f(inputs["sgu_w"]), "sgu_b": f(inputs["sgu_b"]), "w_o": f(inputs["w_o"]),
        "ln1_g": f(inputs["ln1_g"]), "ln1_b": f(inputs["ln1_b"]), "w_f1": f(inputs["w_ff1"]), "w_f2": f(inputs["w_ff2"]),
        "w_pg": f(inputs["w_ple_gate"]), "w_pp": f(inputs["w_ple_proj"]), "ln2_g": f(inputs["ln2_g"]), "ln2_b": f(inputs["ln2_b"]),
        "c_rq": rq, "c_ri": ri, "c_msk": msk, "c_misc": misc, "c_sel": sel,
    }
    xp, xs = f(inputs["x_prompt"]), f(inputs["x_sample"])
    pp, ps = f(inputs["p_prompt"]), f(inputs["p_sample"])
    pt = np.asarray(inputs["page_table"]).astype(np.int32)
    in_maps = []
    for c in range(ncores):
        m = dict(shared)
        m["x_p"] = np.ascontiguousarray(xp[c])
        m["x_s"] = np.ascontiguousarray(xs[4 * c:4 * c + 4].reshape(16, D))
        m["p_p"] = np.ascontiguousarray(pp[:, c])
        m["p_s"] = np.ascontiguousarray(ps[:, 4 * c:4 * c + 4].reshape(DEPTH, 16, DPLE))
        m["ptab"] = np.ascontiguousarray(pt[4 * c:4 * c + 4].reshape(1, 4 * NPG))
        in_maps.append(m)
    res = run_bass_kernel_spmd(nc, in_maps, core_ids=list(range(ncores)))
    rs = res.results
    cat = lambda k, ax: np.stack([np.asarray(r[k]) for r in rs], axis=ax)
    y_p = cat("y_p", 0)
    y_s = cat("y_s", 0).reshape(4 * ncores, 4, D)
    nk_p = cat("nk_p", 1).reshape(DEPTH, ncores, S, 2, 64)
    nv_p = cat("nv_p", 1).reshape(DEPTH, ncores, S, 2, 64)
    nki_p = cat("nki_p", 1)
    ncv_p = cat("ncv_p", 1)
    nk_s = cat("nk_s", 1).reshape(DEPTH, 4 * ncores, 4, 2, 64)
    nv_s = cat("nv_s", 1).reshape(DEPTH, 4 * ncores, 4, 2, 64)
    nki_s = cat("nki_s", 1).reshape(DEPTH, 4 * ncores, 4, 32)
    ncv_s = cat("ncv_s", 1).reshape(DEPTH, 4 * ncores, 4, 512)
    outs = (y_p, y_s, nk_p, nv_p, nki_p, ncv_p, nk_s, nv_s, nki_s, ncv_s)
    return tuple(np.ascontiguousarray(o, dtype=np.float32) for o in outs)


def kernel(**inputs):
    cfg = {"S": 2048, "DEPTH": 4, "PAST": 8192, "NPHYS": 2560, "NIT": 18}
    return run(cfg, 8, inputs)
```

```python
import math
from contextlib import ExitStack
import numpy as np
import ml_dtypes
import concourse.bass as bass
import concourse.mybir as mybir
from concourse.bass_utils import run_bass_kernel_spmd

F32 = mybir.dt.float32
BF16 = mybir.dt.bfloat16
I32 = mybir.dt.int32
ALU = mybir.AluOpType
AF = mybir.ActivationFunctionType
AX = mybir.AxisListType

D = 1024
DIN = 2088
DFF = 4096
DPLE = 256
ALPHA = (2.0 * 4) ** 0.25
EPS = 1e-5
NEG = -1.0e30
BIGV = 29952.0
ND = 8


class Res:
    __slots__ = ("n", "w", "rs")

    def __init__(self, n=""):
        self.n = n
        self.w = None
        self.rs = []


class Prog:
    ENG = ("pe", "act", "dve", "pool", "sp")

    def __init__(self):
        self.ops = {e: [] for e in self.ENG}
        self.cnt = {e: 0 for e in self.ENG}
        self.seen = {e: {} for e in self.ENG}
        self.dcnt = {}
        self.drr = {"sp": 0, "pool": 0}

    def _waits(self, eng, reads, writes):
        waits = []
        seen = self.seen[eng]

        def need(ev, raw):
            if ev is None:
                return
            k, v = ev
            if k == eng:
                if eng == "pe" or not raw:
                    return
            if seen.get(k, 0) >= v:
                return
            seen[k] = v
            waits.append((k, v))

        for r in reads:
            need(r.w, True)
        for w in writes:
            need(w.w, False)
            for ev in w.rs:
                need(ev, False)
        return waits

    def _commit(self, ev, reads, writes):
        for r in reads:
            r.rs.append(ev)
        for w in writes:
            w.w = ev
            w.rs = []

    def op(self, eng, fn, reads=(), writes=()):
        waits = self._waits(eng, reads, writes)
        self.cnt[eng] += 1
        ev = (eng, self.cnt[eng])
        self.ops[eng].append((waits, fn, eng, 1, 1))
        self._commit(ev, reads, writes)

    def dma(self, q, fn, reads=(), writes=(), n=1):
        i = self.drr[q]
        self.drr[q] = (i + 1) % ND
        key = (q, i)
        waits = self._waits(q, reads, writes)
        c = self.dcnt.get(key, 0)
        if c > 0 and self.seen[q].get(key, 0) < c:
            self.seen[q][key] = c
            waits.append((key, c))
        c += 16 * n
        self.dcnt[key] = c
        self.ops[q].append((waits, fn, key, 16, n))
        self._commit((key, c), reads, writes)

    def fence(self, dst, src):
        evs = []
        for s in src:
            if s.w is not None:
                evs.append(s.w)
            evs.extend(s.rs)
        for d in dst:
            d.rs.extend(evs)

    def emit(self, nc, es):
        sems = {}
        for e in self.ENG:
            sems[e] = es.enter_context(nc.semaphore("s_" + e))
        for q in ("sp", "pool"):
            for i in range(ND):
                sems[(q, i)] = es.enter_context(nc.semaphore("d_%s%d" % (q, i)))
        fin = []
        for k, v in self.dcnt.items():
            fin.append((k, v))
        for e in self.ENG:
            if e != "sp" and self.cnt[e] > 0:
                fin.append((e, self.cnt[e]))
        ops = self.ops

        def replay(name, eng):
            for waits, fn, key, inc, n in ops[name]:
                for k, v in waits:
                    eng.wait_ge(sems[k], v)
                r = fn(eng)
                if isinstance(r, (list, tuple)):
                    assert len(r) == n
                    for ins in r:
                        ins.then_inc(sems[key], inc)
                else:
                    assert n == 1
                    r.then_inc(sems[key], inc)
            if name == "sp":
                for k, v in fin:
                    eng.wait_ge(sems[k], v)

        with nc.Block() as block:
            @block.tensor
            def _(e):
                replay("pe", e)

            @block.scalar
            def _(e):
                replay("act", e)

            @block.vector
            def _(e):
                replay("dve", e)

            @block.gpsimd
            def _(e):
                replay("pool", e)

            @block.sync
            def _(e):
                replay("sp", e)


def bc(ap, shape, axis):
    return ap.unsqueeze(axis).to_broadcast(list(shape))


def build(cfg):
    S = cfg["S"]
    DEPTH = cfg["DEPTH"]
    PAST = cfg["PAST"]
    NPHYS = cfg["NPHYS"]
    NIT = cfg.get("NIT", 24)
    NTP = S // 128
    NT = NTP + 1
    ST = NTP
    NPG = PAST // 128
    KP = min(256, S // 4)
    KS = min(256, (PAST + 4) // 4)
    NBS = NPG + 1
    LMAX = NTP * 128
    assert NPG == 64

    nc = bass.Bass("TRN2", target_bir_lowering=False)
    dt = lambda n, s, d, k: nc.dram_tensor(n, list(s), d, kind=k).ap()
    x_p = dt("x_p", [S, D], F32, "ExternalInput")
    x_s = dt("x_s", [16, D], F32, "ExternalInput")
    p_p = dt("p_p", [DEPTH, S, DPLE], F32, "ExternalInput")
    p_s = dt("p_s", [DEPTH, 16, DPLE], F32, "ExternalInput")
    c_k = dt("c_k", [DEPTH, NPHYS, 128, 128], F32, "ExternalInput")
    c_v = dt("c_v", [DEPTH, NPHYS, 128, 128], F32, "ExternalInput")
    c_ki = dt("c_ki", [DEPTH, NPHYS, 128, 32], F32, "ExternalInput")
    ptab = dt("ptab", [1, 4 * NPG], I32, "ExternalInput")
    w_in = dt("w_in", [DEPTH, D, DIN], F32, "ExternalInput")
    sgu_g = dt("sgu_g", [DEPTH, 512], F32, "ExternalInput")
    sgu_bb = dt("sgu_bb", [DEPTH, 512], F32, "ExternalInput")
    sgu_w = dt("sgu_w", [DEPTH, 4, 128, 128], F32, "ExternalInput")
    sgu_b = dt("sgu_b", [DEPTH, 4, 128], F32, "ExternalInput")
    w_o = dt("w_o", [DEPTH, D, D], F32, "ExternalInput")
    ln1_g = dt("ln1_g", [DEPTH, D], F32, "ExternalInput")
    ln1_b = dt("ln1_b", [DEPTH, D], F32, "ExternalInput")
    w_f1 = dt("w_f1", [DEPTH, D, DFF], F32, "ExternalInput")
    w_f2 = dt("w_f2", [DEPTH, DFF, D], F32, "ExternalInput")
    w_pg = dt("w_pg", [DEPTH, D, D], F32, "ExternalInput")
    w_pp = dt("w_pp", [DEPTH, DPLE, D], F32, "ExternalInput")
    ln2_g = dt("ln2_g", [DEPTH, D], F32, "ExternalInput")
    ln2_b = dt("ln2_b", [DEPTH, D], F32, "ExternalInput")
    c_rq = dt("c_rq", [128, NT, 128], F32, "ExternalInput")
    c_ri = dt("c_ri", [128, NT, 64], F32, "ExternalInput")
    c_msk = dt("c_msk", [128, 8, 128], F32, "ExternalInput")
    c_misc = dt("c_misc", [128, 64], F32, "ExternalInput")
    c_sel = dt("c_sel", [128, 768], F32, "ExternalInput")
    y_p = dt("y_p", [S, D], F32, "ExternalOutput")
    y_s = dt("y_s", [16, D], F32, "ExternalOutput")
    nk_p = dt("nk_p", [DEPTH, S, 128], F32, "ExternalOutput")
    nv_p = dt("nv_p", [DEPTH, S, 128], F32, "ExternalOutput")
    nki_p = dt("nki_p", [DEPTH, S, 32], F32, "ExternalOutput")
    ncv_p = dt("ncv_p", [DEPTH, 128, 512], F32, "ExternalOutput")
    nk_s = dt("nk_s", [DEPTH, 16, 128], F32, "ExternalOutput")
    nv_s = dt("nv_s", [DEPTH, 16, 128], F32, "ExternalOutput")
    nki_s = dt("nki_s", [DEPTH, 16, 32], F32, "ExternalOutput")
    ncv_s = dt("ncv_s", [DEPTH, 16, 512], F32, "ExternalOutput")
    Hs = dt("Hs", [NT, 128, D], F32, "Internal")
    HT = dt("HT", [NT, 128, D], BF16, "Internal")

    P = Prog()
    es = ExitStack()
    with es:
        off = [0]
        TOTW = 52700
        big = es.enter_context(nc.sbuf_tensor("big", [128, TOTW], F32))

        def alloc(words):
            a = off[0]
            off[0] += int(words)
            assert off[0] <= TOTW, ("sbuf overflow", off[0])
            return a

        def f32v(a, n):
            return big[:, a:a + n]

        def bfv(a, n):
            return big[:, a:a + (n + 1) // 2].bitcast(BF16)

        def A32(n):
            return f32v(alloc(n), n)

        def A16(n):
            return bfv(alloc((n + 1) // 2), n)

        a_arena = alloc(12448)
        w_in_sb = bfv(a_arena, 8 * DIN).rearrange("p (k n) -> p k n", k=8)
        w_o_sb = bfv(a_arena + 4 * DIN, 8 * D).rearrange("p (k n) -> p k n", k=8)
        slabW = 4096
        slab = [(bfv(a_arena + i * slabW, 8 * 512).rearrange("p (k n) -> p k n", k=8),
                 bfv(a_arena + i * slabW + 2048, 4 * 1024).rearrange("p (k n) -> p k n", k=4)) for i in range(2)]
        R_win, R_wo = Res("win"), Res("wo")
        R_slab = [Res("slab0"), Res("slab1")]
        R_ple = Res("ple")
        kT_all = A16(NT * 128)
        kiT_all = A16(NT * 128)
        Vaug_all = A16(NT * 130).rearrange("p (t k e) -> p t k e", t=NT, k=2)
        R_ks = [Res("ks%d" % i) for i in range(NT)]
        lng = A32(D)
        lnb = A32(D)
        sg_g = A32(512)
        sg_b = A32(512)
        WsT = A16(512).rearrange("p (g t) -> p g t", g=4)
        WsTs = A16(512).rearrange("p (g t) -> p g t", g=4)
        bs_p = A32(4)
        bs_s = A32(4)
        R_ln, R_sg, R_ws = Res("ln"), Res("sg"), Res("ws")
        rq = A32(NT * 128).rearrange("p (t c) -> p t c", t=NT)
        ri = A32(NT * 64).rearrange("p (t c) -> p t c", t=NT)
        msk = A32(1024).rearrange("p (m c) -> p m c", m=8)
        misc = A32(64)
        ident = A16(128)
        bigI = A16(128)
        idx2 = es.enter_context(nc.sbuf_tensor("idx2", [128, 104], I32))
        R_c = Res("consts")
        QIall = A16(128)
        QIB = A16(512).rearrange("p (b n) -> p b n", b=4)
        QBD = A16(256).rearrange("p (b n) -> p b n", b=4)
        WPAD = A32(240)
        BIGSEL = A16(256).rearrange("p (b n) -> p b n", b=4)
        SELR = A16(512)
        On = A16(256).rearrange("p (b n) -> p b n", b=4)
        pow2 = misc[:, 0:NIT]
        rowmask = misc[:, 32:36]
        hT_sb = A16(1024).rearrange("p (k t) -> p k t", k=8)
        au = A32(512)
        av = A32(512)
        vn32 = A32(512)
        vnb = A16(512)
        zq = A32(512)
        zr = A32(552)
        rtA = A32(512)
        rtB = A32(512)
        qr = A16(512)
        qir = A16(256)
        kb = A16(128)
        kirep = A16(128)
        k32 = A32(128)
        ki32 = A32(32)
        qTs = [A16(512).rearrange("p (g t) -> p g t", g=4) for _ in range(2)]
        qiT = A16(384).rearrange("p (g t) -> p g t", g=3)
        wi = A32(8)
        mixs = [A16(1024) for _ in range(2)]
        MBf = [A16(2048) for _ in range(2)]
        R_MBf = [Res("mbf0"), Res("mbf1")]
        st8 = A32(32)
        R = {n: Res(n) for n in ("hT", "au", "av", "vn32", "vnb", "zq", "zr", "rtA", "rtB", "qr", "qir",
                                 "kb", "kirep", "k32", "ki32", "qT0", "qT1", "qiT", "wi", "mix0", "mix1", "st8", "junk2",
                                 "mixT", "hs", "pre", "xh", "hb", "h1T", "acc", "junk", "thr", "bis",
                                 "bo", "stage0", "stage1", "pe32", "peb", "peT", "sgm")}
        a_c = alloc(0)
        hs = A32(D)
        pre = A32(D)
        xh = A32(D)
        hb = A16(D)
        h1T = A16(D).rearrange("p (k t) -> p k t", k=8)
        mixT = A16(1024).rearrange("p (k t) -> p k t", k=8)
        assert max(LMAX, 1152) <= 2 * (off[0] - a_c)
        a_acc = alloc(0)
        AW = max(LMAX, 2048)
        acc = A32(AW)
        junk = bfv(a_c, AW)
        R_junk = [R[n_] for n_ in ("hs", "pre", "xh", "hb", "h1T", "mixT")]
        rl = [A16(512), A16(512), A16(512)]
        Wd = A16(1024).rearrange("p (h c) -> p h c", h=8)
        R_Wd = Res("Wd")
        R_rl = [Res("rl0"), Res("rl1"), Res("rl2")]
        R_MB = [Res("mb0"), Res("mb1")]
        pTt = [A16(512), A16(512), A16(512)]
        R_pT = [Res("pt0"), Res("pt1"), Res("pt2")]
        thr = A32(4)
        bisT = A32(NIT + 2)
        bisC = A32(NIT + 2)
        bisS = A32(NIT + 2)
        bisM = A32(8)
        KIg = A16(2 * 32 * 32).rearrange("p (q r d) -> p q r d", q=2, r=32)
        kiTc_s = A16(2 * 8 * 128).rearrange("p (q r n) -> p q r n", q=2, r=8)
        Kg = A16(2 * 8 * 128).rearrange("p (q r d) -> p q r d", q=2, r=8)
        Vg = A16(4 * 8 * 128).rearrange("p (b r d) -> p b r d", b=4, r=8)
        kTc_s = A16(2 * 8 * 128).rearrange("p (q r n) -> p q r n", q=2, r=8)
        Vaugc_s = A16(32 * 130).rearrange("p (b r k e) -> p b r k e", b=4, r=8, k=2)
        rls = A32(512)
        Atl = A32(128)
        MBc = A16(512)
        pTs = A16(1024).rearrange("p (h n) -> p h n", h=2)
        bm = A32(8)
        g2 = A32(16)
        g2rep = A32(128)
        RS = {n_: Res(n_) for n_ in ("kis", "kiTc", "Kst", "Vst", "kTc", "Vc", "QIall", "QIB", "QBD", "rls", "WPAD", "Atl", "MBc",
                                     "pTs0", "pTs1", "On", "bm", "g2", "sel", "MBc2")}
        alloc(max(0, 12800 - (off[0] - a_acc)))
        a_end = alloc(0)
        dw = [a_acc]

        def dalloc(words):
            a = dw[0]
            dw[0] += int(words)
            assert dw[0] <= a_end, ("D work overflow", dw[0] - a_acc, a_end - a_acc, off[0])
            return a
        a_h1Tg = dalloc(2048)
        h1Tgs = [bfv(a_h1Tg + 1024 * i_, 2048).rearrange("p (k n) -> p k n", k=8) for i_ in range(2)]
        uTs = [bfv(dalloc(1024), 2048).rearrange("p (f c) -> p f c", f=4) for _ in range(2)]
        h1Tg = h1Tgs[0]
        rD = [f32v(dalloc(512), 512) for _ in range(2)]
        stage = [f32v(dalloc(D), D) for _ in range(2)]
        pe32 = f32v(dalloc(256), 256)
        peb = bfv(dalloc(128), 256)
        peT = bfv(dalloc(128), 256).rearrange("p (k t) -> p k t", k=2)
        sgm = rD[0]
        a_ple = dalloc(5120)
        wpg_sb = bfv(a_ple, 8 * D).rearrange("p (k n) -> p k n", k=8)
        wpp_sb = bfv(a_ple + 4 * D, 2 * D).rearrange("p (k n) -> p k n", k=2)
        R_h1Tgs, R_uTs = [Res("h1Tg0"), Res("h1Tg1")], [Res("uT0"), Res("uT1")]
        R_h1Tg = R_h1Tgs[0]
        R_rD = [Res(), Res()]
        R["sgm"] = R_rD[0]
        D_res = R_h1Tgs + R_uTs + [R_rD[0], R_rD[1], R["stage0"], R["stage1"], R["pe32"], R["peb"], R["peT"], R_ple]
        B_res = [R["acc"], R["thr"], R["bis"], R["bo"]] + R_rl + R_MB + R_pT + [RS[n_] for n_ in ("kis", "kiTc", "Kst", "Vst", "kTc", "Vc", "rls", "MBc", "MBc2", "pTs0", "pTs1")]

        Ob = [es.enter_context(nc.psum_tensor("Ob%d" % i, [128, 512], F32)) for i in range(2)]
        Tb = [es.enter_context(nc.psum_tensor("Tb%d" % i, [128, 1024], BF16)) for i in range(2)]
        Rb = [es.enter_context(nc.psum_tensor("Rb%d" % i, [128, 512], F32)) for i in range(4)]
        R_Ob = [Res("O0"), Res("O1")]
        R_Tb = [Res("T0"), Res("T1")]
        R_Rb = [Res("R%d" % i) for i in range(4)]
        rr = {"T": 0, "R": 0, "rl": 0, "pT": 0, "MB": 0}

        def nextT():
            i = rr["T"]
            rr["T"] = (i + 1) % 2
            return Tb[i], R_Tb[i]

        def nextR():
            i = rr["R"]
            rr["R"] = (i + 1) % 4
            return Rb[i], R_Rb[i]

        rr["RD"] = 0
        RDb = Rb + Ob
        R_RDb = R_Rb + R_Ob

        def nextRD():
            i = rr["RD"]
            rr["RD"] = (i + 1) % 6
            return RDb[i], R_RDb[i]

        def transposes(items, dst, dst_res, evac="act", extra_reads=(), nrow=128):
            tb, rtb = nextT()
            n = len(items)
            for i, (ap, res) in enumerate(items):
                P.op("pe", lambda e, ap=ap, i=i, tb=tb: e.transpose(tb[0:int(np.prod(ap.shape[1:])), i * 128:(i + 1) * 128], ap, ident),
                     reads=[res, R_c], writes=[rtb])
            if evac == "act":
                P.op("act", lambda e, tb=tb, n=n: e.activation(out=dst, in_=tb[0:nrow, 0:n * 128], func=AF.Copy),
                     reads=[rtb], writes=[dst_res])
            else:
                P.op("dve", lambda e, tb=tb, n=n: e.tensor_copy(out=dst, in_=tb[0:nrow, 0:n * 128]),
                     reads=[rtb], writes=[dst_res])

        def lnorm(src, rsrc, G, W, gam, bet, rpar, out32, rout, tmp, rtmp):
            s3 = src.rearrange("p (g w) -> p g w", g=G)
            P.op("dve", lambda e: e.tensor_reduce(out=st8[:, 0:G], in_=s3, axis=AX.X, op=ALU.add),
                 reads=[rsrc], writes=[R["st8"]])
            for g in range(G):
                P.op("act", lambda e, g=g: e.activation(out=tmp[:, g * W:(g + 1) * W], in_=src[:, g * W:(g + 1) * W],
                                                        func=AF.Square, accum_out=st8[:, 4 + g:5 + g]),
                     reads=[rsrc, R["st8"]], writes=[rtmp, R["st8"]])
            iw = 1.0 / W
            P.op("dve", lambda e: e.tensor_scalar(out=st8[:, 8:8 + G], in0=st8[:, 0:G], scalar1=iw, scalar2=None, op0=ALU.mult),
                 reads=[R["st8"]], writes=[R["st8"]])
            P.op("dve", lambda e: e.tensor_tensor(out=st8[:, 12:12 + G], in0=st8[:, 8:8 + G], in1=st8[:, 8:8 + G], op=ALU.mult),
                 reads=[R["st8"]], writes=[R["st8"]])
            P.op("dve", lambda e: e.scalar_tensor_tensor(out=st8[:, 16:16 + G], in0=st8[:, 4:4 + G], scalar=iw, in1=st8[:, 12:12 + G],
                                                         op0=ALU.mult, op1=ALU.subtract),
                 reads=[R["st8"]], writes=[R["st8"]])
            P.op("act", lambda e: e.activation(out=st8[:, 28:28 + G], in_=st8[:, 16:16 + G], func=AF.Sqrt, bias=misc[:, 42:43], scale=1.0),
                 reads=[R["st8"], R_c], writes=[R["st8"]])
            P.op("dve", lambda e: e.reciprocal(out=st8[:, 20:20 + G], in_=st8[:, 28:28 + G]),
                 reads=[R["st8"]], writes=[R["st8"]])
            P.op("dve", lambda e: e.scalar_tensor_tensor(out=st8[:, 24:24 + G], in0=st8[:, 8:8 + G], scalar=-1.0, in1=st8[:, 20:20 + G],
                                                         op0=ALU.mult, op1=ALU.mult),
                 reads=[R["st8"]], writes=[R["st8"]])
            for g in range(G):
                P.op("act", lambda e, g=g: e.activation(out=tmp[:, g * W:(g + 1) * W], in_=src[:, g * W:(g + 1) * W], func=AF.Identity,
                                                        scale=st8[:, 20 + g:21 + g], bias=st8[:, 24 + g:25 + g]),
                     reads=[rsrc, R["st8"]], writes=[rtmp])
            P.op("pool", lambda e: e.tensor_tensor(out=tmp, in0=tmp, in1=gam, op=ALU.mult), reads=[rtmp, rpar], writes=[rtmp])
            P.op("pool", lambda e: e.tensor_tensor(out=out32, in0=tmp, in1=bet, op=ALU.add), reads=[rtmp, rpar], writes=[rout])

        def rope(src, rsrc, H, Dh, tab, out, rout, qperm=False):
            hf = Dh // 2
            s3 = src.rearrange("p (h d) -> p h d", h=H)
            a3 = rtA[:, 0:H * Dh].rearrange("p (h d) -> p h d", h=H)
            b3 = rtB[:, 0:H * Dh].rearrange("p (h d) -> p h d", h=H)
            cos2 = bc(tab[:, 0:Dh], [128, H, Dh], 1)
            sn1 = bc(tab[:, Dh:Dh + hf], [128, H, hf], 1)
            sn2 = bc(tab[:, Dh + hf:2 * Dh], [128, H, hf], 1)
            P.op("pool", lambda e: e.tensor_tensor(out=a3, in0=s3, in1=cos2, op=ALU.mult), reads=[rsrc, R_c], writes=[R["rtA"]])
            P.op("pool", lambda e: e.tensor_tensor(out=b3[:, :, 0:hf], in0=s3[:, :, hf:Dh], in1=sn1, op=ALU.mult),
                 reads=[rsrc, R_c], writes=[R["rtB"]])
            P.op("pool", lambda e: e.tensor_tensor(out=b3[:, :, hf:Dh], in0=s3[:, :, 0:hf], in1=sn2, op=ALU.mult),
                 reads=[rsrc, R_c], writes=[R["rtB"]])
            if qperm:
                ov = out.rearrange("p (g k d) -> p k g d", g=4, k=2)
                i0 = rtA[:, 0:H * Dh].rearrange("p (k g d) -> p k g d", k=2, g=4)
                i1 = rtB[:, 0:H * Dh].rearrange("p (k g d) -> p k g d", k=2, g=4)
            else:
                ov, i0, i1 = out, rtA[:, 0:H * Dh], rtB[:, 0:H * Dh]
            P.op("pool", lambda e: e.tensor_tensor(out=ov, in0=i0, in1=i1, op=ALU.add),
                 reads=[R["rtA"], R["rtB"]], writes=[rout])

        P.dma("sp", lambda e: e.dma_start(out=rq, in_=c_rq[:, :, :]), writes=[R_c])
        P.dma("sp", lambda e: e.dma_start(out=ri, in_=c_ri[:, :, :]), writes=[R_c])
        P.dma("sp", lambda e: e.dma_start(out=msk, in_=c_msk[:, :, :]), writes=[R_c])
        P.dma("sp", lambda e: e.dma_start(out=misc, in_=c_misc[:, :]), writes=[R_c])
        P.dma("sp", lambda e: e.dma_start(out=zq.bitcast(I32)[:, 0:2], in_=ptab.rearrange("o (q p) -> p (o q)", q=2),
                                          allow_slow_non_contiguous=True), writes=[R["zq"]])
        P.op("dve", lambda e: e.tensor_copy(out=au[:, 0:2], in_=zq.bitcast(I32)[:, 0:2]), reads=[R["zq"]], writes=[R["au"]])
        P.op("pool", lambda e: e.iota(av[:, 0:16], [[1, 16]], base=0, channel_multiplier=0, allow_small_or_imprecise_dtypes=True), writes=[R["av"]])
        P.op("dve", lambda e: e.tensor_scalar(out=au[:, 2:4], in0=au[:, 0:2], scalar1=16.0, scalar2=None, op0=ALU.mult), reads=[R["au"]], writes=[R["au"]])
        P.op("dve", lambda e: e.tensor_scalar(out=au[:, 4:6], in0=au[:, 0:2], scalar1=4.0, scalar2=None, op0=ALU.mult), reads=[R["au"]], writes=[R["au"]])
        P.dma("sp", lambda e: e.dma_start(out=zq.bitcast(I32)[0:64, 2:6], in_=ptab.rearrange("o (b p) -> p (o b)", b=4),
                                          allow_slow_non_contiguous=True), writes=[R["zq"]])
        P.op("dve", lambda e: e.tensor_copy(out=au[0:64, 8:12], in_=zq.bitcast(I32)[0:64, 2:6]), reads=[R["zq"]], writes=[R["au"]])
        P.op("dve", lambda e: e.tensor_scalar(out=au[0:64, 12:16], in0=au[0:64, 8:12], scalar1=16.0, scalar2=None, op0=ALU.mult),
             reads=[R["au"]], writes=[R["au"]])
        for b in range(4):
            P.op("dve", lambda e, b=b: e.tensor_scalar(out=idx2[0:64, 40 + b * 16:56 + b * 16], in0=av[0:64, 0:16], scalar1=au[0:64, 12 + b:13 + b],
                                                       scalar2=None, op0=ALU.add), reads=[R["au"], R["av"]], writes=[R_c])
        for pr in range(2):
            P.op("dve", lambda e, pr=pr: e.tensor_scalar(out=idx2[:, pr * 16:(pr + 1) * 16], in0=av[:, 0:16], scalar1=au[:, 2 + pr:3 + pr], scalar2=None,
                                                         op0=ALU.add), reads=[R["au"], R["av"]], writes=[R_c])
            P.op("dve", lambda e, pr=pr: e.tensor_scalar(out=idx2[:, 32 + pr * 4:36 + pr * 4], in0=av[:, 0:4], scalar1=au[:, 4 + pr:5 + pr], scalar2=None,
                                                         op0=ALU.add), reads=[R["au"], R["av"]], writes=[R_c])
        P.op("pool", lambda e: e.iota(ident, [[1, 128]], base=0, channel_multiplier=-1, allow_small_or_imprecise_dtypes=True),
             writes=[R_c])
        P.op("dve", lambda e: e.tensor_scalar(out=bigI, in0=ident, scalar1=0.0, scalar2=BIGV, op0=ALU.is_equal, op1=ALU.mult),
             reads=[R_c], writes=[R_c])
        P.op("dve", lambda e: e.tensor_scalar(out=ident, in0=ident, scalar1=0.0, scalar2=None, op0=ALU.is_equal),
             reads=[R_c], writes=[R_c])
        P.op("pool", lambda e: e.memset(Vaug_all, 1.0), writes=R_ks)
        P.op("pool", lambda e: e.memset(hs, 0.0), writes=[R["hs"]])
        P.op("pool", lambda e: e.memset(WsTs, 0.0), writes=[R_ws])
        P.op("pool", lambda e: e.memset(bs_s, 0.0), writes=[R_ws])

        P.dma("sp", lambda e: e.dma_start(out=acc[:, 0:768], in_=c_sel[:, :]), writes=[R["acc"]])
        P.op("dve", lambda e: e.tensor_copy(out=BIGSEL.rearrange("p b n -> p (b n)"), in_=acc[:, 0:256]), reads=[R["acc"]], writes=[RS["sel"]])
        P.op("dve", lambda e: e.tensor_copy(out=SELR, in_=acc[:, 256:768]), reads=[R["acc"]], writes=[RS["sel"]])
        P.op("pool", lambda e: e.memset(QIB, 0.0), writes=[RS["QIB"]])
        P.op("pool", lambda e: e.memset(QBD, 0.0), writes=[RS["QBD"]])
        P.op("pool", lambda e: e.memset(WPAD, 0.0), writes=[RS["WPAD"]])
        P.op("pool", lambda e: e.memset(On, 0.0), writes=[RS["On"]])

        def rows(i):
            return 16 if i == ST else 128

        def to_hT_and_store(src32, rsrc, i):
            P.op("act", lambda e: e.activation(out=hb, in_=src32, func=AF.Copy), reads=[rsrc], writes=[R["hb"]])
            transposes([(hb[:, k * 128:(k + 1) * 128], R["hb"]) for k in range(8)],
                       h1T.rearrange("p k t -> p (k t)"), R["h1T"])
            P.dma("sp", lambda e, i=i: e.dma_start(out=HT[i], in_=h1T.rearrange("p k t -> p (k t)")), reads=[R["h1T"]], writes=[R_HT[i]])

        R_Hs = [Res("Hs%d" % i) for i in range(NT)]
        R_HT = [Res("HT%d" % i) for i in range(NT)]
        for i in range(NT):
            n = rows(i)
            src = x_s[:, :] if i == ST else x_p[i * 128:(i + 1) * 128, :]
            P.dma("sp", lambda e, src=src, n=n: e.dma_start(out=hs[0:n, :], in_=src), writes=[R["hs"]])
            P.dma("sp", lambda e, i=i: e.dma_start(out=Hs[i], in_=hs), reads=[R["hs"]], writes=[R_Hs[i]])
            to_hT_and_store(hs, R["hs"], i)

        def att_index(j):
            nblk, ktop, bias_m = j + 1, KP, 0
            L = nblk * 128
            nch = (nblk + 3) // 4
            P.op("dve", lambda e: e.tensor_tensor(out=Wd, in0=bc(ident, [128, 8, 128], 1), in1=bc(wi[:, 0:8], [128, 8, 128], 2), op=ALU.mult),
                 reads=[R["wi"], R_c], writes=[R_Wd])
            for c in range(nch):
                c0 = c * 512
                w = min(512, L - c0)
                kiT_c, r_ki = kiT_all[:, c0:c0 + 512], R_ks[c * 4:min(c * 4 + 4, j + 1)]
                oa, roa = Ob[c % 2], R_Ob[c % 2]
                def accum(h, ir, oa=oa, roa=roa, w=w):
                    P.op("pe", lambda e: e.matmul(oa[:, 0:w], lhsT=Wd[:, h, :], rhs=rl[ir][:, 0:w], start=(h == 0), stop=(h == 7)),
                         reads=[R_Wd, R_rl[ir]], writes=[roa])
                prev = None
                for h in range(8):
                    pb, rpb = nextR()
                    hq, hh = h % 3, h // 3
                    P.op("pe", lambda e, pb=pb, hq=hq, hh=hh, kiT_c=kiT_c, w=w: e.matmul(
                        pb[:, 0:w], lhsT=qiT[hq * 32:(hq + 1) * 32, hh, :], rhs=kiT_c[hq * 32:(hq + 1) * 32, 0:w], start=True, stop=True),
                        reads=[R["qiT"]] + r_ki, writes=[rpb])
                    ir = rr["rl"]
                    rr["rl"] = (ir + 1) % 3
                    P.op("act", lambda e, pb=pb, ir=ir, w=w: e.activation(out=rl[ir][:, 0:w], in_=pb[:, 0:w], func=AF.Relu),
                         reads=[rpb], writes=[R_rl[ir]])
                    if prev is not None:
                        accum(*prev)
                    prev = (h, ir)
                accum(*prev)
                P.op("act", lambda e, oa=oa, c0=c0, w=w: e.activation(out=acc[:, c0:c0 + w], in_=oa[:, 0:w], func=AF.Copy),
                     reads=[roa], writes=[R["acc"]])
            lastc = acc[:, L - 128:L]
            if L > ktop:
                P.op("dve", lambda e: e.tensor_reduce(out=bisM[:, 0:1], in_=acc[:, 0:L], axis=AX.X, op=ALU.max),
                     reads=[R["acc"]], writes=[R["bis"]])
                P.op("dve", lambda e: e.tensor_reduce(out=bisM[:, 1:2], in_=acc[:, 0:L], axis=AX.X, op=ALU.min),
                     reads=[R["acc"]], writes=[R["bis"]])
            P.op("dve", lambda e: e.tensor_tensor(out=lastc, in0=lastc, in1=msk[:, bias_m, :], op=ALU.add),
                 reads=[R["acc"], R_c], writes=[R["acc"]])
            if L > ktop:
                P.op("dve", lambda e: e.tensor_tensor(out=bisM[:, 2:3], in0=bisM[:, 0:1], in1=bisM[:, 1:2], op=ALU.subtract),
                     reads=[R["bis"]], writes=[R["bis"]])
                P.op("dve", lambda e: e.scalar_tensor_tensor(out=bisT[:, 0:1], in0=bisM[:, 2:3], scalar=0.5, in1=bisM[:, 1:2],
                                                             op0=ALU.mult, op1=ALU.add),
                     reads=[R["bis"]], writes=[R["bis"]])
                P.op("dve", lambda e: e.tensor_scalar(out=bisS[:, 0:NIT], in0=pow2, scalar1=bisM[:, 2:3], scalar2=None, op0=ALU.mult),
                     reads=[R["bis"], R_c], writes=[R["bis"]])
                P.op("dve", lambda e: e.memset(bisC[:, 0:NIT], 0.0), writes=[R["bis"]])
                for k in range(NIT):
                    P.op("dve", lambda e, k=k: e.tensor_scalar(out=MBf[j % 2][:, 0:L], in0=acc[:, 0:L], scalar1=bisT[:, k:k + 1], scalar2=0.0,
                                                               op0=ALU.is_ge, op1=ALU.add, accum_out=bisC[:, k:k + 1]),
                         reads=[R["acc"], R["bis"]], writes=[R_MBf[j % 2], R["bis"]])
                    P.op("dve", lambda e, k=k: e.tensor_scalar(out=bisM[:, 4:5], in0=bisC[:, k:k + 1], scalar1=float(ktop) - 0.5,
                                                               scalar2=0.5, op0=ALU.is_ge, op1=ALU.subtract),
                         reads=[R["bis"]], writes=[R["bis"]])
                    P.op("dve", lambda e, k=k: e.scalar_tensor_tensor(out=bisT[:, k + 1:k + 2], in0=bisM[:, 4:5], scalar=bisS[:, k:k + 1],
                                                                      in1=bisT[:, k:k + 1], op0=ALU.mult, op1=ALU.add),
                         reads=[R["bis"]], writes=[R["bis"]])
                P.op("dve", lambda e: e.scalar_tensor_tensor(out=thr[:, 0:1], in0=bisM[:, 2:3], scalar=-(2.0 ** -(NIT + 1)),
                                                             in1=bisT[:, NIT:NIT + 1], op0=ALU.mult, op1=ALU.add),
                     reads=[R["bis"]], writes=[R["thr"]])
            else:
                P.op("dve", lambda e: e.memset(thr[:, 0:1], -1.0e29), writes=[R["thr"]])
            P.op("dve", lambda e: e.tensor_scalar(out=MBf[j % 2][:, 0:L], in0=acc[:, 0:L], scalar1=thr[:, 0:1], scalar2=misc[:, 40:41],
                                                  op0=ALU.is_ge, op1=ALU.subtract),
                 reads=[R["acc"], R["thr"], R_c], writes=[R_MBf[j % 2]])

        def att_core(j):
            nblk = j + 1
            L = nblk * 128
            nch = (nblk + 3) // 4
            qT = qTs[j % 2]
            rqT = R["qT%d" % (j % 2)]
            MB = MBf[j % 2]
            rMB = R_MBf[j % 2]
            first = [True, True]

            def pv(h, ip, nb, V_c, r_kv):
                kv, g, ob = h // 4, h % 4, h // 4
                for b in range(nb):
                    st_ = first[ob]
                    first[ob] = False
                    P.op("pe", lambda e, b=b, st_=st_: e.matmul(
                        Ob[ob][:, g * 65:(g + 1) * 65], lhsT=pTt[ip][:, b * 128:(b + 1) * 128], rhs=V_c[:, b, kv, :],
                        start=st_, stop=False, skip_group_check=True),
                        reads=[R_pT[ip]] + r_kv, writes=[R_Ob[ob]])

            prev = None
            for c in range(nch):
                c0 = c * 512
                w = min(512, L - c0)
                nb = w // 128
                kT_c, V_c, r_kv = kT_all[:, c0:c0 + 512], Vaug_all[:, c * 4:c * 4 + 4], R_ks[c * 4:min(c * 4 + 4, j + 1)]
                for h in range(8):
                    kv, g = h // 4, h % 4
                    sb, rsb = nextR()
                    for b in range(nb):
                        P.op("pe", lambda e, sb=sb, b=b, kv=kv, g=g, kT_c=kT_c: e.matmul(
                            sb[:, b * 128:(b + 1) * 128], lhsT=kT_c[kv * 64:(kv + 1) * 64, b * 128:(b + 1) * 128],
                            rhs=qT[kv * 64:(kv + 1) * 64, g, :], start=True, stop=False, skip_group_check=True),
                            reads=[rqT] + r_kv, writes=[rsb])
                        P.op("pe", lambda e, sb=sb, b=b, c0=c0: e.matmul(
                            sb[:, b * 128:(b + 1) * 128], lhsT=MB[:, c0 + b * 128:c0 + (b + 1) * 128], rhs=bigI,
                            start=False, stop=True, skip_group_check=True),
                            reads=[rMB, R_c], writes=[rsb])
                    ip = rr["pT"]
                    rr["pT"] = (ip + 1) % 3
                    P.op("act", lambda e, sb=sb, ip=ip, w=w: e.activation(out=pTt[ip][:, 0:w], in_=sb[:, 0:w], func=AF.Exp, scale=0.125),
                         reads=[rsb], writes=[R_pT[ip]])
                    if prev is not None:
                        pv(*prev)
                    prev = (h, ip, nb, V_c, r_kv)
            pv(*prev)
            o_normalize(mixs[j % 2][:, 512:1024], R["mix%d" % (j % 2)])

        def o_normalize(dst, rdst):
            for ob in range(2):
                o3 = Ob[ob][:, 0:260].rearrange("p (g e) -> p g e", g=4)
                P.op("dve", lambda e, o3=o3: e.reciprocal(out=bisM[:, 4:8], in_=o3[:, :, 64]), reads=[R_Ob[ob]], writes=[R["bis"]])
                d3 = dst[:, ob * 256:(ob + 1) * 256].rearrange("p (g d) -> p g d", g=4)
                P.op("dve", lambda e, o3=o3, d3=d3: e.tensor_tensor(out=d3, in0=o3[:, :, 0:64], in1=bc(bisM[:, 4:8], [128, 4, 64], 2),
                                                                   op=ALU.mult),
                     reads=[R_Ob[ob], R["bis"]], writes=[rdst])

        def sample_attention(l):
            qT = qTs[ST % 2]
            mix = mixs[ST % 2]
            rqT = R["qT%d" % (ST % 2)]
            rmix = R["mix%d" % (ST % 2)]
            ckv = c_k.rearrange("l n (c r) d -> (l n c) (r d)", c=16)
            cvv = c_v.rearrange("l n (c r) d -> (l n c) (r d)", c=16)
            ckiv = c_ki.rearrange("l n (c r) d -> (l n c) (r d)", c=4)

            def gather(tab2, dst, col0, c, eoff):
                def f(e):
                    return [e.indirect_dma_start(out=dst[:, pr].rearrange("p r d -> p (r d)"), out_offset=None, in_=tab2,
                                                 in_offset=bass.IndirectOffsetOnAxis(ap=idx2[:, col0 + pr * (16 if col0 == 0 else 4) + c:
                                                                                             col0 + pr * (16 if col0 == 0 else 4) + c + 1], axis=0),
                                                 element_offset=eoff) for pr in range(2)]
                return f

            def gatherV(c, eoff):
                def f(e):
                    return [e.indirect_dma_start(out=Vg[0:64, b].rearrange("p r d -> p (r d)"), out_offset=None, in_=cvv,
                                                 in_offset=bass.IndirectOffsetOnAxis(ap=idx2[0:64, 40 + b * 16 + c:41 + b * 16 + c], axis=0),
                                                 element_offset=eoff) for b in range(4)]
                return f

            tb, rtb = nextT()
            for h in range(8):
                P.op("pe", lambda e, h=h, tb=tb: e.transpose(tb[0:32, h * 128:(h + 1) * 128], qir[:, h * 32:(h + 1) * 32], ident),
                     reads=[R["qir"], R_c], writes=[rtb])
            P.op("act", lambda e, tb=tb: e.activation(out=QIall[0:32, :].rearrange("p (r h) -> p r h", h=8),
                                                      in_=tb[0:32, :].rearrange("p (h r) -> p r h", h=8)[:, 0:16, :], func=AF.Copy),
                 reads=[rtb], writes=[RS["QIall"]])
            for b in range(4):
                P.op("act", lambda e, b=b: e.activation(out=QIB[0:32, b, b * 32:(b + 1) * 32], in_=QIall[0:32, b * 32:(b + 1) * 32], func=AF.Copy),
                     reads=[RS["QIall"]], writes=[RS["QIB"]])
            for kv in range(2):
                P.op("act", lambda e, kv=kv: e.activation(
                    out=QBD[kv * 64:(kv + 1) * 64, :, kv * 32:kv * 32 + 16].rearrange("p b (g t) -> p b g t", g=4),
                    in_=qT[kv * 64:(kv + 1) * 64, :, 0:16].rearrange("p g (b t) -> p b g t", b=4), func=AF.Copy),
                    reads=[rqT], writes=[RS["QBD"]])
            P.op("dve", lambda e: e.tensor_tensor(out=Atl[0:16, :].rearrange("p (r h) -> p r h", h=8), in0=bc(wi[0:16, :], [16, 16, 8], 1),
                                                  in1=msk[0:16, 4, :].rearrange("p (r h) -> p r h", h=8), op=ALU.mult),
                 reads=[R["wi"], R_c], writes=[RS["Atl"]])
            pw, rpw = nextR()
            P.op("pe", lambda e, pw=pw: e.matmul(pw[:, 0:16], lhsT=Atl[0:16, :], rhs=msk[0:16, 5, 0:16], start=True, stop=True),
                 reads=[RS["Atl"], R_c], writes=[rpw])
            P.op("act", lambda e, pw=pw: e.activation(out=WPAD[:, 112:128], in_=pw[:, 0:16], func=AF.Copy), reads=[rpw], writes=[RS["WPAD"]])

            for c in range(16):
                part, grp = c % 8, c // 8
                if c % 4 == 0:
                    P.dma("pool", gather(ckiv, KIg, 32, c // 4, l * NPHYS * 4096), reads=[R_c], writes=[RS["kis"]], n=2)
                for pr in range(2):
                    transposes([(KIg[:, pr, (c % 4) * 8 + rl_, :], RS["kis"]) for rl_ in range(8)],
                               kiTc_s[0:32, pr].rearrange("p r n -> p (r n)"), RS["kiTc"], nrow=32)
                pb, rpb = nextR()
                for rl_ in range(8):
                    for b in range(4):
                        P.op("pe", lambda e, pb=pb, b=b, rl_=rl_: e.matmul(pb[:, rl_ * 64:(rl_ + 1) * 64], lhsT=QIB[0:32, b, :],
                                                                          rhs=kiTc_s[0:32, b // 2, rl_, (b % 2) * 64:(b % 2) * 64 + 64],
                                                                          start=(b == 0), stop=(b == 3), skip_group_check=True),
                             reads=[RS["QIB"], RS["kiTc"]], writes=[rpb])
                P.op("act", lambda e, pb=pb: e.activation(out=rls, in_=pb[:, 0:512], func=AF.Relu), reads=[rpb], writes=[RS["rls"]])
                P.op("pe", lambda e, part=part, grp=grp: e.matmul(Ob[grp][:, 0:512], lhsT=WPAD[:, (7 - part) * 16:(7 - part) * 16 + 128], rhs=rls,
                                                                 start=(part == 0), stop=(part == 7)),
                     reads=[RS["WPAD"], RS["rls"]], writes=[R_Ob[grp]])
                if part == 7:
                    P.op("act", lambda e, grp=grp: e.activation(out=acc[:, grp * 512:(grp + 1) * 512], in_=Ob[grp][:, 0:512], func=AF.Copy),
                         reads=[R_Ob[grp]], writes=[R["acc"]])
            pb, rpb = nextR()
            P.op("pe", lambda e, pb=pb: e.matmul(pb[:, 0:128], lhsT=QIall[0:32, :], rhs=kiT_all[0:32, ST * 128:(ST + 1) * 128], start=True, stop=True),
                 reads=[RS["QIall"], R_ks[ST]], writes=[rpb])
            P.op("act", lambda e, pb=pb: e.activation(out=rls[:, 0:128], in_=pb[:, 0:128], func=AF.Relu), reads=[rpb], writes=[RS["rls"]])
            pn, rpn = nextR()
            P.op("pe", lambda e, pn=pn: e.matmul(pn[:, 0:128], lhsT=WPAD[:, 112:240], rhs=rls[:, 0:128], start=True, stop=True),
                 reads=[RS["WPAD"], RS["rls"]], writes=[rpn])
            P.op("dve", lambda e, pn=pn: e.tensor_tensor(out=acc[:, 1024:1152], in0=pn[:, 0:128], in1=msk[:, 7, :], op=ALU.add),
                 reads=[rpn, R_c], writes=[R["acc"]])

            LS = 1152
            P.op("dve", lambda e: e.tensor_reduce(out=bm[:, 0:1], in_=acc[:, 0:LS], axis=AX.X, op=ALU.max), reads=[R["acc"]], writes=[RS["bm"]])
            P.op("dve", lambda e: e.tensor_reduce(out=bm[:, 2:3], in_=acc[:, 0:1024], axis=AX.X, op=ALU.min), reads=[R["acc"]], writes=[RS["bm"]])
            P.op("dve", lambda e: e.tensor_scalar(out=bm[:, 1:2], in0=bm[:, 2:3], scalar1=-1.0, scalar2=None, op0=ALU.mult),
                 reads=[RS["bm"]], writes=[RS["bm"]])
            p1, rp1 = nextR()
            P.op("pe", lambda e, p1=p1: e.matmul(p1[0:2, 0:128], lhsT=bm[:, 0:2], rhs=msk[:, 5, :], start=True, stop=True),
                 reads=[RS["bm"], R_c], writes=[rp1])
            P.op("dve", lambda e, p1=p1: e.tensor_reduce(out=g2[0:2, 0:16], in_=p1[0:2, 0:128].rearrange("o (p r) -> o r p", p=8), axis=AX.X, op=ALU.max),
                 reads=[rp1], writes=[RS["g2"]])
            P.op("dve", lambda e: e.tensor_copy(out=g2rep[0:2, :].rearrange("o (p r) -> o p r", p=8), in_=bc(g2[0:2, 0:16], [2, 8, 16], 1)),
                 reads=[RS["g2"]], writes=[RS["g2"]])
            p2, rp2 = nextR()
            P.op("pe", lambda e, p2=p2: e.matmul(p2[:, 0:2], lhsT=g2rep[0:2, :], rhs=msk[0:2, 5, 0:2], start=True, stop=True),
                 reads=[RS["g2"], R_c], writes=[rp2])
            P.op("dve", lambda e, p2=p2: e.tensor_copy(out=bisM[:, 0:2], in_=p2[:, 0:2]), reads=[rp2], writes=[R["bis"]])
            P.op("dve", lambda e: e.tensor_tensor(out=bisM[:, 2:3], in0=bisM[:, 0:1], in1=bisM[:, 1:2], op=ALU.add),
                 reads=[R["bis"]], writes=[R["bis"]])
            P.op("dve", lambda e: e.tensor_scalar(out=bisM[:, 3:4], in0=bisM[:, 1:2], scalar1=-1.0, scalar2=None, op0=ALU.mult),
                 reads=[R["bis"]], writes=[R["bis"]])
            P.op("dve", lambda e: e.scalar_tensor_tensor(out=bisT[:, 0:1], in0=bisM[:, 2:3], scalar=0.5, in1=bisM[:, 3:4], op0=ALU.mult, op1=ALU.add),
                 reads=[R["bis"]], writes=[R["bis"]])
            P.op("dve", lambda e: e.tensor_scalar(out=bisS[:, 0:NIT], in0=pow2, scalar1=bisM[:, 2:3], scalar2=None, op0=ALU.mult),
                 reads=[R["bis"], R_c], writes=[R["bis"]])
            P.op("dve", lambda e: e.memset(bisC[:, 0:NIT], 0.0), writes=[R["bis"]])
            for k in range(NIT):
                P.op("dve", lambda e, k=k: e.tensor_scalar(out=junk[:, 0:LS], in0=acc[:, 0:LS], scalar1=bisT[:, k:k + 1], scalar2=0.0,
                                                           op0=ALU.is_ge, op1=ALU.add, accum_out=bisC[:, k:k + 1]),
                     reads=[R["acc"], R["bis"]], writes=R_junk + [R["bis"]])
                pc, rpc = nextR()
                P.op("pe", lambda e, k=k, pc=pc: e.matmul(pc[:, 0:1], lhsT=msk[:, 6, :], rhs=bisC[:, k:k + 1], start=True, stop=True),
                     reads=[R["bis"], R_c], writes=[rpc])
                P.op("dve", lambda e, pc=pc: e.tensor_scalar(out=bisM[:, 4:5], in0=pc[:, 0:1], scalar1=float(KS) - 0.5, scalar2=0.5,
                                                             op0=ALU.is_ge, op1=ALU.subtract),
                     reads=[rpc], writes=[R["bis"]])
                P.op("dve", lambda e, k=k: e.scalar_tensor_tensor(out=bisT[:, k + 1:k + 2], in0=bisM[:, 4:5], scalar=bisS[:, k:k + 1],
                                                                  in1=bisT[:, k:k + 1], op0=ALU.mult, op1=ALU.add),
                     reads=[R["bis"]], writes=[R["bis"]])
            P.op("dve", lambda e: e.scalar_tensor_tensor(out=thr[:, 0:1], in0=bisM[:, 2:3], scalar=-(2.0 ** -(NIT + 1)),
                                                         in1=bisT[:, NIT:NIT + 1], op0=ALU.mult, op1=ALU.add),
                 reads=[R["bis"]], writes=[R["thr"]])

            first = [True, True]

            def mask_chunk(c0, w, part):
                P.op("dve", lambda e: e.tensor_scalar(out=MBc[:, 0:w], in0=acc[:, c0:c0 + w], scalar1=thr[:, 0:1], scalar2=misc[:, 40:41],
                                                      op0=ALU.is_ge, op1=ALU.subtract),
                     reads=[R["acc"], R["thr"], R_c], writes=[RS["MBc"]])
                P.op("dve", lambda e: e.tensor_scalar(out=MBc[:, 0:w], in0=MBc[:, 0:w], scalar1=misc[:, 44 + part:45 + part], scalar2=None, op0=ALU.mult),
                     reads=[RS["MBc"], R_c], writes=[RS["MBc"]])

            def pv(ob, lhs, rhs, rd):
                st_ = first[ob // 2]
                first[ob // 2] = False
                P.op("pe", lambda e: e.matmul(Ob[ob // 2][0:64, (ob % 2) * 130:(ob % 2) * 130 + 130], lhsT=lhs, rhs=rhs,
                                              start=st_, stop=False, skip_group_check=True),
                     reads=rd, writes=[R_Ob[ob // 2]])

            for c in range(16):
                part, grp = c % 8, c // 8
                mask_chunk(grp * 512, 512, part)
                P.dma("pool", gather(ckv, Kg, 0, c, l * NPHYS * 16384), reads=[R_c], writes=[RS["Kst"]], n=2)
                P.dma("pool", gatherV(c, l * NPHYS * 16384), reads=[R_c], writes=[RS["Vst"]], n=4)
                for pr in range(2):
                    transposes([(Kg[:, pr, rl_, :], RS["Kst"]) for rl_ in range(8)],
                               kTc_s[:, pr].rearrange("p r n -> p (r n)"), RS["kTc"], evac="dve" if pr else "act")
                P.op("dve", lambda e: e.tensor_copy(out=Vaugc_s[0:64, :, :, :, 0:64].rearrange("p b r k d -> p (b r) k d"),
                                                     in_=Vg[0:64].rearrange("p b r (k d) -> p (b r) k d", k=2)),
                     reads=[RS["Vst"]], writes=[RS["Vc"]])
                for pr in range(2):
                    for r4 in range(2):
                        sb, rsb = nextR()
                        hb = (pr * 2 + r4) % 2
                        for rq in range(4):
                            rl_ = r4 * 4 + rq
                            for b2 in range(2):
                                b = 2 * pr + b2
                                o_ = (rq * 2 + b2) * 64
                                P.op("pe", lambda e, sb=sb, b=b, b2=b2, pr=pr, rl_=rl_, o_=o_: e.matmul(
                                    sb[0:64, o_:o_ + 64], lhsT=kTc_s[:, pr, rl_, b2 * 64:(b2 + 1) * 64], rhs=QBD[:, b, :],
                                    start=True, stop=False, skip_group_check=True), reads=[RS["kTc"], RS["QBD"]], writes=[rsb])
                                P.op("pe", lambda e, sb=sb, b=b, rl_=rl_, o_=o_: e.matmul(
                                    sb[0:64, o_:o_ + 64], lhsT=MBc[:, rl_ * 64:(rl_ + 1) * 64], rhs=BIGSEL[:, b, :],
                                    start=False, stop=True, skip_group_check=True), reads=[RS["MBc"], RS["sel"]], writes=[rsb])
                        P.op("act", lambda e, sb=sb, hb=hb: e.activation(out=pTs[0:64, hb, :], in_=sb[0:64, 0:512], func=AF.Exp, scale=0.125),
                             reads=[rsb], writes=[RS["pTs%d" % hb]])
                        for rq in range(4):
                            rl_ = r4 * 4 + rq
                            for b2 in range(2):
                                b = 2 * pr + b2
                                o_ = (rq * 2 + b2) * 64
                                pv(b, pTs[0:64, hb, o_:o_ + 64],
                                   Vaugc_s[0:64, b, rl_].rearrange("p k e -> p (k e)"), [RS["pTs%d" % hb], RS["Vc"]])
            mask_chunk(1024, 128, 0)
            sb, rsb = nextR()
            for b in range(4):
                P.op("pe", lambda e, sb=sb, b=b: e.matmul(sb[:, b * 64:(b + 1) * 64], lhsT=kT_all[:, ST * 128:(ST + 1) * 128], rhs=QBD[:, b, :],
                                                          start=True, stop=False, skip_group_check=True), reads=[R_ks[ST], RS["QBD"]], writes=[rsb])
                P.op("pe", lambda e, sb=sb, b=b: e.matmul(sb[:, b * 64:(b + 1) * 64], lhsT=MBc[:, 0:128], rhs=BIGSEL[:, b, :],
                                                          start=False, stop=True, skip_group_check=True), reads=[RS["MBc"], RS["sel"]], writes=[rsb])
            P.op("act", lambda e, sb=sb: e.activation(out=pTs[:, 0, 0:256], in_=sb[:, 0:256], func=AF.Exp, scale=0.125), reads=[rsb], writes=[RS["pTs0"]])
            for b in range(4):
                pv(b, pTs[:, 0, b * 64:(b + 1) * 64], Vaug_all[:, ST].rearrange("p k e -> p (k e)"), [RS["pTs0"], R_ks[ST]])
            for b in range(4):
                o3 = Ob[b // 2][0:64, (b % 2) * 130:(b % 2) * 130 + 130].rearrange("p (k e) -> p k e", k=2)
                for kv in range(2):
                    r0 = kv * 32
                    P.op("dve", lambda e, o3=o3, kv=kv, r0=r0: e.reciprocal(out=bm[r0:r0 + 16, 4:5], in_=o3[r0:r0 + 16, kv, 64:65]),
                         reads=[R_Ob[b // 2]], writes=[RS["bm"]])
                    P.op("dve", lambda e, o3=o3, kv=kv, r0=r0, b=b: e.tensor_scalar(out=On[r0:r0 + 16, b, :], in0=o3[r0:r0 + 16, kv, 0:64],
                                                                                   scalar1=bm[r0:r0 + 16, 4:5], scalar2=None, op0=ALU.mult),
                         reads=[R_Ob[b // 2], RS["bm"]], writes=[RS["On"]])
            bp, rbp = nextR()
            for h in range(8):
                for b in range(4):
                    P.op("pe", lambda e, bp=bp, h=h, b=b: e.matmul(bp[0:16, h * 64:(h + 1) * 64], lhsT=SELR[0:64, (b * 8 + h) * 16:(b * 8 + h) * 16 + 16],
                                                                  rhs=On[0:64, b, :], start=(b == 0), stop=(b == 3), skip_group_check=True),
                         reads=[RS["sel"], RS["On"]], writes=[rbp])
            P.op("act", lambda e, bp=bp: e.activation(out=mix[0:16, 512:1024], in_=bp[0:16, 0:512], func=AF.Copy), reads=[rbp], writes=[rmix])

        for l in range(DEPTH):
            last = l == DEPTH - 1
            def load_attn_weights(ll):
                P.fence([R_win, R_wo], [R_slab[0], R_slab[1]])
                P.dma("pool", lambda e: e.dma_start(out=w_in_sb, in_=w_in[ll].rearrange("(k p) n -> p k n", p=128)), writes=[R_win])
                P.dma("pool", lambda e: e.dma_start(out=w_o_sb, in_=w_o[ll].rearrange("(k p) n -> p k n", p=128)), writes=[R_wo])

            def load_slab(p_, l=l):
                s_ = p_ % 2
                W1s_, W2s_ = slab[s_]
                P.dma("pool", lambda e: e.dma_start(
                    out=W1s_, in_=w_f1[l, :, p_ * 512:(p_ + 1) * 512].rearrange("(k p) n -> p k n", p=128)), writes=[R_slab[s_]])
                P.dma("pool", lambda e: e.dma_start(
                    out=W2s_, in_=w_f2[l, p_ * 512:(p_ + 1) * 512, :].rearrange("(k p) n -> p k n", p=128)), writes=[R_slab[s_]])

            if l == 0:
                load_attn_weights(0)
            P.dma("sp", lambda e, l=l: e.dma_start(out=lng, in_=ln1_g[l:l + 1, :].to_broadcast([128, D])), writes=[R_ln])
            P.dma("sp", lambda e, l=l: e.dma_start(out=lnb, in_=ln1_b[l:l + 1, :].to_broadcast([128, D])), writes=[R_ln])
            P.dma("sp", lambda e, l=l: e.dma_start(out=sg_g, in_=sgu_g[l:l + 1, :].to_broadcast([128, 512])), writes=[R_sg])
            P.dma("sp", lambda e, l=l: e.dma_start(out=sg_b, in_=sgu_bb[l:l + 1, :].to_broadcast([128, 512])), writes=[R_sg])
            P.dma("sp", lambda e, l=l: e.dma_start(out=zq.rearrange("p (g s) -> p g s", g=4), in_=sgu_w[l].rearrange("g t s -> t g s")),
                  writes=[R["zq"]])
            P.op("dve", lambda e: e.tensor_tensor(out=zq.rearrange("p (g s) -> p g s", g=4), in0=zq.rearrange("p (g s) -> p g s", g=4),
                                                  in1=bc(msk[:, 2, :], [128, 4, 128], 1), op=ALU.mult),
                 reads=[R["zq"], R_c], writes=[R["zq"]])
            P.op("dve", lambda e: e.tensor_copy(out=qr, in_=zq), reads=[R["zq"]], writes=[R["qr"]])
            transposes([(qr[:, g * 128:(g + 1) * 128], R["qr"]) for g in range(4)], WsT.rearrange("p g t -> p (g t)"), R_ws)
            P.dma("sp", lambda e, l=l: [e.dma_start(out=zq[b * 4:(b + 1) * 4, :].rearrange("p (g s) -> p g s", g=4)[:, :, b2 * 4:(b2 + 1) * 4],
                                                    in_=sgu_w[l, :, 0:4, 0:4].rearrange("g t s -> t g s"))
                                        for b in range(4) for b2 in range(4)],
                  reads=[R["qr"]], writes=[R["zq"]], n=16)
            P.op("dve", lambda e: e.tensor_tensor(out=zq[0:16, :].rearrange("p (g s) -> p g s", g=4)[:, :, 0:16],
                                                  in0=zq[0:16, :].rearrange("p (g s) -> p g s", g=4)[:, :, 0:16],
                                                  in1=bc(msk[0:16, 3, 0:16], [16, 4, 16], 1), op=ALU.mult),
                 reads=[R["zq"], R_c], writes=[R["zq"]])
            P.op("dve", lambda e: e.memset(qr, 0.0), reads=[], writes=[R["qr"]])
            P.op("dve", lambda e: e.tensor_copy(out=qr[0:16, :].rearrange("p (g s) -> p g s", g=4)[:, :, 0:16],
                                                in_=zq[0:16, :].rearrange("p (g s) -> p g s", g=4)[:, :, 0:16]),
                 reads=[R["zq"]], writes=[R["qr"]])
            transposes([(qr[:, g * 128:(g + 1) * 128], R["qr"]) for g in range(4)], WsTs.rearrange("p g t -> p (g t)"), R_ws)
            P.dma("sp", lambda e, l=l: e.dma_start(out=bs_p, in_=sgu_b[l].rearrange("g t -> t g"), allow_slow_non_contiguous=True),
                  writes=[R_ws])
            P.dma("sp", lambda e, l=l: [e.dma_start(out=bs_s[b * 4:(b + 1) * 4, :], in_=sgu_b[l, :, 0:4].rearrange("g t -> t g"),
                                                    allow_slow_non_contiguous=True) for b in range(4)],
                  writes=[R_ws], n=4)
            P.fence(B_res, D_res)
            P.op("pool", lambda e: e.memset(Vaugc_s, 1.0), writes=[RS["Vc"]])
            P.fence(R_ks, [])

            def phaseA(i):
                n = rows(i)
                is_s = i == ST
                qT = qTs[i % 2]
                mix = mixs[i % 2]
                rqT = R["qT%d" % (i % 2)]
                rmix = R["mix%d" % (i % 2)]
                P.dma("sp", lambda e, i=i: e.dma_start(out=hT_sb.rearrange("p k t -> p (k t)"), in_=HT[i]), reads=[R_HT[i]], writes=[R["hT"]])
                chunks = [(0, 512), (512, 512), (1024, 512), (1536, 512), (2048, 40)]
                zb = []
                for (c0, w) in chunks:
                    pb, rpb = nextR()
                    for k in range(8):
                        P.op("pe", lambda e, pb=pb, k=k, c0=c0, w=w: e.matmul(pb[:, 0:w], lhsT=hT_sb[:, k, :], rhs=w_in_sb[:, k, c0:c0 + w],
                                                                            start=(k == 0), stop=(k == 7)),
                             reads=[R["hT"], R_win], writes=[rpb])
                    zb.append((pb, rpb))
                    if c0 == 0:
                        P.op("act", lambda e, pb=pb: e.activation(out=au, in_=pb[:, 0:512], func=AF.Gelu), reads=[rpb], writes=[R["au"]])
                    elif c0 == 512:
                        P.op("act", lambda e, pb=pb: e.activation(out=av, in_=pb[:, 0:512], func=AF.Gelu), reads=[rpb], writes=[R["av"]])
                    elif c0 == 1024:
                        P.op("act", lambda e, pb=pb: e.activation(out=zq, in_=pb[:, 0:512], func=AF.Copy), reads=[rpb], writes=[R["zq"]])
                    elif c0 == 1536:
                        P.op("act", lambda e, pb=pb: e.activation(out=zr[:, 0:512], in_=pb[:, 0:512], func=AF.Copy), reads=[rpb], writes=[R["zr"]])
                    else:
                        P.op("act", lambda e, pb=pb: e.activation(out=zr[:, 512:552], in_=pb[:, 0:40], func=AF.Copy), reads=[rpb], writes=[R["zr"]])
                lnorm(av, R["av"], 4, 128, sg_g, sg_b, R_sg, vn32, R["vn32"], rtA, R["rtA"])
                P.op("pool", lambda e: e.tensor_copy(out=vnb, in_=vn32), reads=[R["vn32"]], writes=[R["vnb"]])
                if i == NTP - 1:
                    P.dma("sp", lambda e, l=l: e.dma_start(out=ncv_p[l], in_=vn32), reads=[R["vn32"]])
                if is_s:
                    P.dma("sp", lambda e, l=l: e.dma_start(out=ncv_s[l], in_=vn32[0:16, :]), reads=[R["vn32"]])
                gb, rgb = nextR()
                wst = WsTs if is_s else WsT
                bsx = bs_s if is_s else bs_p
                for g in range(4):
                    P.op("pe", lambda e, g=g, gb=gb, wst=wst: e.matmul(gb[:, g * 128:(g + 1) * 128], lhsT=wst[:, g, :],
                                                                      rhs=vnb[:, g * 128:(g + 1) * 128], start=True, stop=True,
                                                                      skip_group_check=True),
                         reads=[R_ws, R["vnb"]], writes=[rgb])
                for g in range(4):
                    P.op("dve", lambda e, g=g, gb=gb, bsx=bsx: e.scalar_tensor_tensor(
                        out=mix[:, g * 128:(g + 1) * 128], in0=gb[:, g * 128:(g + 1) * 128], scalar=bsx[:, g:g + 1],
                        in1=au[:, g * 128:(g + 1) * 128], op0=ALU.add, op1=ALU.mult),
                        reads=[rgb, R_ws, R["au"]], writes=[rmix])
                rope(zq, R["zq"], 8, 64, rq[:, i, :], qr, R["qr"], qperm=True)
                rope(zr[:, 0:128], R["zr"], 2, 64, rq[:, i, :], k32, R["k32"])
                P.op("act", lambda e: e.activation(out=kb, in_=k32, func=AF.Copy), reads=[R["k32"]], writes=[R["kb"]])
                rope(zr[:, 256:512], R["zr"], 8, 32, ri[:, i, :], qir, R["qir"])
                rope(zr[:, 512:544], R["zr"], 1, 32, ri[:, i, :], ki32, R["ki32"])
                P.op("act", lambda e: e.activation(out=kirep.rearrange("p (r d) -> p r d", r=4), in_=bc(ki32, [128, 4, 32], 1), func=AF.Copy),
                     reads=[R["ki32"]], writes=[R["kirep"]])
                P.op("act", lambda e: e.activation(out=wi, in_=zr[:, 544:552], func=AF.Copy), reads=[R["zr"]], writes=[R["wi"]])
                P.op("act", lambda e, i=i: e.activation(out=Vaug_all[:, i, :, 0:64], in_=zr[:, 128:256].rearrange("p (k d) -> p k d", k=2),
                                                        func=AF.Copy), reads=[R["zr"]], writes=[R_ks[i]])
                if is_s:
                    P.dma("sp", lambda e, l=l: e.dma_start(out=nk_s[l], in_=k32[0:16, :]), reads=[R["k32"]])
                    P.dma("sp", lambda e, l=l: e.dma_start(out=nv_s[l], in_=zr[0:16, 128:256]), reads=[R["zr"]])
                    P.dma("sp", lambda e, l=l: e.dma_start(out=nki_s[l], in_=ki32[0:16, :]), reads=[R["ki32"]])
                else:
                    P.dma("sp", lambda e, l=l, i=i: e.dma_start(out=nk_p[l, i * 128:(i + 1) * 128, :], in_=k32), reads=[R["k32"]])
                    P.dma("sp", lambda e, l=l, i=i: e.dma_start(out=nv_p[l, i * 128:(i + 1) * 128, :], in_=zr[:, 128:256]), reads=[R["zr"]])
                    P.dma("sp", lambda e, l=l, i=i: e.dma_start(out=nki_p[l, i * 128:(i + 1) * 128, :], in_=ki32), reads=[R["ki32"]])
                tb, rtb = nextT()
                its = [(qr[:, g * 128:(g + 1) * 128], R["qr"]) for g in range(4)] + [(kb, R["kb"])] + \
                      [(qir[:, 0:96], R["qir"]), (qir[:, 96:192], R["qir"]), (qir[:, 192:256], R["qir"])]
                for j_, (ap, res) in enumerate(its):
                    P.op("pe", lambda e, ap=ap, j_=j_, tb=tb: e.transpose(tb[0:int(np.prod(ap.shape[1:])), j_ * 128:(j_ + 1) * 128], ap, ident),
                         reads=[res, R_c], writes=[rtb])
                P.op("act", lambda e, tb=tb: e.activation(out=qT.rearrange("p g t -> p (g t)"), in_=tb[:, 0:512], func=AF.Copy),
                     reads=[rtb], writes=[rqT])
                P.op("act", lambda e, tb=tb, i=i: e.activation(out=kT_all[:, i * 128:(i + 1) * 128], in_=tb[:, 512:640], func=AF.Copy),
                     reads=[rtb], writes=[R_ks[i]])
                P.op("act", lambda e, tb=tb: e.activation(out=qiT.rearrange("p g t -> p (g t)"), in_=tb[:, 640:1024], func=AF.Copy),
                     reads=[rtb], writes=[R["qiT"]])
                transposes([(kirep, R["kirep"])], kiT_all[:, i * 128:(i + 1) * 128], R_ks[i])


            def phaseC(i):
                n = rows(i)
                is_s = i == ST
                qT = qTs[i % 2]
                mix = mixs[i % 2]
                rqT = R["qT%d" % (i % 2)]
                rmix = R["mix%d" % (i % 2)]
                transposes([(mix[:, k * 128:(k + 1) * 128], rmix) for k in range(8)], mixT.rearrange("p k t -> p (k t)"), R["mixT"])
                P.dma("sp", lambda e, i=i: e.dma_start(out=hs, in_=Hs[i]), reads=[R_Hs[i]], writes=[R["hs"]])
                for hf in range(2):
                    pb, rpb = nextR()
                    for k in range(8):
                        P.op("pe", lambda e, pb=pb, k=k, hf=hf: e.matmul(pb[:, 0:512], lhsT=mixT[:, k, :], rhs=w_o_sb[:, k, hf * 512:(hf + 1) * 512],
                                                                        start=(k == 0), stop=(k == 7)),
                             reads=[R["mixT"], R_wo], writes=[rpb])
                    P.op("dve", lambda e, pb=pb, hf=hf: e.scalar_tensor_tensor(out=pre[:, hf * 512:(hf + 1) * 512], in0=hs[:, hf * 512:(hf + 1) * 512],
                                                                               scalar=ALPHA, in1=pb[:, 0:512], op0=ALU.mult, op1=ALU.add),
                         reads=[rpb, R["hs"]], writes=[R["pre"]])
                lnorm(pre, R["pre"], 1, D, lng, lnb, R_ln, hs, R["hs"], xh, R["xh"])
                P.op("act", lambda e: e.activation(out=xh, in_=hs, func=AF.Copy, scale=ALPHA), reads=[R["hs"]], writes=[R["xh"]])
                P.dma("sp", lambda e, i=i: e.dma_start(out=Hs[i], in_=xh), reads=[R["xh"]], writes=[R_Hs[i]])
                to_hT_and_store(hs, R["hs"], i)


            phaseA(0)
            att_index(0)
            for j in range(NTP):
                if j + 1 < NTP:
                    phaseA(j + 1)
                    att_index(j + 1)
                else:
                    phaseA(ST)
                    P.fence([R_slab[0], R_slab[1]], [R_win])
                    load_slab(0)
                    load_slab(1)
                att_core(j)
                phaseC(j)
            sample_attention(l)
            phaseC(ST)

            P.fence(D_res, B_res)
            P.op("pool", lambda e: e.memset(pe32, 0.0), writes=[R["pe32"]])
            P.dma("sp", lambda e, l=l: e.dma_start(out=lng, in_=ln2_g[l:l + 1, :].to_broadcast([128, D])), writes=[R_ln])
            P.dma("sp", lambda e, l=l: e.dma_start(out=lnb, in_=ln2_b[l:l + 1, :].to_broadcast([128, D])), writes=[R_ln])
            groups = [list(range(g0, min(g0 + 2, NTP))) for g0 in range(0, NTP, 2)] + [[ST]]
            sgi = [0]

            def accum(i, src, rsrc):
                P.dma("pool", lambda e, i=i: e.dma_start(out=Hs[i], in_=src, accum_op=ALU.add), reads=[rsrc], writes=[R_Hs[i]])

            gsel = [0]
            for p_ in range(8):
                s = p_ % 2
                W1s, W2s = slab[s]
                if 1 <= p_ <= 6:
                    load_slab(p_ + 1)
                if p_ == 0:
                    P.dma("pool", lambda e, l=l: e.dma_start(out=wpg_sb, in_=w_pg[l].rearrange("(k p) n -> p k n", p=128)), writes=[R_ple])
                    P.dma("pool", lambda e, l=l: e.dma_start(out=wpp_sb, in_=w_pp[l].rearrange("(k p) n -> p k n", p=128)), writes=[R_ple])
                for grp in groups:
                    ng = len(grp)
                    gi_ = gsel[0]
                    gsel[0] = 1 - gi_
                    h1Tg_, R_h1Tg_, uT, R_uT = h1Tgs[gi_], R_h1Tgs[gi_], uTs[gi_], R_uTs[gi_]
                    P.dma("sp", lambda e, grp=grp, ng=ng, h1Tg_=h1Tg_: [e.dma_start(
                        out=h1Tg_[:, :, ti * 128:(ti + 1) * 128], in_=HT[t].rearrange("p (k c) -> p k c", k=8)) for ti, t in enumerate(grp)],
                        n=ng,
                        reads=[R_HT[t] for t in grp], writes=[R_h1Tg_])
                    for fb in range(4):
                        pb, rpb = nextRD()
                        for k in range(8):
                            P.op("pe", lambda e, pb=pb, k=k, fb=fb, ng=ng, W1s=W1s, h1Tg_=h1Tg_: e.matmul(
                                pb[:, 0:ng * 128], lhsT=W1s[:, k, fb * 128:(fb + 1) * 128], rhs=h1Tg_[:, k, 0:ng * 128],
                                start=(k == 0), stop=(k == 7)), reads=[R_slab[s], R_h1Tg_], writes=[rpb])
                        ir = fb % 2
                        P.op("act", lambda e, pb=pb, ir=ir, ng=ng: e.activation(out=rD[ir][:, 0:ng * 128], in_=pb[:, 0:ng * 128], func=AF.Relu),
                             reads=[rpb], writes=[R_rD[ir]])
                        P.op("pool", lambda e, ir=ir, fb=fb, ng=ng, uT=uT: e.tensor_tensor(out=uT[:, fb, 0:ng * 128], in0=rD[ir][:, 0:ng * 128],
                                                                                   in1=rD[ir][:, 0:ng * 128], op=ALU.mult),
                             reads=[R_rD[ir]], writes=[R_uT])
                    for ti, t in enumerate(grp):
                        si = sgi[0]
                        sgi[0] = 1 - si
                        for hf in range(2):
                            pb, rpb = nextRD()
                            for fb in range(4):
                                P.op("pe", lambda e, pb=pb, fb=fb, ti=ti, hf=hf, W2s=W2s, uT=uT: e.matmul(
                                    pb[:, 0:512], lhsT=uT[:, fb, ti * 128:(ti + 1) * 128], rhs=W2s[:, fb, hf * 512:(hf + 1) * 512],
                                    start=(fb == 0), stop=(fb == 3)), reads=[R_slab[s], R_uT], writes=[rpb])
                            P.op("act", lambda e, pb=pb, si=si, hf=hf: e.activation(out=stage[si][:, hf * 512:(hf + 1) * 512], in_=pb[:, 0:512],
                                                                                    func=AF.Copy),
                                 reads=[rpb], writes=[R["stage%d" % si]])
                        accum(t, stage[si], R["stage%d" % si])
            if l + 1 < DEPTH:
                load_attn_weights(l + 1)
            for i in range(NT):
                n = rows(i)
                src = p_s[l] if i == ST else p_p[l, i * 128:(i + 1) * 128, :]
                P.dma("sp", lambda e, src=src, n=n: e.dma_start(out=pe32[0:n, :], in_=src), writes=[R["pe32"]])
                P.op("act", lambda e: e.activation(out=peb, in_=pe32, func=AF.Copy), reads=[R["pe32"]], writes=[R["peb"]])
                transposes([(peb[:, k * 128:(k + 1) * 128], R["peb"]) for k in range(2)], peT.rearrange("p k t -> p (k t)"), R["peT"])
                P.dma("sp", lambda e, i=i: e.dma_start(out=h1Tg[:, :, 0:128], in_=HT[i].rearrange("p (k c) -> p k c", k=8)), reads=[R_HT[i]], writes=[R_h1Tg])
                si = sgi[0]
                sgi[0] = 1 - si
                for hf in range(2):
                    pbg, rpbg = nextR()
                    for k in range(8):
                        P.op("pe", lambda e, pbg=pbg, k=k, hf=hf: e.matmul(pbg[:, 0:512], lhsT=h1Tg[:, k, 0:128], rhs=wpg_sb[:, k, hf * 512:(hf + 1) * 512],
                                                                          start=(k == 0), stop=(k == 7)), reads=[R_ple, R_h1Tg], writes=[rpbg])
                    pbp, rpbp = nextR()
                    for k in range(2):
                        P.op("pe", lambda e, pbp=pbp, k=k, hf=hf: e.matmul(pbp[:, 0:512], lhsT=peT[:, k, :], rhs=wpp_sb[:, k, hf * 512:(hf + 1) * 512],
                                                                          start=(k == 0), stop=(k == 1)), reads=[R_ple, R["peT"]], writes=[rpbp])
                    P.op("act", lambda e, pbg=pbg: e.activation(out=sgm, in_=pbg[:, 0:512], func=AF.Sigmoid), reads=[rpbg], writes=[R["sgm"]])
                    P.op("dve", lambda e, pbp=pbp, si=si, hf=hf: e.tensor_tensor(out=stage[si][:, hf * 512:(hf + 1) * 512], in0=sgm, in1=pbp[:, 0:512],
                                                                                 op=ALU.mult),
                         reads=[rpbp, R["sgm"]], writes=[R["stage%d" % si]])
                P.dma("sp", lambda e, i=i: e.dma_start(out=pre, in_=Hs[i]), reads=[R_Hs[i]], writes=[R["pre"]])
                P.op("dve", lambda e, si=si: e.tensor_tensor(out=pre, in0=pre, in1=stage[si], op=ALU.add),
                     reads=[R["pre"], R["stage%d" % si]], writes=[R["pre"]])
                lnorm(pre, R["pre"], 1, D, lng, lnb, R_ln, hs, R["hs"], xh, R["xh"])
                if last:
                    if i == ST:
                        P.dma("sp", lambda e: e.dma_start(out=y_s[:, :], in_=hs[0:16, :]), reads=[R["hs"]])
                    else:
                        P.dma("sp", lambda e, i=i: e.dma_start(out=y_p[i * 128:(i + 1) * 128, :], in_=hs), reads=[R["hs"]])
                else:
                    P.dma("sp", lambda e, i=i: e.dma_start(out=Hs[i], in_=hs), reads=[R["hs"]], writes=[R_Hs[i]])
                    to_hT_and_store(hs, R["hs"], i)

        P.emit(nc, es)
    return nc


def host_consts(S, PAST, NIT):
    NTP = S // 128
    NT = NTP + 1
    pos = np.zeros((NT, 128), np.float64)
    for i in range(NTP):
        pos[i] = i * 128 + np.arange(128)
    pos[NTP, :16] = PAST + (np.arange(16) % 4)

    def tab(dh):
        half = dh // 2
        inv = (10000.0 ** (-np.arange(half, dtype=np.float32) / half)).astype(np.float32)
        ang = pos.astype(np.float32)[:, :, None] * inv[None, None, :]
        c, s = np.cos(ang), np.sin(ang)
        t = np.concatenate([c, c, -s, s], axis=-1)
        return np.ascontiguousarray(t.transpose(1, 0, 2)).astype(np.float32)

    rq = tab(64)
    ri = tab(32)
    msk = np.zeros((128, 8, 128), np.float32)
    t = np.arange(128)[:, None]
    s = np.arange(128)[None, :]
    msk[:, 0, :] = np.where(s <= t, 0.0, NEG)
    nb = np.full((128, 128), NEG, np.float32)
    for r in range(16):
        for r2 in range(16):
            if r // 4 == r2 // 4 and r2 % 4 <= r % 4:
                nb[r, r2] = 0.0
    msk[:, 1, :] = nb
    msk[:, 2, :] = (s <= t).astype(np.float32)
    bd = np.zeros((128, 128), np.float32)
    for r in range(16):
        for r2 in range(16):
            if r // 4 == r2 // 4 and r2 % 4 <= r % 4:
                bd[r, r2] = 1.0
    msk[:, 3, :] = bd
    for r in range(16):
        msk[r, 4, r * 8:(r + 1) * 8] = 1.0
    msk[:, 5, :] = np.eye(128, dtype=np.float32)
    pidx = np.arange(128)
    msk[:, 6, :] = ((pidx[:, None] % 16) == (pidx[None, :] % 16)).astype(np.float32)
    msk[:, 7, :] = NEG
    msk[0:16, 7, :] = nb[0:16, :]
    sel = np.zeros((128, 768), np.float32)
    bigsel = np.zeros((128, 4, 64), np.float32)
    selr = np.zeros((64, 4, 8, 16), np.float32)
    for p in range(128):
        r = p % 16
        b, t = r // 4, r % 4
        for kv in range(2):
            for g in range(4):
                bigsel[p, b, kv * 32 + g * 4 + t] = BIGV
    for b in range(4):
        for h in range(8):
            kv, g = h // 4, h % 4
            for t in range(4):
                selr[kv * 32 + g * 4 + t, b, h, b * 4 + t] = 1.0
    sel[:, 0:256] = bigsel.reshape(128, 256)
    sel[0:64, 256:768] = selr.reshape(64, 512)
    misc = np.zeros((128, 64), np.float32)
    misc[:, :NIT] = (2.0 ** -(np.arange(NIT) + 1.0))[None, :]
    for b in range(4):
        misc[b * 4:(b + 1) * 4, 32 + b] = 1.0
    misc[:, 40] = 1.0
    misc[:, 41] = 128.0
    misc[:, 42] = EPS
    for part in range(8):
        misc[part * 16:(part + 1) * 16, 44 + part] = 1.0
    return rq, ri, msk, misc, sel


_CACHE = {}


def run(cfg, ncores, inputs):
    key = tuple(sorted(cfg.items()))
    if key not in _CACHE:
        _CACHE[key] = build(cfg)
    nc = _CACHE[key]
    S, DEPTH, PAST, NPHYS = cfg["S"], cfg["DEPTH"], cfg["PAST"], cfg["NPHYS"]
    NIT = cfg.get("NIT", 24)
    NPG = PAST // 128
    rq, ri, msk, misc, sel = host_consts(S, PAST, NIT)
    f = lambda a: np.ascontiguousarray(np.asarray(a, dtype=np.float32))
    ck = f(inputs["cache_k"]).reshape(DEPTH, NPHYS, 128, 128)
    cv = f(inputs["cache_v"]).reshape(DEPTH, NPHYS, 128, 128)
    cki = f(inputs["cache_kidx"])
    shared = {
        "c_k": ck, "c_v": cv, "c_ki": cki,
        "w_in": f(inputs["w_in"]), "sgu_g": f(inputs["sgu_ln_g"]).reshape(DEPTH, 512), "sgu_bb": f(inputs["sgu_ln_b"]).reshape(DEPTH, 512),
        "sgu_w": f(inputs["sgu_w"]), "sgu_b": f(inputs["sgu_b"]), "w_o": f(inputs["w_o"]),
        "ln1_g": f(inputs["ln1_g"]), "ln1_b": f(inputs["ln1_b"]), "w_f1": f(inputs["w_ff1"]), "w_f2": f(inputs["w_ff2"]),
        "w_pg": f(inputs["w_ple_gate"]), "w_pp": f(inputs["w_ple_proj"]), "ln2_g": f(inputs["ln2_g"]), "ln2_b": f(inputs["ln2_b"]),
        "c_rq": rq, "c_ri": ri, "c_msk": msk, "c_misc": misc, "c_sel": sel,
    }
    xp, xs = f(inputs["x_prompt"]), f(inputs["x_sample"])
    pp, ps = f(inputs["p_prompt"]), f(inputs["p_sample"])
    pt = np.asarray(inputs["page_table"]).astype(np.int32)
    in_maps = []
    for c in range(ncores):
        m = dict(shared)
        m["x_p"] = np.ascontiguousarray(xp[c])
        m["x_s"] = np.ascontiguousarray(xs[4 * c:4 * c + 4].reshape(16, D))
        m["p_p"] = np.ascontiguousarray(pp[:, c])
        m["p_s"] = np.ascontiguousarray(ps[:, 4 * c:4 * c + 4].reshape(DEPTH, 16, DPLE))
        m["ptab"] = np.ascontiguousarray(pt[4 * c:4 * c + 4].reshape(1, 4 * NPG))
        in_maps.append(m)
    res = run_bass_kernel_spmd(nc, in_maps, core_ids=list(range(ncores)))
    rs = res.results
    cat = lambda k, ax: np.stack([np.asarray(r[k]) for r in rs], axis=ax)
    y_p = cat("y_p", 0)
    y_s = cat("y_s", 0).reshape(4 * ncores, 4, D)
    nk_p = cat("nk_p", 1).reshape(DEPTH, ncores, S, 2, 64)
    nv_p = cat("nv_p", 1).reshape(DEPTH, ncores, S, 2, 64)
    nki_p = cat("nki_p", 1)
    ncv_p = cat("ncv_p", 1)
    nk_s = cat("nk_s", 1).reshape(DEPTH, 4 * ncores, 4, 2, 64)
    nv_s = cat("nv_s", 1).reshape(DEPTH, 4 * ncores, 4, 2, 64)
    nki_s = cat("nki_s", 1).reshape(DEPTH, 4 * ncores, 4, 32)
    ncv_s = cat("ncv_s", 1).reshape(DEPTH, 4 * ncores, 4, 512)
    outs = (y_p, y_s, nk_p, nv_p, nki_p, ncv_p, nk_s, nv_s, nki_s, ncv_s)
    return tuple(np.ascontiguousarray(o, dtype=np.float32) for o in outs)


def kernel(**inputs):
    cfg = {"S": 2048, "DEPTH": 4, "PAST": 8192, "NPHYS": 2560, "NIT": 16}
    return run(cfg, 8, inputs)
```
